# Optimizing a Trainium2 kernel written in Bass

```python
import math
import jax
import jax.numpy as jnp
from jax import lax
import numpy as np

D_MODEL = 1024
BATCH = 2
SEQ = 16384
DEPTH = 2

SSD_HEADS = 16
SSD_HEAD_DIM = 64
SSD_INNER = SSD_HEADS * SSD_HEAD_DIM
SSD_GROUPS = 4
SSD_STATE = 128
SSD_CONV = 4
SSD_CHUNK = 128
SSD_XBC = SSD_INNER + 2 * SSD_GROUPS * SSD_STATE

MOBA_HEADS = 8
MOBA_HEAD_DIM = 128
MOBA_INNER = MOBA_HEADS * MOBA_HEAD_DIM
MOBA_BLOCK = 256
MOBA_TOPK = 3
MOBA_QCHUNK = 32

REL_BUCKETS = 32
REL_MAX_DIST = 1024

RWKV_HEADS = 16
RWKV_HEAD_DIM = 64
RWKV_INNER = RWKV_HEADS * RWKV_HEAD_DIM
RWKV_DECAY_LORA = 64
RWKV_ICLR_LORA = 64
RWKV_GATE_LORA = 128
RWKV_IN = 3 * RWKV_INNER + RWKV_DECAY_LORA + RWKV_ICLR_LORA + RWKV_GATE_LORA
RWKV_LN_EPS = 64e-5

PEER_HEADS = 8
PEER_KEYS = 128
PEER_EXPERTS = PEER_KEYS * PEER_KEYS
PEER_TOPK = 16
PEER_KEY_DIM = 256
PEER_CHUNK = 128

N_BRANCH = 3
IN_SSD = SSD_INNER + SSD_XBC + SSD_HEADS
IN_MOBA = 3 * MOBA_INNER
IN_TOTAL = IN_SSD + IN_MOBA + RWKV_IN
NORM_EPS = 1e-6
NEG = -1e30

kernel_name = "hybrid_ssd_moba_rwkv7_peer"


def rms_norm(x, g, eps=NORM_EPS):
    xf = x.astype(jnp.float32)
    return xf * lax.rsqrt(jnp.mean(xf * xf, axis=-1, keepdims=True) + eps) * g.astype(jnp.float32)


def t5_bucket(dist):
    max_exact = REL_BUCKETS // 2
    d = jnp.maximum(dist, 0)
    df = jnp.maximum(d, 1).astype(jnp.float32)
    large = max_exact + (jnp.log(df / max_exact) / math.log(REL_MAX_DIST / max_exact)
                         * (REL_BUCKETS - max_exact)).astype(jnp.int32)
    large = jnp.minimum(large, REL_BUCKETS - 1)
    return jnp.where(d < max_exact, d, large)


def ssd_mixer(zxbcdt, conv_w, conv_b, dt_bias, a_log, d_skip, norm_g):
    f32 = jnp.float32
    bsz, slen, _ = zxbcdt.shape
    G, E, P, N, Lc = SSD_GROUPS, SSD_HEADS // SSD_GROUPS, SSD_HEAD_DIM, SSD_STATE, SSD_CHUNK
    nc = slen // Lc
    zxbcdt = zxbcdt.astype(f32)
    z = zxbcdt[..., :SSD_INNER]
    xbc = zxbcdt[..., SSD_INNER:SSD_INNER + SSD_XBC]
    dt = zxbcdt[..., SSD_INNER + SSD_XBC:]
    xbc = lax.conv_general_dilated(xbc, conv_w.astype(f32)[:, None, :], window_strides=(1,),
                                   padding=[(SSD_CONV - 1, 0)], dimension_numbers=("NWC", "WIO", "NWC"),
                                   feature_group_count=SSD_XBC)
    xbc = jax.nn.silu(xbc + conv_b.astype(f32))
    xs = xbc[..., :SSD_INNER]
    b_in = xbc[..., SSD_INNER:SSD_INNER + G * N].reshape(bsz, nc, Lc, G, N)
    c_in = xbc[..., SSD_INNER + G * N:].reshape(bsz, nc, Lc, G, N)
    dt = jax.nn.softplus(dt + dt_bias.astype(f32))
    a = -jnp.exp(a_log.astype(f32))
    xdt = xs.reshape(bsz, nc, Lc, G, E, P) * dt.reshape(bsz, nc, Lc, G, E)[..., None]
    acs = jnp.cumsum((dt * a).reshape(bsz, nc, Lc, G, E).transpose(0, 1, 3, 4, 2), axis=-1)
    causal = jnp.tril(jnp.ones((Lc, Lc), dtype=bool))
    seg = jnp.exp(jnp.where(causal, acs[..., :, None] - acs[..., None, :], -jnp.inf))
    cb = jnp.einsum("bclgn,bcsgn->bcgls", c_in, b_in)
    y_diag = jnp.einsum("bcgels,bcsgep->bclgep", cb[:, :, :, None] * seg, xdt)
    decay_to_end = jnp.exp(acs[..., -1:] - acs)
    chunk_states = jnp.einsum("bclgn,bcgel,bclgep->bcgepn", b_in, decay_to_end, xdt)
    chunk_decay = jnp.exp(acs[..., -1])

    def carry_state(h, inp):
        st, dec = inp
        return h * dec[..., None, None] + st, h

    h0 = jnp.zeros((bsz, G, E, P, N), f32)
    _, h_in = lax.scan(carry_state, h0, (jnp.moveaxis(chunk_states, 1, 0), jnp.moveaxis(chunk_decay, 1, 0)))
    h_in = jnp.moveaxis(h_in, 0, 1)
    y_off = jnp.einsum("bclgn,bcgepn,bcgel->bclgep", c_in, h_in, jnp.exp(acs))
    y = (y_diag + y_off).reshape(bsz, slen, SSD_INNER) + xs * jnp.repeat(d_skip.astype(f32), P)
    y = y * jax.nn.silu(z)
    yg = y.reshape(bsz, slen, G, SSD_INNER // G)
    yg = yg * lax.rsqrt(jnp.mean(yg * yg, axis=-1, keepdims=True) + NORM_EPS)
    return yg.reshape(bsz, slen, SSD_INNER) * norm_g.astype(f32)


def moba_mixer(qkv, q_norm_g, k_norm_g, rel_bias):
    f32 = jnp.float32
    bsz, slen, _ = qkv.shape
    H, Dh, BLK, QC = MOBA_HEADS, MOBA_HEAD_DIM, MOBA_BLOCK, MOBA_QCHUNK
    qkv = qkv.astype(f32)

    def heads(t):
        return t.reshape(bsz, slen, H, Dh).transpose(0, 2, 1, 3)

    q = rms_norm(heads(qkv[..., :MOBA_INNER]), q_norm_g)
    k = rms_norm(heads(qkv[..., MOBA_INNER:2 * MOBA_INNER]), k_norm_g)
    v = heads(qkv[..., 2 * MOBA_INNER:])
    nb = -(-slen // BLK)
    s_pad = nb * BLK
    padw = ((0, 0), (0, 0), (0, s_pad - slen), (0, 0))
    q, k, v = jnp.pad(q, padw), jnp.pad(k, padw), jnp.pad(v, padw)
    kb = k.reshape(bsz, H, nb, BLK, Dh)
    vb = v.reshape(bsz, H, nb, BLK, Dh)
    kmean = jnp.mean(kb, axis=3)
    n_sel = min(MOBA_TOPK, nb)
    nqc = s_pad // QC
    q_chunks = q.reshape(bsz, H, nqc, QC, Dh).transpose(2, 0, 1, 3, 4)
    bias_t = rel_bias.astype(f32).T
    scale = Dh ** -0.5
    b_idx = jnp.arange(bsz)[:, None, None, None]
    h_idx = jnp.arange(H)[None, :, None, None]

    def one_chunk(args):
        qi, c = args
        q_pos = c * QC + jnp.arange(QC, dtype=jnp.int32)
        qblk = (c * QC) // BLK
        gate = jnp.einsum("bhqd,bhnd->bhqn", qi, kmean)
        gate = jnp.where(jnp.arange(nb) < qblk, gate, NEG)
        _, sel = lax.top_k(gate, n_sel)
        sel_valid = jnp.arange(n_sel) < qblk
        k_sel = kb[b_idx, h_idx, sel]
        v_sel = vb[b_idx, h_idx, sel]
        s_past = jnp.einsum("bhqd,bhqjkd->bhqjk", qi, k_sel) * scale
        k_pos_past = sel[..., None] * BLK + jnp.arange(BLK, dtype=jnp.int32)
        bias_past = bias_t[h_idx[..., None], t5_bucket(q_pos[:, None, None] - k_pos_past)]
        s_past = jnp.where(sel_valid[:, None], s_past + bias_past, NEG)
        k_own = lax.dynamic_slice_in_dim(kb, qblk, 1, axis=2)[:, :, 0]
        v_own = lax.dynamic_slice_in_dim(vb, qblk, 1, axis=2)[:, :, 0]
        dist = q_pos[:, None] - (qblk * BLK + jnp.arange(BLK, dtype=jnp.int32))[None, :]
        s_own = jnp.einsum("bhqd,bhkd->bhqk", qi, k_own) * scale + bias_t[:, t5_bucket(dist)]
        s_own = jnp.where(dist >= 0, s_own, NEG)
        logits = jnp.concatenate([s_own, s_past.reshape(bsz, H, QC, n_sel * BLK)], axis=-1)
        p = jax.nn.softmax(logits, axis=-1)
        p_past = p[..., BLK:].reshape(bsz, H, QC, n_sel, BLK)
        return (jnp.einsum("bhqk,bhkd->bhqd", p[..., :BLK], v_own)
                + jnp.einsum("bhqjk,bhqjkd->bhqd", p_past, v_sel))

    out = lax.map(one_chunk, (q_chunks, jnp.arange(nqc, dtype=jnp.int32)))
    out = out.transpose(1, 2, 0, 3, 4).reshape(bsz, H, s_pad, Dh)[:, :, :slen]
    return out.transpose(0, 2, 1, 3).reshape(bsz, slen, MOBA_INNER)


def rwkv7_mixer(proj, mu, w0, w_w2, a0, w_a2, w_g2, k_k, k_a, r_k, lnx_g, lnx_b):
    f32 = jnp.float32
    bsz, slen, _ = proj.shape
    H, N, C = RWKV_HEADS, RWKV_HEAD_DIM, RWKV_INNER
    proj = proj.astype(f32)
    prev = jnp.pad(proj, ((0, 0), (1, 0), (0, 0)))[:, :-1]
    xs = proj + (prev - proj) * mu.astype(f32)
    o1, o2 = 3 * C + RWKV_DECAY_LORA, 3 * C + RWKV_DECAY_LORA + RWKV_ICLR_LORA
    r, k, v = xs[..., :C], xs[..., C:2 * C], xs[..., 2 * C:3 * C]
    wl, al, gl = xs[..., 3 * C:o1], xs[..., o1:o2], xs[..., o2:]
    w = -jax.nn.softplus(-(w0.astype(f32) + jnp.tanh(wl) @ w_w2.astype(f32))) - 0.5
    decay = jnp.exp(-jnp.exp(w))
    a = jax.nn.sigmoid(a0.astype(f32) + al @ w_a2.astype(f32))
    g = jax.nn.sigmoid(gl) @ w_g2.astype(f32)
    kk = (k * k_k.astype(f32)).reshape(bsz, slen, H, N)
    kk = kk * lax.rsqrt(jnp.maximum(jnp.sum(kk * kk, axis=-1, keepdims=True), 1e-12))
    kk = kk.reshape(bsz, slen, C)
    k = k * (1.0 + (a - 1.0) * k_a.astype(f32))

    def tm(t):
        return t.reshape(bsz, slen, H, N).transpose(1, 0, 2, 3)

    def step(state, inp):
        r_t, w_t, k_t, v_t, a_t, b_t = inp
        sa = jnp.einsum("bhvk,bhk->bhv", state, a_t)
        state = state * w_t[:, :, None, :] + sa[..., None] * b_t[:, :, None, :] + v_t[..., None] * k_t[:, :, None, :]
        return state, jnp.einsum("bhvk,bhk->bhv", state, r_t)

    s0 = jnp.zeros((bsz, H, N, N), f32)
    _, y = lax.scan(step, s0, (tm(r), tm(decay), tm(k), tm(v), tm(-kk), tm(kk * a)))
    y = y.transpose(1, 0, 2, 3)
    mean = jnp.mean(y, axis=-1, keepdims=True)
    var = jnp.mean(jnp.square(y - mean), axis=-1, keepdims=True)
    y = ((y - mean) * lax.rsqrt(var + RWKV_LN_EPS)).reshape(bsz, slen, C) * lnx_g.astype(f32) + lnx_b.astype(f32)
    rh, kh, vh = r.reshape(bsz, slen, H, N), k.reshape(bsz, slen, H, N), v.reshape(bsz, slen, H, N)
    bonus = jnp.sum(rh * kh * r_k.astype(f32), axis=-1, keepdims=True) * vh
    return (y + bonus.reshape(bsz, slen, C)) * g


def peer_ffn(h, wq, k1, k2, u, v):
    f32 = jnp.float32
    bsz, slen, dm = h.shape
    T, PH, half = PEER_CHUNK, PEER_HEADS, PEER_KEY_DIM // 2
    nc = slen // T
    h_chunks = h.astype(f32).reshape(bsz, nc, T, dm).transpose(1, 0, 2, 3)
    wq, k1, k2 = wq.astype(f32), k1.astype(f32), k2.astype(f32)

    def one_chunk(hi):
        q = (hi @ wq).reshape(bsz, T, PH, PEER_KEY_DIM)
        s1 = jnp.einsum("bthd,nd->bthn", q[..., :half], k1)
        s2 = jnp.einsum("bthd,nd->bthn", q[..., half:], k2)
        v1, i1 = lax.top_k(s1, PEER_TOPK)
        v2, i2 = lax.top_k(s2, PEER_TOPK)
        cand = (v1[..., :, None] + v2[..., None, :]).reshape(bsz, T, PH, PEER_TOPK * PEER_TOPK)
        cand_idx = (i1[..., :, None] * PEER_KEYS + i2[..., None, :]).reshape(bsz, T, PH, PEER_TOPK * PEER_TOPK)
        sc, pos = lax.top_k(cand, PEER_TOPK)
        idx = jnp.take_along_axis(cand_idx, pos, axis=-1)
        gate = jax.nn.softmax(sc, axis=-1)
        act = jax.nn.gelu(jnp.einsum("btd,bthkd->bthk", hi, u[idx].astype(f32)), approximate=False)
        return jnp.einsum("bthk,bthkd->btd", gate * act, v[idx].astype(f32))

    out = lax.map(one_chunk, h_chunks)
    return out.transpose(1, 0, 2, 3).reshape(bsz, slen, dm)


def setup_inputs(seed: int = 0) -> dict:
    key = jax.random.key(seed)
    ks = iter(jax.random.split(key, 48))
    L, D = DEPTH, D_MODEL
    f32 = jnp.float32

    def nrm(shape, scale):
        return jax.random.normal(next(ks), shape, f32) * scale

    def unif(shape, lo, hi):
        return jax.random.uniform(next(ks), shape, f32, lo, hi)

    def gain(shape):
        return 1.0 + nrm(shape, 0.05)

    dt0 = jnp.exp(unif((L, SSD_HEADS), math.log(1e-3), math.log(1e-1)))
    inp = {}
    inp["x"] = nrm((BATCH, SEQ, D), 1.0)
    inp["rel_bias"] = nrm((REL_BUCKETS, MOBA_HEADS), 0.5)
    inp["norm1_g"] = gain((L, D))
    inp["w_in"] = nrm((L, D, IN_TOTAL), D ** -0.5)
    inp["conv_w"] = nrm((L, SSD_CONV, SSD_XBC), 0.5)
    inp["conv_b"] = nrm((L, SSD_XBC), 0.02)
    inp["dt_bias"] = dt0 + jnp.log(-jnp.expm1(-dt0))
    inp["a_log"] = jnp.log(unif((L, SSD_HEADS), 1.0, 16.0))
    inp["d_skip"] = 1.0 + nrm((L, SSD_HEADS), 0.1)
    inp["ssd_norm_g"] = gain((L, SSD_INNER))
    inp["q_norm_g"] = gain((L, MOBA_HEAD_DIM))
    inp["k_norm_g"] = gain((L, MOBA_HEAD_DIM))
    inp["rwkv_mu"] = unif((L, RWKV_IN), 0.0, 1.0)
    inp["w0"] = unif((L, RWKV_INNER), -7.0, -1.0)
    inp["w_w2"] = nrm((L, RWKV_DECAY_LORA, RWKV_INNER), 0.1)
    inp["a0"] = nrm((L, RWKV_INNER), 0.5)
    inp["w_a2"] = nrm((L, RWKV_ICLR_LORA, RWKV_INNER), RWKV_ICLR_LORA ** -0.5)
    inp["w_g2"] = nrm((L, RWKV_GATE_LORA, RWKV_INNER), RWKV_GATE_LORA ** -0.5)
    inp["k_k"] = 0.85 + nrm((L, RWKV_INNER), 0.05)
    inp["k_a"] = 1.0 + nrm((L, RWKV_INNER), 0.05)
    inp["r_k"] = nrm((L, RWKV_HEADS, RWKV_HEAD_DIM), 0.1)
    inp["lnx_g"] = gain((L, RWKV_INNER))
    inp["lnx_b"] = nrm((L, RWKV_INNER), 0.02)
    inp["p_ssd"] = nrm((L, SSD_INNER, D), SSD_INNER ** -0.5)
    inp["p_moba"] = nrm((L, MOBA_INNER, D), MOBA_INNER ** -0.5)
    inp["p_rwkv"] = nrm((L, RWKV_INNER, D), RWKV_INNER ** -0.5)
    inp["w_gate"] = nrm((L, D, N_BRANCH * D), D ** -0.5)
    inp["b_gate"] = nrm((L, N_BRANCH * D), 0.02)
    inp["w_out"] = nrm((L, D, D), 0.5 * D ** -0.5)
    inp["norm2_g"] = gain((L, D))
    inp["peer_wq"] = nrm((L, D, PEER_HEADS * PEER_KEY_DIM), D ** -0.5)
    inp["peer_k1"] = nrm((L, PEER_KEYS, PEER_KEY_DIM // 2), (PEER_KEY_DIM // 2) ** -0.5)
    inp["peer_k2"] = nrm((L, PEER_KEYS, PEER_KEY_DIM // 2), (PEER_KEY_DIM // 2) ** -0.5)
    inp["peer_u"] = nrm((L, PEER_EXPERTS, D), D ** -0.5)
    inp["peer_v"] = nrm((L, PEER_EXPERTS, D), 0.1)
    return inp


def reference(x, rel_bias, norm1_g, w_in, conv_w, conv_b, dt_bias, a_log, d_skip, ssd_norm_g,
              q_norm_g, k_norm_g, rwkv_mu, w0, w_w2, a0, w_a2, w_g2, k_k, k_a, r_k, lnx_g, lnx_b,
              p_ssd, p_moba, p_rwkv, w_gate, b_gate, w_out, norm2_g,
              peer_wq, peer_k1, peer_k2, peer_u, peer_v):
    bsz, slen, dm = x.shape
    for l in range(DEPTH):
        h = rms_norm(x, norm1_g[l]).astype(x.dtype)
        proj = h @ w_in[l]
        u_ssd = proj[..., :IN_SSD]
        u_moba = proj[..., IN_SSD:IN_SSD + IN_MOBA]
        u_rwkv = proj[..., IN_SSD + IN_MOBA:]
        y_a = ssd_mixer(u_ssd, conv_w[l], conv_b[l], dt_bias[l], a_log[l], d_skip[l], ssd_norm_g[l])
        y_b = moba_mixer(u_moba, q_norm_g[l], k_norm_g[l], rel_bias)
        y_c = rwkv7_mixer(u_rwkv, rwkv_mu[l], w0[l], w_w2[l], a0[l], w_a2[l], w_g2[l],
                          k_k[l], k_a[l], r_k[l], lnx_g[l], lnx_b[l])
        gates = jax.nn.sigmoid((h @ w_gate[l] + b_gate[l]).astype(jnp.float32)).reshape(bsz, slen, N_BRANCH, dm)
        merged = (gates[..., 0, :] * (y_a @ p_ssd[l].astype(jnp.float32))
                  + gates[..., 1, :] * (y_b @ p_moba[l].astype(jnp.float32))
                  + gates[..., 2, :] * (y_c @ p_rwkv[l].astype(jnp.float32)))
        x = x + (merged.astype(x.dtype) @ w_out[l])
        h2 = rms_norm(x, norm2_g[l]).astype(x.dtype)
        x = x + peer_ffn(h2, peer_wq[l], peer_k1[l], peer_k2[l], peer_u[l], peer_v[l]).astype(x.dtype)
    return x
```

```python
import contextlib
import numpy as np
import concourse.bass as bass
import concourse.mybir as mybir

F32 = mybir.dt.float32
BF16 = mybir.dt.bfloat16
ALU = mybir.AluOpType
AF = mybir.ActivationFunctionType
AX = mybir.AxisListType

EP = 30000
KD = 24


class Buf:
    __slots__ = ("t", "w", "r", "name")

    def __init__(self, t, name=""):
        self.t = t
        self.w = {}
        self.r = {}
        self.name = name

    def __getitem__(self, idx):
        return self.t[idx]


class KB:
    def __init__(self, nc, same_engine_sync=True):
        self.nc = nc
        self.es = contextlib.ExitStack()
        self.eng = {"pe": nc.tensor, "dve": nc.vector, "act": nc.scalar, "pool": nc.gpsimd, "sp": nc.sync}
        self.cnt = {e: 0 for e in ("pe", "dve", "act", "pool")}
        self.sems = {}
        self.waited = {e: {} for e in self.eng}
        self.dma_i = 0
        self.dma_last = {}
        self.same = same_engine_sync
        self.nbuf = 0
        self.n_wait = 0

    def sem(self, key):
        s = self.sems.get(key)
        if s is None:
            s = self.es.enter_context(self.nc.semaphore("s_%s_%s" % key))
            self.sems[key] = s
        return s

    def sb(self, shape, dt=F32, name=None, stack=None):
        self.nbuf += 1
        name = ("s%d_" % self.nbuf + name) if name else "sb%d" % self.nbuf
        t = (stack or self.es).enter_context(self.nc.sbuf_tensor(name, list(shape), dt))
        return Buf(t, name)

    def ps(self, shape, dt=F32, name=None, stack=None):
        self.nbuf += 1
        name = ("p%d_" % self.nbuf + name) if name else "ps%d" % self.nbuf
        t = (stack or self.es).enter_context(self.nc.psum_tensor(name, list(shape), dt))
        return Buf(t, name)

    def dram(self, name, shape, dt=F32, kind="Internal"):
        t = self.nc.dram_tensor(name, list(shape), dt, kind=kind)
        return Buf(t.ap(), name)

    def _deps(self, reads, writes):
        deps = {}
        for b in reads:
            for k, v in b.w.items():
                if deps.get(k, 0) < v:
                    deps[k] = v
        for b in writes:
            for k, v in b.w.items():
                if deps.get(k, 0) < v:
                    deps[k] = v
            for k, v in b.r.items():
                if deps.get(k, 0) < v:
                    deps[k] = v
        return deps

    def _wait(self, E, deps):
        wd = self.waited[E]
        for k, v in deps.items():
            if k[0] == E:
                if E == "pe" or not self.same:
                    continue
            if wd.get(k, 0) >= v:
                continue
            self.eng[E].wait_ge(self.sem(k), v)
            self.n_wait += 1
            wd[k] = v

    def _mark(self, dep, reads, writes):
        k, v = dep
        for b in reads:
            b.r[k] = v
        for b in writes:
            b.w = {k: v}
            b.r = {}

    def op(self, E, fn, reads=(), writes=()):
        self._wait(E, self._deps(reads, writes))
        ins = fn(self.eng[E])
        n = self.cnt[E] = self.cnt[E] + 1
        key = (E, (n - 1) // EP)
        val = (n - 1) % EP + 1
        ins.then_inc(self.sem(key), 1)
        self._mark((key, val), reads, writes)
        return ins

    def dma(self, out, in_, reads=(), writes=(), Q="sp", **kw):
        self._wait(Q, self._deps(reads, writes))
        i = self.dma_i
        self.dma_i += 1
        s = i % KD
        key = ("dma", s)
        val = 16 * (i // KD + 1)
        if i >= KD:
            wd = self.waited[Q]
            if wd.get(key, 0) < val - 16:
                self.eng[Q].wait_ge(self.sem(key), val - 16)
                wd[key] = val - 16
        self.eng[Q].dma_start(out=out, in_=in_, **kw).then_inc(self.sem(key), 16)
        self.dma_last[s] = val
        self._mark((key, val), reads, writes)

    def barrier(self):
        state = {}
        for e, n in self.cnt.items():
            if n > 0:
                state[(e, (n - 1) // EP)] = (n - 1) % EP + 1
        for s, v in self.dma_last.items():
            state[("dma", s)] = v
        for E in self.eng:
            wd = self.waited[E]
            for k, v in state.items():
                if k[0] == E and E == "pe":
                    continue
                if wd.get(k, 0) >= v:
                    continue
                self.eng[E].wait_ge(self.sem(k), v)
                wd[k] = v

    def finish(self):
        self.barrier()

import math


def din(nc, name, shape, dt=F32):
    return nc.dram_tensor(name, list(shape), dt, kind="ExternalInput").ap()


def dout(nc, name, shape, dt=F32):
    return nc.dram_tensor(name, list(shape), dt, kind="ExternalOutput").ap()


class Ctx:
    pass


def setup_consts(kb, nc, cx):
    cx.ident_d = din(nc, "ident", [128, 128])
    cx.ident = kb.sb([128, 128], name="ident")
    kb.dma(cx.ident[:], cx.ident_d, writes=[cx.ident])
    cx.ones = kb.sb([128, 128], name="ones")
    kb.op("dve", lambda e: e.memset(cx.ones[:], 1.0), writes=[cx.ones])
    cx.bones = kb.sb([128, 128], name="bones")
    kb.op("dve", lambda e: e.memset(cx.bones[:], 0.0), writes=[cx.bones])
    kb.op("dve", lambda e: e.memset(cx.bones[0:64, 0:64], 1.0), writes=[cx.bones])
    kb.op("dve", lambda e: e.memset(cx.bones[64:128, 64:128], 1.0), writes=[cx.bones])
    cx.eps6 = kb.sb([128, 1], name="eps6")
    kb.op("dve", lambda e: e.memset(cx.eps6[:], 1e-6), writes=[cx.eps6])
    cx.psum = [kb.ps([128, 512], name="bank%d" % i) for i in range(8)]
    cx.OUT = Buf(None, "OUT")


def phase_norm(kb, nc, cx, T, xT_d, g_d, hT_dram):
    import contextlib
    with contextlib.ExitStack() as st:
        g = kb.sb([128, 8], name="n_g", stack=st)
        kb.dma(g[:], g_d, writes=[g])
        xs = [kb.sb([128, 8, 512], name="n_x%d" % i, stack=st) for i in range(2)]
        sq = kb.sb([128, 8, 512], name="n_sq", stack=st)
        rstd = kb.sb([128, 512], name="n_rstd", stack=st)
        hs = [kb.sb([128, 8, 512], name="n_h%d" % i, stack=st) for i in range(2)]
        p0 = cx.psum[0]
        for i in range(T // 512):
            x = xs[i % 2]
            h = hs[i % 2]
            kb.dma(x[:], xT_d[:, :, i * 512:(i + 1) * 512].rearrange("k p t -> p k t"), writes=[x])
            kb.op("act", lambda e: e.activation(out=sq[:], in_=x[:], func=AF.Square), reads=[x], writes=[sq])
            for k in range(8):
                kb.op("pe", lambda e: e.matmul(p0[:, :], lhsT=cx.ones[:], rhs=sq[:, k, :], start=(k == 0), stop=(k == 7)),
                      reads=[cx.ones, sq], writes=[p0])
            kb.op("act", lambda e: e.activation(out=rstd[:], in_=p0[:, :], func=AF.Sqrt, scale=1.0 / 1024, bias=cx.eps6[:]),
                  reads=[p0, cx.eps6], writes=[rstd])
            kb.op("dve", lambda e: e.reciprocal(out=rstd[:], in_=rstd[:]), reads=[rstd], writes=[rstd])
            for k in range(8):
                kb.op("dve", lambda e: e.scalar_tensor_tensor(out=h[:, k, :], in0=x[:, k, :], scalar=g[:, k:k + 1], in1=rstd[:],
                                                              op0=ALU.mult, op1=ALU.mult), reads=[x, g, rstd], writes=[h])
            kb.dma(hT_dram.t[:, :, i * 512:(i + 1) * 512].rearrange("k p t -> p k t"), h[:], reads=[h], writes=[hT_dram])
        kb.barrier()


EM05 = math.exp(-0.5)


def rwkv_inputs(nc, cx):
    cx.rw_w = din(nc, "rw_w", [8, 128, 1024])
    cx.rw_mu = din(nc, "rw_mu", [128, 8])
    cx.rw_pc = din(nc, "rw_pc", [128, 14])
    cx.rw_la2 = din(nc, "rw_la2", [128, 256])
    cx.rw_g2 = din(nc, "rw_g2", [128, 256])


def rwkv_phase1(kb, nc, cx, T, hT_dram, strm, fm, strm_w=None):
    with contextlib.ExitStack() as st:
        W = kb.sb([128, 8, 1024], name="rw_W", stack=st)
        kb.dma(W[:], cx.rw_w.rearrange("k p c -> p k c"), writes=[W])
        mu = kb.sb([128, 8], name="rw_mu", stack=st)
        kb.dma(mu[:], cx.rw_mu, writes=[mu])
        pc = kb.sb([128, 14], name="rw_pc", stack=st)
        kb.dma(pc[:], cx.rw_pc, writes=[pc])
        la2 = kb.sb([128, 256], name="rw_la2", stack=st)
        kb.dma(la2[:], cx.rw_la2, writes=[la2])
        g2 = kb.sb([128, 256], name="rw_g2", stack=st)
        kb.dma(g2[:], cx.rw_g2, writes=[g2])
        eps12 = kb.sb([128, 1], name="rw_eps12", stack=st)
        hTs = [kb.sb([128, 8, 512], name="rw_h%d" % i, stack=st) for i in range(2)]
        raw = [kb.sb([128, 513], name="rw_raw%d" % c, stack=st) for c in range(8)]
        xs = [kb.sb([128, 512], name="rw_xs%d" % c, stack=st) for c in range(8)]
        for c in range(8):
            kb.op("dve", lambda e: e.memset(raw[c][:, 0:1], 0.0), writes=[raw[c]])
        tmp = [kb.sb([128, 512], name="rw_tmp%d" % i, stack=st) for i in range(8)]
        sbs = [kb.sb([128, 512], name="rw_s%d" % i, stack=st) for i in range(5)]
        gj = kb.sb([128, 512], name="rw_gj", stack=st)
        tok = [kb.sb([128, 512], BF16, name="rw_tok%d" % i, stack=st) for i in range(2)]
        tokf = [kb.sb([128, 512], name="rw_tokf%d" % i, stack=st) for i in range(2)]
        pp = cx.psum
        ntok = 0
        for i in range(T // 512):
            hT = hTs[i % 2]
            kb.dma(hT[:], hT_dram.t[:, :, i * 512:(i + 1) * 512].rearrange("k p t -> p k t"), writes=[hT])
            for c in range(8):
                ps = pp[c % 2]
                for k in range(8):
                    kb.op("pe", lambda e: e.matmul(ps[:, :], lhsT=W[:, k, c * 128:(c + 1) * 128], rhs=hT[:, k, :], start=(k == 0), stop=(k == 7)),
                          reads=[W, hT], writes=[ps])
                kb.op("act", lambda e: e.copy(out=raw[c][:, 1:513], in_=ps[:, :]), reads=[ps], writes=[raw[c]])
                kb.op("dve", lambda e: e.tensor_tensor(out=xs[c][:], in0=raw[c][:, 0:512], in1=raw[c][:, 1:513], op=ALU.subtract),
                      reads=[raw[c]], writes=[xs[c]])
                kb.op("dve", lambda e: e.scalar_tensor_tensor(out=xs[c][:], in0=xs[c][:], scalar=mu[:, c:c + 1], in1=raw[c][:, 1:513],
                                                              op0=ALU.mult, op1=ALU.add), reads=[xs[c], mu, raw[c]], writes=[xs[c]])
                kb.op("dve", lambda e: e.tensor_copy(out=raw[c][:, 0:1], in_=raw[c][:, 512:513]), reads=[raw[c]], writes=[raw[c]])
            xla, xgl = xs[6], xs[7]
            tanh_wl, sig_gl = tmp[0], tmp[1]
            kb.op("act", lambda e: e.activation(out=tanh_wl[0:64, :], in_=xla[0:64, :], func=AF.Tanh), reads=[xla], writes=[tanh_wl])
            kb.op("act", lambda e: e.activation(out=sig_gl[:], in_=xgl[:], func=AF.Sigmoid), reads=[xgl], writes=[sig_gl])
            for j in range(2):
                R, K, V = xs[0 + j], xs[2 + j], xs[4 + j]
                A_s, W_s, B_s, K_s = sbs[0], sbs[1], sbs[2], sbs[3]
                asig, kx, t1 = tmp[2], tmp[3], tmp[4]
                js = slice(j * 128, (j + 1) * 128)
                kb.op("pe", lambda e: e.matmul(pp[2][:, :], lhsT=la2[0:64, js], rhs=tanh_wl[0:64, :], start=True, stop=True),
                      reads=[la2, tanh_wl], writes=[pp[2]])
                kb.op("act", lambda e: e.activation(out=W_s[:], in_=pp[2][:, :], func=AF.Sigmoid, bias=pc[:, 0 + j:1 + j]),
                      reads=[pp[2], pc], writes=[W_s])
                kb.op("act", lambda e: e.activation(out=W_s[:], in_=W_s[:], func=AF.Exp, scale=-EM05), reads=[W_s], writes=[W_s])
                kb.op("pe", lambda e: e.matmul(pp[3][:, :], lhsT=la2[64:128, js], rhs=xla[64:128, :], start=True, stop=True),
                      reads=[la2, xla], writes=[pp[3]])
                kb.op("act", lambda e: e.activation(out=asig[:], in_=pp[3][:, :], func=AF.Sigmoid, bias=pc[:, 2 + j:3 + j]),
                      reads=[pp[3], pc], writes=[asig])
                kb.op("pe", lambda e: e.matmul(pp[4][:, :], lhsT=g2[:, js], rhs=sig_gl[:], start=True, stop=True),
                      reads=[g2, sig_gl], writes=[pp[4]])
                kb.op("act", lambda e: e.copy(out=gj[:], in_=pp[4][:, :]), reads=[pp[4]], writes=[gj])
                kb.op("dve", lambda e: e.tensor_scalar(out=kx[:], in0=K[:], scalar1=pc[:, 4 + j:5 + j], scalar2=None, op0=ALU.mult),
                      reads=[K, pc], writes=[kx])
                kb.op("act", lambda e: e.activation(out=t1[:], in_=kx[:], func=AF.Square), reads=[kx], writes=[t1])
                kb.op("pe", lambda e: e.matmul(pp[5][:, :], lhsT=cx.bones[:], rhs=t1[:], start=True, stop=True),
                      reads=[cx.bones, t1], writes=[pp[5]])
                kb.op("dve", lambda e: e.tensor_scalar(out=t1[:], in0=pp[5][:, :], scalar1=1e-12, scalar2=None, op0=ALU.max),
                      reads=[pp[5]], writes=[t1])
                kb.op("act", lambda e: e.activation(out=t1[:], in_=t1[:], func=AF.Sqrt), reads=[t1], writes=[t1])
                kb.op("dve", lambda e: e.reciprocal(out=t1[:], in_=t1[:]), reads=[t1], writes=[t1])
                kb.op("dve", lambda e: e.tensor_tensor(out=kx[:], in0=kx[:], in1=t1[:], op=ALU.mult), reads=[kx, t1], writes=[kx])
                kb.op("dve", lambda e: e.tensor_scalar(out=A_s[:], in0=kx[:], scalar1=-1.0, scalar2=None, op0=ALU.mult),
                      reads=[kx], writes=[A_s])
                kb.op("dve", lambda e: e.tensor_tensor(out=B_s[:], in0=kx[:], in1=asig[:], op=ALU.mult), reads=[kx, asig], writes=[B_s])
                kb.op("dve", lambda e: e.tensor_scalar(out=t1[:], in0=asig[:], scalar1=-1.0, scalar2=pc[:, 6 + j:7 + j], op0=ALU.add, op1=ALU.mult),
                      reads=[asig, pc], writes=[t1])
                kb.op("dve", lambda e: e.scalar_tensor_tensor(out=K_s[:], in0=t1[:], scalar=1.0, in1=K[:], op0=ALU.add, op1=ALU.mult),
                      reads=[t1, K], writes=[K_s])
                for q, src in enumerate((R, K_s, V, gj)):
                    kb.dma(fm.t[q, j, :, i * 512:(i + 1) * 512], src[:], reads=[src])
                for q, src in enumerate((A_s, W_s, B_s, K_s, R)):
                    ps = pp[6 + (ntok % 2)]
                    tk = (tokf if q == 1 else tok)[ntok % 2]
                    ntok += 1
                    for s in range(4):
                        kb.op("pe", lambda e: e.transpose(ps[:, s * 128:(s + 1) * 128], src[:, s * 128:(s + 1) * 128], cx.ident[:]),
                              reads=[src, cx.ident], writes=[ps])
                    kb.op("act", lambda e: e.copy(out=tk[:], in_=ps[:, :]), reads=[ps], writes=[tk])
                    for hp in range(2):
                        dd = strm_w.t[hp] if q == 1 else strm.t[q, hp]
                        dst = dd[i * 512:(i + 1) * 512, j * 64:(j + 1) * 64].rearrange("(s p) k -> p s k", p=128)
                        srcap = tk[:].rearrange("p (s h k) -> p s h k", s=4, h=2)[:, :, hp, :]
                        kb.dma(dst, srcap, reads=[tk])
        kb.barrier()


def rwkv_phase2(kb, nc, cx, T, strm, fm, yc_out, CH=16, same=True, strm_w=None):
    with contextlib.ExitStack() as st:
        pc = kb.sb([128, 14], name="r2_pc", stack=st)
        kb.dma(pc[:], cx.rw_pc, writes=[pc])
        eps = kb.sb([128, 1], name="r2_eps", stack=st)
        kb.op("dve", lambda e: e.memset(eps[:], 64e-5), writes=[eps])
        S = kb.sb([128, 128], name="r2_S", stack=st)
        kb.op("dve", lambda e: e.memset(S[:], 0.0), writes=[S])
        T1 = kb.sb([128, 128], name="r2_T1", stack=st)
        T2 = kb.sb([128, 128], name="r2_T2", stack=st)
        T4 = kb.sb([128, 128], name="r2_T4", stack=st)
        T3 = [kb.sb([128, 128], name="r2_T3%d" % i, stack=st) for i in range(4)]
        sa = kb.sb([128, 2], name="r2_sa", stack=st)
        bcs = [[kb.sb([128, CH, 128], (F32 if q == 1 else BF16), name="r2_bc%d_%d" % (q, b), stack=st) for b in range(2)] for q in range(5)]
        fms = [[kb.sb([128, 2, 512], name="r2_fm%d_%d" % (q, b), stack=st) for b in range(2)] for q in range(4)]
        ys = [kb.sb([128, 2, 512], name="r2_y%d" % b, stack=st) for b in range(2)]
        wk = [kb.sb([128, 512], name="r2_wk%d" % b, stack=st) for b in range(3)]
        pp = cx.psum
        v3 = lambda ap: ap.rearrange("p (j k) -> p j k", j=2)
        old_same = kb.same
        for i in range(T // 512):
            fb = [fms[q][i % 2] for q in range(4)]
            for q in range(4):
                kb.dma(fb[q][:], fm.t[q, :, :, i * 512:(i + 1) * 512].rearrange("j p t -> p j t"), writes=[fb[q]])
            Rf, Kf, Vf, Gf = fb
            Y = ys[i % 2]
            kb.same = same
            for cch in range(512 // CH):
                t0 = i * 512 + cch * CH
                cb = [bcs[q][cch % 2] for q in range(5)]
                for q in range(5):
                    for hp in range(2):
                        kb.dma(cb[q][hp * 64:(hp + 1) * 64, :, :],

                               (strm_w.t[hp] if q == 1 else strm.t[q, hp])[t0:t0 + CH, :].unsqueeze(0).broadcast_to([64, CH, 128]),
                               writes=[cb[q]])
                Ab, Wb, Bb, Kb, Rb = cb
                for s in range(CH):
                    tt = cch * CH + s
                    t3 = T3[tt % 4]
                    kb.op("pool", lambda e: e.tensor_tensor(out=v3(t3[:]), in0=v3(Kb[:, s, :]),
                                                            in1=Vf[:, :, tt:tt + 1].broadcast_to([128, 2, 64]), op=ALU.mult),
                          reads=[Kb, Vf], writes=[t3])
                    kb.op("dve", lambda e: e.tensor_tensor(out=T1[:], in0=S[:], in1=Ab[:, s, :], op=ALU.mult), reads=[S, Ab], writes=[T1])
                    kb.op("dve", lambda e: e.tensor_reduce(out=sa[:], in_=v3(T1[:]), axis=AX.X, op=ALU.add), reads=[T1], writes=[sa])
                    kb.op("dve", lambda e: e.tensor_tensor(out=S[:], in0=S[:], in1=Wb[:, s, :], op=ALU.mult), reads=[S, Wb], writes=[S])
                    kb.op("dve", lambda e: e.tensor_tensor(out=v3(T2[:]), in0=v3(Bb[:, s, :]), in1=sa[:].unsqueeze(2).broadcast_to([128, 2, 64]),
                                                           op=ALU.mult), reads=[Bb, sa], writes=[T2])
                    kb.op("dve", lambda e: e.tensor_tensor(out=S[:], in0=S[:], in1=T2[:], op=ALU.add), reads=[S, T2], writes=[S])
                    kb.op("dve", lambda e: e.tensor_tensor(out=S[:], in0=S[:], in1=t3[:], op=ALU.add), reads=[S, t3], writes=[S])
                    kb.op("dve", lambda e: e.tensor_tensor(out=T4[:], in0=S[:], in1=Rb[:, s, :], op=ALU.mult), reads=[S, Rb], writes=[T4])
                    kb.op("dve", lambda e: e.tensor_reduce(out=Y[:, :, tt], in_=v3(T4[:]), axis=AX.X, op=ALU.add), reads=[T4], writes=[Y])
            kb.same = old_same
            for j in range(2):
                yj = Y[:, j, :]
                a, b, c = wk
                kb.op("pe", lambda e: e.matmul(pp[0][:, :], lhsT=cx.bones[:], rhs=yj, start=True, stop=True), reads=[cx.bones, Y], writes=[pp[0]])
                kb.op("dve", lambda e: e.scalar_tensor_tensor(out=a[:], in0=pp[0][:, :], scalar=-1.0 / 64, in1=yj, op0=ALU.mult, op1=ALU.add),
                      reads=[pp[0], Y], writes=[a])
                kb.op("act", lambda e: e.activation(out=b[:], in_=a[:], func=AF.Square), reads=[a], writes=[b])
                kb.op("pe", lambda e: e.matmul(pp[1][:, :], lhsT=cx.bones[:], rhs=b[:], start=True, stop=True), reads=[cx.bones, b], writes=[pp[1]])
                kb.op("act", lambda e: e.activation(out=b[:], in_=pp[1][:, :], func=AF.Sqrt, scale=1.0 / 64, bias=eps[:]),
                      reads=[pp[1], eps], writes=[b])
                kb.op("dve", lambda e: e.reciprocal(out=b[:], in_=b[:]), reads=[b], writes=[b])
                kb.op("dve", lambda e: e.tensor_tensor(out=a[:], in0=a[:], in1=b[:], op=ALU.mult), reads=[a, b], writes=[a])
                kb.op("dve", lambda e: e.tensor_scalar(out=a[:], in0=a[:], scalar1=pc[:, 10 + j:11 + j], scalar2=pc[:, 12 + j:13 + j],
                                                       op0=ALU.mult, op1=ALU.add), reads=[a, pc], writes=[a])
                kb.op("dve", lambda e: e.scalar_tensor_tensor(out=b[:], in0=Rf[:, j, :], scalar=pc[:, 8 + j:9 + j], in1=Kf[:, j, :],
                                                              op0=ALU.mult, op1=ALU.mult), reads=[Rf, Kf, pc], writes=[b])
                kb.op("pe", lambda e: e.matmul(pp[2][:, :], lhsT=cx.bones[:], rhs=b[:], start=True, stop=True), reads=[cx.bones, b], writes=[pp[2]])
                kb.op("dve", lambda e: e.tensor_tensor(out=b[:], in0=pp[2][:, :], in1=Vf[:, j, :], op=ALU.mult), reads=[pp[2], Vf], writes=[b])
                kb.op("dve", lambda e: e.tensor_tensor(out=a[:], in0=a[:], in1=b[:], op=ALU.add), reads=[a, b], writes=[a])
                kb.op("dve", lambda e: e.tensor_tensor(out=c[:], in0=a[:], in1=Gf[:, j, :], op=ALU.mult), reads=[a, Gf], writes=[c])
                kb.dma(yc_out[j, :, i * 512:(i + 1) * 512], c[:], reads=[c], writes=[cx.OUT])
        kb.barrier()


def ssd_inputs(nc, cx):
    cx.ss_w = din(nc, "ss_w", [8, 128, 1024])
    cx.ss_pc = din(nc, "ss_pc", [128, 28])


def ssd_phase1(kb, nc, cx, T, hT_dram, fm, btok, ctok):
    with contextlib.ExitStack() as st:
        W = kb.sb([128, 8, 1024], name="ss_W", stack=st)
        kb.dma(W[:], cx.ss_w.rearrange("k p c -> p k c"), writes=[W])
        pc = kb.sb([128, 28], name="ss_pc", stack=st)
        kb.dma(pc[:], cx.ss_pc, writes=[pc])
        nega = kb.sb([128, 2], name="ss_nega", stack=st)
        kb.op("act", lambda e: e.activation(out=nega[:], in_=pc[:, 22:24], func=AF.Exp), reads=[pc], writes=[nega])
        kb.op("dve", lambda e: e.tensor_scalar(out=nega[:], in0=nega[:], scalar1=-1.0, scalar2=None, op0=ALU.mult), reads=[nega], writes=[nega])
        hTs = [kb.sb([128, 8, 512], name="ss_h%d" % i, stack=st) for i in range(2)]
        raw = [kb.sb([128, 515], name="ss_raw%d" % c, stack=st) for c in range(4)]
        for c in range(4):
            kb.op("dve", lambda e: e.memset(raw[c][:, 0:3], 0.0), writes=[raw[c]])
        cv = [kb.sb([128, 512], name="ss_cv%d" % c, stack=st) for c in range(4)]
        sz = kb.sb([128, 512], name="ss_sz", stack=st)
        dt = kb.sb([128, 512], name="ss_dt", stack=st)
        dec = kb.sb([128, 512], name="ss_dec", stack=st)
        xdt = kb.sb([128, 512], name="ss_xdt", stack=st)
        tok = [kb.sb([128, 512], name="ss_tok%d" % i, stack=st) for i in range(2)]
        pp = cx.psum
        for i in range(T // 512):
            hT = hTs[i % 2]
            ts = slice(i * 512, (i + 1) * 512)
            kb.dma(hT[:], hT_dram.t[:, :, ts].rearrange("k p t -> p k t"), writes=[hT])

            def proj(c, ps):
                for k in range(8):
                    kb.op("pe", lambda e: e.matmul(ps[:, :], lhsT=W[:, k, c * 128:(c + 1) * 128], rhs=hT[:, k, :], start=(k == 0), stop=(k == 7)),
                          reads=[W, hT], writes=[ps])
            for j in range(2):
                ps = pp[j]
                proj(j, ps)
                kb.op("act", lambda e: e.activation(out=sz[:], in_=ps[:, :], func=AF.Silu), reads=[ps], writes=[sz])
                kb.dma(fm.t[0, j, :, ts], sz[:], reads=[sz])
            for c in range(4):
                ps = pp[2 + c % 2]
                proj(2 + c, ps)
                r = raw[c]
                kb.op("act", lambda e: e.copy(out=r[:, 3:515], in_=ps[:, :]), reads=[ps], writes=[r])
                kb.op("dve", lambda e: e.tensor_scalar(out=cv[c][:], in0=r[:, 0:512], scalar1=pc[:, c * 4:c * 4 + 1], scalar2=None, op0=ALU.mult),
                      reads=[r, pc], writes=[cv[c]])
                for k in range(1, 4):
                    kb.op("dve", lambda e: e.scalar_tensor_tensor(out=cv[c][:], in0=r[:, k:k + 512], scalar=pc[:, c * 4 + k:c * 4 + k + 1], in1=cv[c][:],
                                                                  op0=ALU.mult, op1=ALU.add), reads=[r, pc, cv[c]], writes=[cv[c]])
                kb.op("act", lambda e: e.activation(out=cv[c][:], in_=cv[c][:], func=AF.Silu, bias=pc[:, 16 + c:17 + c]), reads=[cv[c], pc], writes=[cv[c]])
                kb.op("dve", lambda e: e.tensor_copy(out=r[:, 0:3], in_=r[:, 512:515]), reads=[r], writes=[r])
            for j in range(2):
                ps = pp[4 + j]
                proj(6 + j, ps)
                kb.op("act", lambda e: e.activation(out=dt[:], in_=ps[:, :], func=AF.Exp, bias=pc[:, 20 + j:21 + j]), reads=[ps, pc], writes=[dt])
                kb.op("act", lambda e: e.activation(out=dt[:], in_=dt[:], func=AF.Ln, bias=1.0), reads=[dt], writes=[dt])
                kb.op("act", lambda e: e.activation(out=dec[:], in_=dt[:], func=AF.Exp, scale=nega[:, j:j + 1]), reads=[dt, nega], writes=[dec])
                kb.op("dve", lambda e: e.tensor_tensor(out=xdt[:], in0=cv[j][:], in1=dt[:], op=ALU.mult), reads=[cv[j], dt], writes=[xdt])
                kb.dma(fm.t[1, j, :, ts], cv[j][:], reads=[cv[j]])
                kb.dma(fm.t[2, j, :, ts], xdt[:], reads=[xdt])
                kb.dma(fm.t[3, j, :, ts], dec[:], reads=[dec])
            for q, (src, dst) in enumerate(((cv[2], btok), (cv[3], ctok))):
                ps = pp[6 + q]
                tk = tok[q]
                for s in range(4):
                    kb.op("pe", lambda e: e.transpose(ps[:, s * 128:(s + 1) * 128], src[:, s * 128:(s + 1) * 128], cx.ident[:]),
                          reads=[src, cx.ident], writes=[ps])
                kb.op("act", lambda e: e.copy(out=tk[:], in_=ps[:, :]), reads=[ps], writes=[tk])
                kb.dma(dst.t[ts, :].rearrange("(s p) n -> p s n", p=128), tk[:].rearrange("p (s n) -> p s n", s=4), reads=[tk])
        kb.barrier()


def ssd_phase2(kb, nc, cx, T, fm, btok, ctok, ya_out, CH=16, same=True):
    with contextlib.ExitStack() as st:
        pc = kb.sb([128, 28], name="s2_pc", stack=st)
        kb.dma(pc[:], cx.ss_pc, writes=[pc])
        S = kb.sb([128, 2, 128], name="s2_S", stack=st)
        kb.op("dve", lambda e: e.memset(S[:], 0.0), writes=[S])
        T4 = kb.sb([128, 2, 128], name="s2_T4", stack=st)
        T3 = [kb.sb([128, 2, 128], name="s2_T3%d" % i, stack=st) for i in range(4)]
        bcs = [[kb.sb([128, CH, 128], name="s2_bc%d_%d" % (q, b), stack=st) for b in range(2)] for q in range(2)]
        fms = [[kb.sb([128, 2, 512], name="s2_fm%d_%d" % (q, b), stack=st) for b in range(2)] for q in range(4)]
        ys = [kb.sb([128, 2, 512], name="s2_y%d" % b, stack=st) for b in range(2)]
        wk = [kb.sb([128, 512], name="s2_wk%d" % b, stack=st) for b in range(3)]
        rstd = kb.sb([128, 512], name="s2_rstd", stack=st)
        pp = cx.psum
        old_same = kb.same
        for i in range(T // 512):
            ts = slice(i * 512, (i + 1) * 512)
            fb = [fms[q][i % 2] for q in range(4)]
            for q in range(4):
                kb.dma(fb[q][:], fm.t[q, :, :, ts].rearrange("j p t -> p j t"), writes=[fb[q]])
            SZ, XS, XDT, DEC = fb
            Y = ys[i % 2]
            kb.same = same
            for cch in range(512 // CH):
                t0 = i * 512 + cch * CH
                Bb, Cb = bcs[0][cch % 2], bcs[1][cch % 2]
                kb.dma(Bb[:], btok.t[t0:t0 + CH, :].unsqueeze(0).broadcast_to([128, CH, 128]), writes=[Bb])
                kb.dma(Cb[:], ctok.t[t0:t0 + CH, :].unsqueeze(0).broadcast_to([128, CH, 128]), writes=[Cb])
                for s in range(CH):
                    tt = cch * CH + s
                    t3 = T3[tt % 4]
                    kb.op("pool", lambda e: e.tensor_tensor(out=t3[:], in0=Bb[:, s, :].unsqueeze(1).broadcast_to([128, 2, 128]),
                                                            in1=XDT[:, :, tt:tt + 1].broadcast_to([128, 2, 128]), op=ALU.mult),
                          reads=[Bb, XDT], writes=[t3])
                    kb.op("dve", lambda e: e.tensor_tensor(out=S[:], in0=S[:], in1=DEC[:, :, tt:tt + 1].broadcast_to([128, 2, 128]), op=ALU.mult),
                          reads=[S, DEC], writes=[S])
                    kb.op("dve", lambda e: e.tensor_tensor(out=S[:], in0=S[:], in1=t3[:], op=ALU.add), reads=[S, t3], writes=[S])
                    kb.op("dve", lambda e: e.tensor_tensor(out=T4[:], in0=S[:], in1=Cb[:, s, :].unsqueeze(1).broadcast_to([128, 2, 128]), op=ALU.mult),
                          reads=[S, Cb], writes=[T4])
                    kb.op("dve", lambda e: e.tensor_reduce(out=Y[:, :, tt], in_=T4[:], axis=AX.X, op=ALU.add), reads=[T4], writes=[Y])
            kb.same = old_same
            a, b, c = wk
            ygs = []
            for j in range(2):
                yj = Y[:, j, :]
                kb.op("dve", lambda e: e.scalar_tensor_tensor(out=yj, in0=XS[:, j, :], scalar=pc[:, 24 + j:25 + j], in1=yj, op0=ALU.mult, op1=ALU.add),
                      reads=[XS, pc, Y], writes=[Y])
                kb.op("dve", lambda e: e.tensor_tensor(out=yj, in0=yj, in1=SZ[:, j, :], op=ALU.mult), reads=[Y, SZ], writes=[Y])
                kb.op("act", lambda e: e.activation(out=a[:], in_=yj, func=AF.Square), reads=[Y], writes=[a])
                kb.op("pe", lambda e: e.matmul(pp[0][:, :], lhsT=cx.ones[:], rhs=a[:], start=(j == 0), stop=(j == 1)), reads=[cx.ones, a], writes=[pp[0]])
            kb.op("act", lambda e: e.activation(out=rstd[:], in_=pp[0][:, :], func=AF.Sqrt, scale=1.0 / 256, bias=cx.eps6[:]),
                  reads=[pp[0], cx.eps6], writes=[rstd])
            kb.op("dve", lambda e: e.reciprocal(out=rstd[:], in_=rstd[:]), reads=[rstd], writes=[rstd])
            for j in range(2):
                o = (b, c)[j]
                kb.op("dve", lambda e: e.scalar_tensor_tensor(out=o[:], in0=Y[:, j, :], scalar=pc[:, 26 + j:27 + j], in1=rstd[:], op0=ALU.mult, op1=ALU.mult),
                      reads=[Y, pc, rstd], writes=[o])
                kb.dma(ya_out[j, :, ts], o[:], reads=[o], writes=[cx.OUT])
        kb.barrier()


NEGM = -30000.0


def moba_inputs(nc, cx):
    cx.mb_wqk = din(nc, "mb_wqk", [8, 128, 512])
    cx.mb_wv = din(nc, "mb_wv", [8, 128, 256])
    cx.mb_g = din(nc, "mb_g", [128, 2])
    cx.mb_rel = din(nc, "mb_rel", [1, 64])
    cx.mb_bi = din(nc, "mb_bi", [128, 1280])


def moba_phase1(kb, nc, cx, T, hT_dram, qT_d, kT_d, v_d, nm_d):
    with contextlib.ExitStack() as st:
        Wqk = kb.sb([128, 8, 512], name="mb_Wqk", stack=st)
        kb.dma(Wqk[:], cx.mb_wqk.rearrange("k p c -> p k c"), writes=[Wqk])
        Wv = kb.sb([128, 8, 256], name="mb_Wv", stack=st)
        kb.dma(Wv[:], cx.mb_wv.rearrange("k p c -> p k c"), writes=[Wv])
        gg = kb.sb([128, 2], name="mb_g", stack=st)
        kb.dma(gg[:], cx.mb_g, writes=[gg])
        hTs = [kb.sb([128, 8, 512], name="mb_h%d" % i, stack=st) for i in range(2)]
        kmT = [kb.sb([128, 64], name="mb_km%d" % h, stack=st) for h in range(2)]
        for h in range(2):
            kb.op("dve", lambda e: e.memset(kmT[h][:], 0.0), writes=[kmT[h]])
        rawt = kb.sb([128, 512], name="mb_raw", stack=st)
        sq = kb.sb([128, 512], name="mb_sq", stack=st)
        rstd = kb.sb([128, 512], name="mb_rstd", stack=st)
        QT = kb.sb([128, 512], name="mb_QT", stack=st)
        KT = kb.sb([128, 512], name="mb_KT", stack=st)
        QS = kb.sb([128, 512], BF16, name="mb_QS", stack=st)
        KTb = kb.sb([128, 512], BF16, name="mb_KTb", stack=st)
        gt = kb.sb([128, 64], name="mb_gt", stack=st)
        sel = kb.sb([128, 64], name="mb_sel", stack=st)
        mx = kb.sb([128, 8], name="mb_mx", stack=st)
        nmT = kb.sb([64, 512], BF16, name="mb_nmT", stack=st)
        va = [kb.sb([128, 4, 2, 129], BF16, name="mb_va%d" % i, stack=st) for i in range(2)]
        for i in range(2):
            kb.op("dve", lambda e: e.memset(va[i][:], 1.0), writes=[va[i]])
        pp = cx.psum
        for i in range(T // 512):
            hT = hTs[i % 2]
            ts = slice(i * 512, (i + 1) * 512)
            kb.dma(hT[:], hT_dram.t[:, :, ts].rearrange("k p t -> p k t"), writes=[hT])

            def normed(c, gcol, dst):
                ps = pp[c % 2]
                for k in range(8):
                    kb.op("pe", lambda e: e.matmul(ps[:, :], lhsT=Wqk[:, k, c * 128:(c + 1) * 128], rhs=hT[:, k, :], start=(k == 0), stop=(k == 7)),
                          reads=[Wqk, hT], writes=[ps])
                kb.op("act", lambda e: e.copy(out=rawt[:], in_=ps[:, :]), reads=[ps], writes=[rawt])
                kb.op("act", lambda e: e.activation(out=sq[:], in_=rawt[:], func=AF.Square), reads=[rawt], writes=[sq])
                kb.op("pe", lambda e: e.matmul(pp[2][:, :], lhsT=cx.ones[:], rhs=sq[:], start=True, stop=True), reads=[cx.ones, sq], writes=[pp[2]])
                kb.op("act", lambda e: e.activation(out=rstd[:], in_=pp[2][:, :], func=AF.Sqrt, scale=1.0 / 128, bias=cx.eps6[:]),
                      reads=[pp[2], cx.eps6], writes=[rstd])
                kb.op("dve", lambda e: e.reciprocal(out=rstd[:], in_=rstd[:]), reads=[rstd], writes=[rstd])
                kb.op("dve", lambda e: e.scalar_tensor_tensor(out=dst[:], in0=rawt[:], scalar=gg[:, gcol:gcol + 1], in1=rstd[:], op0=ALU.mult, op1=ALU.mult),
                      reads=[rawt, gg, rstd], writes=[dst])
            for hh in range(2):
                normed(2 + hh, 1, KT)
                kb.op("act", lambda e: e.copy(out=KTb[:], in_=KT[:]), reads=[KT], writes=[KTb])
                kb.dma(kT_d.t[hh, :, ts], KTb[:], reads=[KTb])
                kb.op("dve", lambda e: e.tensor_reduce(out=kmT[hh][:, 2 * i:2 * i + 2], in_=KT[:].rearrange("p (b t) -> p b t", b=2), axis=AX.X, op=ALU.add),
                      reads=[KT], writes=[kmT[hh]])
                normed(hh, 0, QT)
                kb.op("dve", lambda e: e.tensor_scalar(out=QS[:], in0=QT[:], scalar1=128.0 ** -0.5, scalar2=None, op0=ALU.mult), reads=[QT], writes=[QS])
                kb.dma(qT_d.t[hh, :, ts], QS[:], reads=[QS])
                for s in range(4):
                    qb = 2 * i + s // 2
                    if qb == 0:
                        kb.op("dve", lambda e: e.memset(sel[:], 0.0), writes=[sel])
                    else:
                        kb.op("pe", lambda e: e.matmul(pp[3][:, 0:64], lhsT=QT[:, s * 128:(s + 1) * 128], rhs=kmT[hh][:, 0:64], start=True, stop=True),
                              reads=[QT, kmT[hh]], writes=[pp[3]])
                        kb.op("dve", lambda e: e.memset(gt[:], -1e30), writes=[gt])
                        kb.op("dve", lambda e: e.tensor_copy(out=gt[:, 0:qb], in_=pp[3][:, 0:qb]), reads=[pp[3]], writes=[gt])
                        if qb >= 3:
                            kb.op("dve", lambda e: e.max(out=mx[:], in_=gt[:]), reads=[gt], writes=[mx])
                            kb.op("dve", lambda e: e.tensor_scalar(out=sel[:], in0=gt[:], scalar1=mx[:, 2:3], scalar2=None, op0=ALU.is_ge),
                                  reads=[gt, mx], writes=[sel])
                        else:
                            kb.op("dve", lambda e: e.tensor_scalar(out=sel[:], in0=gt[:], scalar1=-1e29, scalar2=None, op0=ALU.is_ge),
                                  reads=[gt], writes=[sel])
                    kb.op("dve", lambda e: e.tensor_scalar(out=sel[:], in0=sel[:], scalar1=-1.0, scalar2=-NEGM, op0=ALU.add, op1=ALU.mult),
                          reads=[sel], writes=[sel])
                    kb.op("pe", lambda e: e.transpose(pp[4][0:64, 0:128], sel[:, 0:64], cx.ident[:]), reads=[sel, cx.ident], writes=[pp[4]])
                    kb.op("act", lambda e: e.copy(out=nmT[0:64, s * 128:(s + 1) * 128], in_=pp[4][0:64, 0:128]), reads=[pp[4]], writes=[nmT])
                kb.dma(nm_d.t[hh, :, ts], nmT[0:64, :], reads=[nmT])
            vt = va[i % 2]
            for s in range(4):
                ps = pp[5 + s % 2]
                for k in range(8):
                    kb.op("pe", lambda e: e.matmul(ps[:, 0:256], lhsT=hT[:, k, s * 128:(s + 1) * 128], rhs=Wv[:, k, :], start=(k == 0), stop=(k == 7)),
                          reads=[hT, Wv], writes=[ps])
                kb.op("act", lambda e: e.copy(out=vt[:, s, :, 0:128], in_=ps[:, 0:256].rearrange("p (h d) -> p h d", h=2)), reads=[ps], writes=[vt])
            for hh in range(2):
                kb.dma(v_d.t[hh, ts, :].rearrange("(s p) c -> p s c", p=128), vt[:, :, hh, :], reads=[vt])
        kb.barrier()


def moba_phase2(kb, nc, cx, T, qT_d, kT_d, v_d, nm_d, yb_out):
    NT = T // 128
    NG = T // 256
    with contextlib.ExitStack() as st:
        KT = kb.sb([128, T], BF16, name="m2_KT", stack=st)
        VA = kb.sb([128, NT, 129], BF16, name="m2_VA", stack=st)
        En = kb.sb([64, 64, 128], BF16, name="m2_En", stack=st)
        identb = kb.sb([128, 128], BF16, name="m2_identb", stack=st)
        kb.op("dve", lambda e: e.tensor_copy(out=identb[:], in_=cx.ident[:]), reads=[cx.ident], writes=[identb])
        kb.op("dve", lambda e: e.tensor_copy(out=En[:], in_=cx.ident[0:64, 0:64].unsqueeze(2).broadcast_to([64, 64, 128])), reads=[cx.ident], writes=[En])
        BI = kb.sb([128, 1280], name="m2_BI", stack=st)
        kb.dma(BI[:], cx.mb_bi, writes=[BI])
        rel = kb.sb([128, 64], name="m2_rel", stack=st)
        kb.dma(rel[:], cx.mb_rel.broadcast_to([128, 64]), writes=[rel])
        BB = kb.sb([128, 1280], name="m2_BB", stack=st)
        BBb = kb.sb([128, 1280], BF16, name="m2_BBb", stack=st)
        tmpb = kb.sb([128, 1280], name="m2_tmpb", stack=st)
        QG = [kb.sb([128, 256], BF16, name="m2_QG%d" % i, stack=st) for i in range(2)]
        NM = [kb.sb([64, 256], BF16, name="m2_NM%d" % i, stack=st) for i in range(2)]
        PT = [kb.sb([128, 256], BF16, name="m2_PT%d" % i, stack=st) for i in range(3)]
        yt = [kb.sb([128, 2, 128], name="m2_yt%d" % i, stack=st) for i in range(2)]
        rinv = kb.sb([128, 1], name="m2_rinv", stack=st)
        pp = cx.psum
        npt = 0
        for hh in range(2):
            kb.dma(KT[:], kT_d.t[hh, :, :], writes=[KT])
            kb.dma(VA[:], v_d.t[hh, :, :].rearrange("(n p) c -> p n c", p=128), writes=[VA])
            kb.op("dve", lambda e: e.tensor_scalar(out=BB[:], in0=BI[:], scalar1=-1.0, scalar2=NEGM, op0=ALU.is_equal, op1=ALU.mult), reads=[BI], writes=[BB])
            for b in range(32):
                kb.op("dve", lambda e: e.tensor_scalar(out=tmpb[:], in0=BI[:], scalar1=float(b), scalar2=rel[:, hh * 32 + b:hh * 32 + b + 1],
                                                       op0=ALU.is_equal, op1=ALU.mult), reads=[BI, rel], writes=[tmpb])
                kb.op("dve", lambda e: e.tensor_tensor(out=BB[:], in0=BB[:], in1=tmpb[:], op=ALU.add), reads=[BB, tmpb], writes=[BB])
            kb.op("act", lambda e: e.copy(out=BBb[:], in_=BB[:]), reads=[BB], writes=[BBb])
            for g in range(NG):
                qg, nm = QG[g % 2], NM[g % 2]
                gs = slice(g * 256, (g + 1) * 256)
                kb.dma(qg[:], qT_d.t[hh, :, gs], writes=[qg])
                kb.dma(nm[:], nm_d.t[hh, :, gs], writes=[nm])
                O = [pp[(g % 2) * 2 + 0], pp[(g % 2) * 2 + 1]]
                for kt in range(2 * g + 2):
                    n = kt // 2
                    past = n < g
                    if kt == 2 * g + 1:
                        q0, w, b0 = 128, 128, 0
                    else:
                        q0, w, b0 = 0, 256, min(2 * g - kt, 8) * 128
                    S = pp[4 + npt % 2]
                    pt = PT[npt % 3]
                    npt += 1
                    kb.op("pe", lambda e: e.matmul(S[:, 0:w], lhsT=KT[:, kt * 128:(kt + 1) * 128], rhs=qg[:, q0:q0 + w], start=True, stop=False),
                          reads=[KT, qg], writes=[S])
                    kb.op("pe", lambda e: e.matmul(S[:, 0:w], lhsT=identb[:], rhs=BBb[:, b0:b0 + w], start=False, stop=(not past)),
                          reads=[identb, BBb], writes=[S])
                    if past:
                        kb.op("pe", lambda e: e.matmul(S[:, 0:w], lhsT=En[0:64, n, :], rhs=nm[0:64, q0:q0 + w], start=False, stop=True),
                              reads=[En, nm], writes=[S])
                    kb.op("act", lambda e: e.activation(out=pt[:, 0:w], in_=S[:, 0:w], func=AF.Exp), reads=[S], writes=[pt])
                    for qq in range(2):
                        if kt == 2 * g + 1 and qq == 0:
                            continue
                        c0 = qq * 128 - q0
                        last = (2 * g) if qq == 0 else (2 * g + 1)
                        kb.op("pe", lambda e: e.matmul(O[qq][:, 0:129], lhsT=pt[:, c0:c0 + 128], rhs=VA[:, kt, :], start=(kt == 0), stop=(kt == last)),
                              reads=[pt, VA], writes=[O[qq]])
                y = yt[g % 2]
                for qq in range(2):
                    kb.op("dve", lambda e: e.reciprocal(out=rinv[:], in_=O[qq][:, 128:129]), reads=[O[qq]], writes=[rinv])
                    kb.op("dve", lambda e: e.tensor_scalar(out=y[:, qq, :], in0=O[qq][:, 0:128], scalar1=rinv[:, 0:1], scalar2=None, op0=ALU.mult),
                          reads=[O[qq], rinv], writes=[y])
                kb.dma(yb_out[hh, gs, :].rearrange("(q p) d -> p q d", p=128), y[:], reads=[y], writes=[cx.OUT])
        kb.barrier()


def lb_inputs(nc, cx, TB):
    cx.xT = din(nc, "xT", [8, 128, TB]); cx.xtok = din(nc, "xtok", [TB, 1024])
    cx.yT = din(nc, "yT", [24, 128, TB])
    cx.g1 = din(nc, "g1", [128, 8]); cx.g2 = din(nc, "g2", [128, 8])
    cx.wg = din(nc, "wg", [8, 128, 3072]); cx.bg = din(nc, "bg", [128, 24])
    cx.pall = din(nc, "pall", [24, 128, 1024])
    cx.wout = din(nc, "wout", [8, 128, 1024])
    cx.wq = din(nc, "wq", [8, 128, 2048])
    cx.k12T = din(nc, "k12T", [128, 256])
    cx.uT = din(nc, "uT", [8, 128, 16384])
    cx.vv = din(nc, "vv", [16384, 1024])


def lb_gate_merge(kb, nc, cx, TB, hT_d, mT_d):
    with contextlib.ExitStack() as st:
        bg = kb.sb([128, 24], name="b1_bg", stack=st)
        kb.dma(bg[:], cx.bg, writes=[bg])
        Wg = [kb.sb([128, 8, 3, 128], name="b1_wg%d" % i, stack=st) for i in range(2)]
        Pc = [kb.sb([128, 24, 128], name="b1_pc%d" % i, stack=st) for i in range(2)]
        hTs = [kb.sb([128, 8, 512], name="b1_h%d" % i, stack=st) for i in range(2)]
        yTs = [kb.sb([128, 24, 512], name="b1_y%d" % i, stack=st) for i in range(2)]
        gs = [kb.sb([128, 512], name="b1_g%d" % i, stack=st) for i in range(3)]
        m = [kb.sb([128, 512], name="b1_m%d" % i, stack=st) for i in range(2)]
        t = kb.sb([128, 512], name="b1_t", stack=st)
        pp = cx.psum
        it = 0
        for c in range(8):
            wg, pc = Wg[c % 2], Pc[c % 2]
            for br in range(3):
                kb.dma(wg[:, :, br, :], cx.wg[:, :, br * 1024 + c * 128:br * 1024 + (c + 1) * 128].rearrange("k p c -> p k c"), writes=[wg])
            kb.dma(pc[:], cx.pall[:, :, c * 128:(c + 1) * 128].rearrange("k p c -> p k c"), writes=[pc])
            for ti in range(TB // 512):
                ts = slice(ti * 512, (ti + 1) * 512)
                hT, yT = hTs[it % 2], yTs[it % 2]
                mm = m[it % 2]
                it += 1
                kb.dma(hT[:], hT_d.t[:, :, ts].rearrange("k p t -> p k t"), writes=[hT])
                kb.dma(yT[:], cx.yT[:, :, ts].rearrange("k p t -> p k t"), writes=[yT])
                for br in range(3):
                    pg, pj = pp[br], pp[3 + br]
                    for k in range(8):
                        kb.op("pe", lambda e: e.matmul(pg[:, :], lhsT=wg[:, k, br, :], rhs=hT[:, k, :], start=(k == 0), stop=(k == 7)), reads=[wg, hT], writes=[pg])
                    kb.op("act", lambda e: e.activation(out=gs[br][:], in_=pg[:, :], func=AF.Sigmoid, bias=bg[:, br * 8 + c:br * 8 + c + 1]),
                          reads=[pg, bg], writes=[gs[br]])
                    for k in range(8):
                        kb.op("pe", lambda e: e.matmul(pj[:, :], lhsT=pc[:, br * 8 + k, :], rhs=yT[:, br * 8 + k, :], start=(k == 0), stop=(k == 7)), reads=[pc, yT], writes=[pj])
                    if br == 0:
                        kb.op("dve", lambda e: e.tensor_tensor(out=mm[:], in0=gs[br][:], in1=pj[:, :], op=ALU.mult), reads=[gs[br], pj], writes=[mm])
                    else:
                        kb.op("dve", lambda e: e.tensor_tensor(out=t[:], in0=gs[br][:], in1=pj[:, :], op=ALU.mult), reads=[gs[br], pj], writes=[t])
                        kb.op("dve", lambda e: e.tensor_tensor(out=mm[:], in0=mm[:], in1=t[:], op=ALU.add), reads=[mm, t], writes=[mm])
                kb.dma(mT_d.t[c, :, ts], mm[:], reads=[mm])
        kb.barrier()


def lb_outproj(kb, nc, cx, TB, mT_d, xnT_d, xn_tok_d):
    with contextlib.ExitStack() as st:
        Wo = kb.sb([128, 8, 1024], name="b2_wo", stack=st)
        kb.dma(Wo[:], cx.wout.rearrange("k p c -> p k c"), writes=[Wo])
        mTs = [kb.sb([128, 8, 512], name="b2_m%d" % i, stack=st) for i in range(2)]
        xTs = [kb.sb([128, 8, 512], name="b2_x%d" % i, stack=st) for i in range(2)]
        xn = [kb.sb([128, 8, 512], name="b2_xn%d" % i, stack=st) for i in range(2)]
        xt = [kb.sb([128, 1024], name="b2_xt%d" % i, stack=st) for i in range(2)]
        pp = cx.psum
        nx = 0
        for ti in range(TB // 512):
            ts = slice(ti * 512, (ti + 1) * 512)
            mT, xT, xo = mTs[ti % 2], xTs[ti % 2], xn[ti % 2]
            kb.dma(mT[:], mT_d.t[:, :, ts].rearrange("k p t -> p k t"), writes=[mT])
            kb.dma(xT[:], cx.xT[:, :, ts].rearrange("k p t -> p k t"), writes=[xT])
            for c in range(8):
                ps = pp[c % 2]
                for k in range(8):
                    kb.op("pe", lambda e: e.matmul(ps[:, :], lhsT=Wo[:, k, c * 128:(c + 1) * 128], rhs=mT[:, k, :], start=(k == 0), stop=(k == 7)), reads=[Wo, mT], writes=[ps])
                kb.op("dve", lambda e: e.tensor_tensor(out=xo[:, c, :], in0=ps[:, :], in1=xT[:, c, :], op=ALU.add), reads=[ps, xT], writes=[xo])
            kb.dma(xnT_d.t[:, :, ts].rearrange("k p t -> p k t"), xo[:], reads=[xo])
            for s in range(4):
                xk = xt[nx % 2]
                nx += 1
                r0 = ti * 512 + s * 128
                kb.dma(xk[:], cx.xtok[r0:r0 + 128, :], writes=[xk])
                for hf in range(2):
                    ps = pp[2 + hf]
                    for k in range(8):
                        kb.op("pe", lambda e: e.matmul(ps[:, :], lhsT=mT[:, k, s * 128:(s + 1) * 128], rhs=Wo[:, k, hf * 512:(hf + 1) * 512], start=(k == 0), stop=(k == 7)),
                              reads=[Wo, mT], writes=[ps])
                    kb.op("dve", lambda e: e.tensor_tensor(out=xk[:, hf * 512:(hf + 1) * 512], in0=ps[:, :], in1=xk[:, hf * 512:(hf + 1) * 512], op=ALU.add),
                          reads=[ps, xk], writes=[xk])
                kb.dma(xn_tok_d.t[r0:r0 + 128, :], xk[:], reads=[xk])
        kb.barrier()


def lb_peer_scores(kb, nc, cx, TB, h2T_d, s_d):
    with contextlib.ExitStack() as st:
        Wq = kb.sb([128, 8, 2048], name="b3_wq", stack=st)
        kb.dma(Wq[:], cx.wq.rearrange("k p c -> p k c"), writes=[Wq])
        k12 = kb.sb([128, 256], name="b3_k12", stack=st)
        kb.dma(k12[:], cx.k12T, writes=[k12])
        hTs = [kb.sb([128, 8, 512], name="b3_h%d" % i, stack=st) for i in range(2)]
        qT = [kb.sb([128, 512], name="b3_q%d" % i, stack=st) for i in range(2)]
        sc = [kb.sb([128, 4, 128], name="b3_s%d" % i, stack=st) for i in range(2)]
        pp = cx.psum
        for ti in range(TB // 512):
            ts = slice(ti * 512, (ti + 1) * 512)
            hT = hTs[ti % 2]
            kb.dma(hT[:], h2T_d.t[:, :, ts].rearrange("k p t -> p k t"), writes=[hT])
            for ct in range(16):
                ps = pp[ct % 2]
                q = qT[ct % 2]
                so = sc[ct % 2]
                for k in range(8):
                    kb.op("pe", lambda e: e.matmul(ps[:, :], lhsT=Wq[:, k, ct * 128:(ct + 1) * 128], rhs=hT[:, k, :], start=(k == 0), stop=(k == 7)), reads=[Wq, hT], writes=[ps])
                kb.op("act", lambda e: e.copy(out=q[:], in_=ps[:, :]), reads=[ps], writes=[q])
                p2 = pp[2 + ct % 2]
                for s in range(4):
                    kb.op("pe", lambda e: e.matmul(p2[:, s * 128:(s + 1) * 128], lhsT=q[:, s * 128:(s + 1) * 128], rhs=k12[:, (ct % 2) * 128:(ct % 2) * 128 + 128], start=True, stop=True),
                          reads=[q, k12], writes=[p2])
                kb.op("dve", lambda e: e.tensor_copy(out=so[:], in_=p2[:, :].rearrange("p (s n) -> p s n", s=4)), reads=[p2], writes=[so])
                kb.dma(s_d.t[ct % 2, ts, ct // 2, :].rearrange("(s p) n -> p s n", p=128), so[:], reads=[so])
        kb.barrier()


def lb_peer_main(kb, nc, cx, TB, h2T_d, s_d, xn_tok_d, out_d, NQ=4):
    QI = 128 // NQ
    QE = QI * 128
    with contextlib.ExitStack() as st:
        h2 = [kb.sb([128, 8, 128], name="b4_h%d" % i, stack=st) for i in range(2)]
        s1 = [kb.sb([128, 8, 128], name="b4_s1%d" % i, stack=st) for i in range(2)]
        s2 = [kb.sb([128, 8, 128], name="b4_s2%d" % i, stack=st) for i in range(2)]
        xk = [kb.sb([128, 1024], name="b4_xk%d" % i, stack=st) for i in range(2)]
        wk = kb.sb([128, 256], name="b4_wk", stack=st)
        v12 = kb.sb([128, 2, 16], name="b4_v12", stack=st)
        cand = kb.sb([128, 16, 16], name="b4_cand", stack=st)
        c16 = kb.sb([128, 16], name="b4_c16", stack=st)
        e16 = kb.sb([128, 16], name="b4_e16", stack=st)
        tau = kb.sb([128, 8], name="b4_tau", stack=st)
        negm = kb.sb([128, 8], name="b4_negm", stack=st)
        rz = kb.sb([128, 8], name="b4_rz", stack=st)
        E = kb.sb([128, QI, 128], name="b4_E", stack=st)
        M = kb.sb([128, QI, 128], name="b4_M", stack=st)
        G = kb.sb([128, QE], name="b4_G", stack=st)
        Ac = [kb.sb([128, 512], name="b4_A%d" % i, stack=st) for i in range(2)]
        GAT = [kb.sb([128, 4, 128], name="b4_GAT%d" % i, stack=st) for i in range(2)]
        uc = [kb.sb([128, 8, 512], name="b4_u%d" % i, stack=st) for i in range(2)]
        vc = [kb.sb([128, 4, 1024], name="b4_v%d" % i, stack=st) for i in range(2)]
        pp = cx.psum
        nch = 0
        for ti in range(TB // 128):
            r0 = ti * 128
            h, a1, a2, xo = h2[ti % 2], s1[ti % 2], s2[ti % 2], xk[ti % 2]
            kb.dma(h[:], h2T_d.t[:, :, r0:r0 + 128].rearrange("k p t -> p k t"), writes=[h])
            kb.dma(a1[:], s_d.t[0, r0:r0 + 128, :, :], writes=[a1])
            kb.dma(a2[:], s_d.t[1, r0:r0 + 128, :, :], writes=[a2])
            kb.dma(xo[:], xn_tok_d.t[r0:r0 + 128, :], writes=[xo])
            for hd in range(8):
                for w, src in enumerate((a1, a2)):
                    kb.op("dve", lambda e: e.max(out=v12[:, w, 0:8], in_=src[:, hd, :]), reads=[src], writes=[v12])
                    kb.op("dve", lambda e: e.match_replace(out=wk[:, 0:128], in_to_replace=v12[:, w, 0:8], in_values=src[:, hd, :], imm_value=-1e30),
                          reads=[src, v12], writes=[wk])
                    kb.op("dve", lambda e: e.max(out=v12[:, w, 8:16], in_=wk[:, 0:128]), reads=[wk], writes=[v12])
                kb.op("dve", lambda e: e.tensor_tensor(out=cand[:], in0=v12[:, 0, :].unsqueeze(2).broadcast_to([128, 16, 16]),
                                                       in1=v12[:, 1, :].unsqueeze(1).broadcast_to([128, 16, 16]), op=ALU.add), reads=[v12], writes=[cand])
                cf = cand[:].rearrange("p a b -> p (a b)")
                kb.op("dve", lambda e: e.max(out=c16[:, 0:8], in_=cf), reads=[cand], writes=[c16])
                kb.op("dve", lambda e: e.match_replace(out=wk[:], in_to_replace=c16[:, 0:8], in_values=cf, imm_value=-1e30), reads=[cand, c16], writes=[wk])
                kb.op("dve", lambda e: e.max(out=c16[:, 8:16], in_=wk[:]), reads=[wk], writes=[c16])
                kb.op("dve", lambda e: e.tensor_copy(out=tau[:, hd:hd + 1], in_=c16[:, 15:16]), reads=[c16], writes=[tau])
                kb.op("dve", lambda e: e.tensor_scalar(out=negm[:, hd:hd + 1], in0=c16[:, 0:1], scalar1=-1.0, scalar2=None, op0=ALU.mult), reads=[c16], writes=[negm])
                kb.op("act", lambda e: e.activation(out=e16[:], in_=c16[:], func=AF.Exp, bias=negm[:, hd:hd + 1]), reads=[c16, negm], writes=[e16])
                kb.op("dve", lambda e: e.tensor_reduce(out=rz[:, hd:hd + 1], in_=e16[:], axis=AX.X, op=ALU.add), reads=[e16], writes=[rz])
            kb.op("dve", lambda e: e.reciprocal(out=rz[:], in_=rz[:]), reads=[rz], writes=[rz])
            O = [pp[4], pp[5]]
            for qi in range(NQ):
                kb.op("pool", lambda e: e.memset(G[:], 0.0), writes=[G])
                G3 = G[:].rearrange("p (i j) -> p i j", j=128)
                for hd in range(8):
                    kb.op("dve", lambda e: e.tensor_tensor(out=E[:], in0=a1[:, hd, qi * QI:(qi + 1) * QI].unsqueeze(2).broadcast_to([128, QI, 128]),
                                                           in1=a2[:, hd, :].unsqueeze(1).broadcast_to([128, QI, 128]), op=ALU.add), reads=[a1, a2], writes=[E])
                    kb.op("dve", lambda e: e.tensor_scalar(out=M[:], in0=E[:], scalar1=tau[:, hd:hd + 1], scalar2=None, op0=ALU.is_ge), reads=[E, tau], writes=[M])
                    kb.op("act", lambda e: e.activation(out=E[:], in_=E[:], func=AF.Exp, bias=negm[:, hd:hd + 1]), reads=[E, negm], writes=[E])
                    kb.op("pool", lambda e: e.tensor_tensor(out=M[:], in0=M[:], in1=E[:], op=ALU.mult), reads=[M, E], writes=[M])
                    kb.op("dve", lambda e: e.scalar_tensor_tensor(out=G3, in0=M[:], scalar=rz[:, hd:hd + 1], in1=G3, op0=ALU.mult, op1=ALU.add),
                          reads=[M, rz, G], writes=[G])
                for ch in range(QE // 512):
                    e0 = qi * QE + ch * 512
                    u, v = uc[nch % 2], vc[nch % 2]
                    A, gat = Ac[nch % 2], GAT[nch % 2]
                    pa, ptr = pp[nch % 2], pp[2 + nch % 2]
                    nch += 1
                    kb.dma(u[:], cx.uT[:, :, e0:e0 + 512].rearrange("k p e -> p k e"), writes=[u])
                    kb.dma(v[:], cx.vv[e0:e0 + 512, :].rearrange("(s p) d -> p s d", p=128), writes=[v])
                    for k in range(8):
                        kb.op("pe", lambda e: e.matmul(pa[:, :], lhsT=h[:, k, :], rhs=u[:, k, :], start=(k == 0), stop=(k == 7)), reads=[h, u], writes=[pa])
                    kb.op("act", lambda e: e.activation(out=A[:], in_=pa[:, :], func=AF.Gelu), reads=[pa], writes=[A])
                    kb.op("dve", lambda e: e.tensor_tensor(out=A[:], in0=A[:], in1=G[:, ch * 512:(ch + 1) * 512], op=ALU.mult), reads=[A, G], writes=[A])
                    for s in range(4):
                        kb.op("pe", lambda e: e.transpose(ptr[:, s * 128:(s + 1) * 128], A[:, s * 128:(s + 1) * 128], cx.ident[:]), reads=[A, cx.ident], writes=[ptr])
                    kb.op("act", lambda e: e.copy(out=gat[:], in_=ptr[:, :].rearrange("p (s t) -> p s t", s=4)), reads=[ptr], writes=[gat])
                    first = (qi == 0 and ch == 0)
                    for s in range(4):
                        lastmm = (qi == NQ - 1 and ch == QE // 512 - 1 and s == 3)
                        for hf in range(2):
                            kb.op("pe", lambda e: e.matmul(O[hf][:, :], lhsT=gat[:, s, :], rhs=v[:, s, hf * 512:(hf + 1) * 512], start=(first and s == 0), stop=lastmm),
                                  reads=[gat, v], writes=[O[hf]])
            for hf in range(2):
                kb.op("dve", lambda e: e.tensor_tensor(out=xo[:, hf * 512:(hf + 1) * 512], in0=O[hf][:, :], in1=xo[:, hf * 512:(hf + 1) * 512], op=ALU.add),
                      reads=[O[hf], xo], writes=[xo])
            kb.dma(out_d[r0:r0 + 128, :], xo[:], reads=[xo], writes=[cx.OUT])
        kb.barrier()


def build_lb(nc, kb, cx, TB):
    setup_consts(kb, nc, cx)
    lb_inputs(nc, cx, TB)
    out_d = dout(nc, "xout", [TB, 1024])
    hT_d = kb.dram("hT_s", [8, 128, TB]); mT_d = kb.dram("mT_s", [8, 128, TB])
    xnT_d = kb.dram("xnT_s", [8, 128, TB]); xn_tok_d = kb.dram("xntok_s", [TB, 1024])
    h2T_d = kb.dram("h2T_s", [8, 128, TB]); s_d = kb.dram("s_s", [2, TB, 8, 128])
    phase_norm(kb, nc, cx, TB, cx.xT, cx.g1, hT_d)
    lb_gate_merge(kb, nc, cx, TB, hT_d, mT_d)
    lb_outproj(kb, nc, cx, TB, mT_d, xnT_d, xn_tok_d)
    phase_norm(kb, nc, cx, TB, xnT_d.t, cx.g2, h2T_d)
    lb_peer_scores(kb, nc, cx, TB, h2T_d, s_d)
    lb_peer_main2(kb, nc, cx, TB, h2T_d, s_d, xn_tok_d, out_d)
    kb.finish()


def lb_peer_main2(kb, nc, cx, TB, h2T_d, s_d, xn_tok_d, out_d, NQ=8, NT=2):
    QI = 128 // NQ
    QE = QI * 128
    NCH = QE // 512
    TP = NT * 128
    with contextlib.ExitStack() as st:
        identb = kb.sb([128, 128], BF16, name="b5_identb", stack=st)
        kb.op("dve", lambda e: e.tensor_copy(out=identb[:], in_=cx.ident[:]), reads=[cx.ident], writes=[identb])
        hf = kb.sb([128, 8, TP], name="b5_hf", stack=st)
        hb = kb.sb([128, 8, TP], BF16, name="b5_hb", stack=st)
        s1 = kb.sb([128, NT, 8, 128], name="b5_s1", stack=st)
        s2 = kb.sb([128, NT, 8, 128], name="b5_s2", stack=st)
        xk = [kb.sb([128, 1024], name="b5_xk%d" % i, stack=st) for i in range(NT)]
        wk = kb.sb([128, 256], name="b5_wk", stack=st)
        v12 = kb.sb([128, 2, 16], name="b5_v12", stack=st)
        cand = kb.sb([128, 16, 16], name="b5_cand", stack=st)
        c16 = kb.sb([128, 16], name="b5_c16", stack=st)
        e16 = kb.sb([128, 16], name="b5_e16", stack=st)
        tau = kb.sb([128, NT, 8], name="b5_tau", stack=st)
        negm = kb.sb([128, NT, 8], name="b5_negm", stack=st)
        rz = kb.sb([128, NT, 8], name="b5_rz", stack=st)
        E = kb.sb([128, QI, 128], name="b5_E", stack=st)
        M = kb.sb([128, QI, 128], name="b5_M", stack=st)
        G = [kb.sb([128, QE], name="b5_G%d" % i, stack=st) for i in range(NT)]
        Ac = [kb.sb([128, 512], name="b5_A%d" % i, stack=st) for i in range(2)]
        GAb = [kb.sb([128, 512], BF16, name="b5_GAb%d" % i, stack=st) for i in range(2)]
        GAT = [kb.sb([128, 4, 128], BF16, name="b5_GAT%d" % i, stack=st) for i in range(2)]
        uc = [kb.sb([128, 8, 512], name="b5_u%d" % i, stack=st) for i in range(2)]
        ub = [kb.sb([128, 8, 512], BF16, name="b5_ub%d" % i, stack=st) for i in range(2)]
        vc = [kb.sb([128, 4, 1024], name="b5_v%d" % i, stack=st) for i in range(2)]
        vb = [kb.sb([128, 4, 1024], BF16, name="b5_vb%d" % i, stack=st) for i in range(2)]
        pp = cx.psum
        nch = 0
        nt = 0
        for tp in range(TB // TP):
            r0 = tp * TP
            kb.dma(hf[:], h2T_d.t[:, :, r0:r0 + TP].rearrange("k p t -> p k t"), writes=[hf])
            kb.op("act", lambda e: e.copy(out=hb[:], in_=hf[:]), reads=[hf], writes=[hb])
            for t in range(NT):
                kb.dma(s1[:, t, :, :], s_d.t[0, r0 + t * 128:r0 + (t + 1) * 128, :, :], writes=[s1])
                kb.dma(s2[:, t, :, :], s_d.t[1, r0 + t * 128:r0 + (t + 1) * 128, :, :], writes=[s2])
                kb.dma(xk[t][:], xn_tok_d.t[r0 + t * 128:r0 + (t + 1) * 128, :], writes=[xk[t]])
            for t in range(NT):
                for hd in range(8):
                    for w, src in enumerate((s1, s2)):
                        kb.op("dve", lambda e: e.max(out=v12[:, w, 0:8], in_=src[:, t, hd, :]), reads=[src], writes=[v12])
                        kb.op("dve", lambda e: e.match_replace(out=wk[:, 0:128], in_to_replace=v12[:, w, 0:8], in_values=src[:, t, hd, :], imm_value=-1e30),
                              reads=[src, v12], writes=[wk])
                        kb.op("dve", lambda e: e.max(out=v12[:, w, 8:16], in_=wk[:, 0:128]), reads=[wk], writes=[v12])
                    kb.op("dve", lambda e: e.tensor_tensor(out=cand[:], in0=v12[:, 0, :].unsqueeze(2).broadcast_to([128, 16, 16]),
                                                           in1=v12[:, 1, :].unsqueeze(1).broadcast_to([128, 16, 16]), op=ALU.add), reads=[v12], writes=[cand])
                    cf = cand[:].rearrange("p a b -> p (a b)")
                    kb.op("dve", lambda e: e.max(out=c16[:, 0:8], in_=cf), reads=[cand], writes=[c16])
                    kb.op("dve", lambda e: e.match_replace(out=wk[:], in_to_replace=c16[:, 0:8], in_values=cf, imm_value=-1e30), reads=[cand, c16], writes=[wk])
                    kb.op("dve", lambda e: e.max(out=c16[:, 8:16], in_=wk[:]), reads=[wk], writes=[c16])
                    kb.op("dve", lambda e: e.tensor_copy(out=tau[:, t, hd:hd + 1], in_=c16[:, 15:16]), reads=[c16], writes=[tau])
                    kb.op("dve", lambda e: e.tensor_scalar(out=negm[:, t, hd:hd + 1], in0=c16[:, 0:1], scalar1=-1.0, scalar2=None, op0=ALU.mult), reads=[c16], writes=[negm])
                    kb.op("act", lambda e: e.activation(out=e16[:], in_=c16[:], func=AF.Exp, bias=negm[:, t, hd:hd + 1]), reads=[c16, negm], writes=[e16])
                    kb.op("dve", lambda e: e.tensor_reduce(out=rz[:, t, hd:hd + 1], in_=e16[:], axis=AX.X, op=ALU.add), reads=[e16], writes=[rz])
            kb.op("dve", lambda e: e.reciprocal(out=rz[:], in_=rz[:]), reads=[rz], writes=[rz])
            O = [[pp[4 + 2 * t + h2] for h2 in range(2)] for t in range(NT)]
            for qi in range(NQ):
                for t in range(NT):
                    kb.op("pool", lambda e: e.memset(G[t][:], 0.0), writes=[G[t]])
                    G3 = G[t][:].rearrange("p (i j) -> p i j", j=128)
                    for hd in range(8):
                        kb.op("dve", lambda e: e.tensor_tensor(out=E[:], in0=s1[:, t, hd, qi * QI:(qi + 1) * QI].unsqueeze(2).broadcast_to([128, QI, 128]),
                                                               in1=s2[:, t, hd, :].unsqueeze(1).broadcast_to([128, QI, 128]), op=ALU.add), reads=[s1, s2], writes=[E])
                        kb.op("dve", lambda e: e.tensor_scalar(out=M[:], in0=E[:], scalar1=tau[:, t, hd:hd + 1], scalar2=None, op0=ALU.is_ge), reads=[E, tau], writes=[M])
                        kb.op("act", lambda e: e.activation(out=E[:], in_=E[:], func=AF.Exp, bias=negm[:, t, hd:hd + 1]), reads=[E, negm], writes=[E])
                        kb.op("pool", lambda e: e.tensor_tensor(out=M[:], in0=M[:], in1=E[:], op=ALU.mult), reads=[M, E], writes=[M])
                        kb.op("dve", lambda e: e.scalar_tensor_tensor(out=G3, in0=M[:], scalar=rz[:, t, hd:hd + 1], in1=G3, op0=ALU.mult, op1=ALU.add),
                              reads=[M, rz, G[t]], writes=[G[t]])
                for ch in range(NCH):
                    e0 = qi * QE + ch * 512
                    u, v, ubb, vbb = uc[nch % 2], vc[nch % 2], ub[nch % 2], vb[nch % 2]
                    nch += 1
                    kb.dma(u[:], cx.uT[:, :, e0:e0 + 512].rearrange("k p e -> p k e"), writes=[u])
                    kb.dma(v[:], cx.vv[e0:e0 + 512, :].rearrange("(s p) d -> p s d", p=128), writes=[v])
                    kb.op("act", lambda e: e.copy(out=ubb[:], in_=u[:]), reads=[u], writes=[ubb])
                    kb.op("pool", lambda e: e.tensor_copy(out=vbb[:], in_=v[:]), reads=[v], writes=[vbb])
                    for t in range(NT):
                        A, gab, gat = Ac[nt % 2], GAb[nt % 2], GAT[nt % 2]
                        pa, ptr = pp[nt % 2], pp[2 + nt % 2]
                        nt += 1
                        ptrb = ptr[:, 0:256].bitcast(BF16)
                        for k in range(8):
                            kb.op("pe", lambda e: e.matmul(pa[:, :], lhsT=hb[:, k, t * 128:(t + 1) * 128], rhs=ubb[:, k, :], start=(k == 0), stop=(k == 7)),
                                  reads=[hb, ubb], writes=[pa])
                        kb.op("act", lambda e: e.activation(out=A[:], in_=pa[:, :], func=AF.Gelu), reads=[pa], writes=[A])
                        kb.op("dve", lambda e: e.tensor_tensor(out=gab[:], in0=A[:], in1=G[t][:, ch * 512:(ch + 1) * 512], op=ALU.mult), reads=[A, G[t]], writes=[gab])
                        for s in range(4):
                            kb.op("pe", lambda e: e.transpose(ptrb[:, s * 128:(s + 1) * 128], gab[:, s * 128:(s + 1) * 128], identb[:]), reads=[gab, identb], writes=[ptr])
                        kb.op("act", lambda e: e.copy(out=gat[:], in_=ptrb.rearrange("p (s t) -> p s t", s=4)), reads=[ptr], writes=[gat])
                        first = (qi == 0 and ch == 0)
                        for s in range(4):
                            lastmm = (qi == NQ - 1 and ch == NCH - 1 and s == 3)
                            for h2 in range(2):
                                kb.op("pe", lambda e: e.matmul(O[t][h2][:, :], lhsT=gat[:, s, :], rhs=vbb[:, s, h2 * 512:(h2 + 1) * 512], start=(first and s == 0), stop=lastmm),
                                      reads=[gat, vbb], writes=[O[t][h2]])
            for t in range(NT):
                for h2 in range(2):
                    kb.op("dve", lambda e: e.tensor_tensor(out=xk[t][:, h2 * 512:(h2 + 1) * 512], in0=O[t][h2][:, :], in1=xk[t][:, h2 * 512:(h2 + 1) * 512], op=ALU.add),
                          reads=[O[t][h2], xk[t]], writes=[xk[t]])
                kb.dma(out_d[r0 + t * 128:r0 + (t + 1) * 128, :], xk[t][:], reads=[xk[t]], writes=[cx.OUT])
        kb.barrier()

C = np.ascontiguousarray
IN_SSD = 1024 + 2048 + 16
B2 = IN_SSD + 3072

def colT(v):
    return C(np.asarray(v, np.float32).reshape(-1, 128).T)

def wtiles(w):
    return C(np.asarray(w, np.float32).reshape(8, 128, -1))

def prep_common(x_b, norm_g):
    T = x_b.shape[0]
    return {"xT": C(x_b.T.reshape(8, 128, T)), "g1": colT(norm_g), "ident": np.eye(128, dtype=np.float32)}

def prep_rwkv(inp, l, hg):
    w_in = inp["w_in"][l]
    cs = slice(256 * hg, 256 * hg + 256)
    r = w_in[:, B2:B2 + 1024][:, cs]; k = w_in[:, B2 + 1024:B2 + 2048][:, cs]; v = w_in[:, B2 + 2048:B2 + 3072][:, cs]
    la = w_in[:, B2 + 3072:B2 + 3200]; gl = w_in[:, B2 + 3200:B2 + 3328]
    W = np.concatenate([r, k, v, la, gl], axis=1)
    mu = inp["rwkv_mu"][l]
    mu_all = np.concatenate([mu[0:1024][cs], mu[1024:2048][cs], mu[2048:3072][cs], mu[3072:3200], mu[3200:3328]])
    pc = np.concatenate([colT(inp[n][l].reshape(-1)[cs]) for n in ("w0", "a0", "k_k", "k_a", "r_k", "lnx_g", "lnx_b")], axis=1)
    la2 = np.concatenate([inp["w_w2"][l][:, cs], inp["w_a2"][l][:, cs]], axis=0)
    return {"rw_w": wtiles(W), "rw_mu": colT(mu_all), "rw_pc": C(pc), "rw_la2": C(la2.astype(np.float32)), "rw_g2": C(inp["w_g2"][l][:, cs])}

def prep_ssd(inp, l, hg):
    w_in = inp["w_in"][l]
    z = w_in[:, 256 * hg:256 * hg + 256]
    x = w_in[:, 1024 + 256 * hg:1024 + 256 * hg + 256]
    Bc = w_in[:, 2048 + 128 * hg:2048 + 128 * hg + 128]
    Cc = w_in[:, 2560 + 128 * hg:2560 + 128 * hg + 128]
    dtc = w_in[:, 3072 + 4 * hg:3072 + 4 * hg + 4]
    dtrep = np.concatenate([np.repeat(dtc[:, 2 * j + ep:2 * j + ep + 1], 64, axis=1) for j in range(2) for ep in range(2)], axis=1)
    W = np.concatenate([z, x, Bc, Cc, dtrep], axis=1)
    cw = inp["conv_w"][l]; cb = inp["conv_b"][l]
    chans = np.concatenate([np.arange(256 * hg, 256 * hg + 256), 1024 + np.arange(128 * hg, 128 * hg + 128), 1536 + np.arange(128 * hg, 128 * hg + 128)])
    convw = np.zeros((128, 16), np.float32); convb = np.zeros((128, 4), np.float32)
    for c in range(4):
        ch = chans[c * 128:(c + 1) * 128]
        for k in range(4):
            convw[:, c * 4 + k] = cw[k, ch]
        convb[:, c] = cb[ch]
    def hrep(v):
        v = np.asarray(v)[4 * hg:4 * hg + 4]
        return np.stack([np.repeat(v[2 * j:2 * j + 2], 64) for j in range(2)], axis=1).astype(np.float32)
    pc = np.concatenate([convw, convb, hrep(inp["dt_bias"][l]), hrep(inp["a_log"][l]), hrep(inp["d_skip"][l]),
                         colT(inp["ssd_norm_g"][l][256 * hg:256 * hg + 256])], axis=1)
    return {"ss_w": wtiles(W), "ss_pc": C(pc)}

def bucket_strip():
    import jax, jax.numpy as jnp, math
    cpu = jax.devices("cpu")[0]
    with jax.default_device(cpu):
        d = jnp.arange(0, 1280, dtype=jnp.int32)
        max_exact = 16
        df = jnp.maximum(d, 1).astype(jnp.float32)
        large = max_exact + (jnp.log(df / max_exact) / math.log(1024 / max_exact) * (32 - max_exact)).astype(jnp.int32)
        large = jnp.minimum(large, 31)
        bk = np.asarray(jnp.where(d < max_exact, d, large))
    m = np.arange(1280)[None, :] - np.arange(128)[:, None]
    out = np.where(m >= 0, bk[np.clip(m, 0, 1279)], -1).astype(np.float32)
    return C(out)

def prep_moba(inp, l, hg):
    w_in = inp["w_in"][l]
    hs = [2 * hg, 2 * hg + 1]
    q = np.concatenate([w_in[:, IN_SSD + h * 128:IN_SSD + (h + 1) * 128] for h in hs], axis=1)
    k = np.concatenate([w_in[:, IN_SSD + 1024 + h * 128:IN_SSD + 1024 + (h + 1) * 128] for h in hs], axis=1)
    v = np.concatenate([w_in[:, IN_SSD + 2048 + h * 128:IN_SSD + 2048 + (h + 1) * 128] for h in hs], axis=1)
    g = np.stack([inp["q_norm_g"][l], inp["k_norm_g"][l]], axis=1).astype(np.float32)
    rel = np.concatenate([inp["rel_bias"][:, h] for h in hs])[None, :].astype(np.float32)
    return {"mb_wqk": wtiles(np.concatenate([q, k], axis=1)), "mb_wv": wtiles(v), "mb_g": C(g), "mb_rel": C(rel), "mb_bi": bucket_strip()}

def prep_lb_weights(inp, l):
    return {"g1": colT(inp["norm1_g"][l]), "g2": colT(inp["norm2_g"][l]),
            "wg": wtiles(inp["w_gate"][l]), "bg": colT(inp["b_gate"][l]),
            "pall": C(np.concatenate([inp["p_ssd"][l], inp["p_moba"][l], inp["p_rwkv"][l]], axis=0).reshape(24, 128, 1024).astype(np.float32)),
            "wout": wtiles(inp["w_out"][l]), "wq": wtiles(inp["peer_wq"][l]),
            "k12T": C(np.concatenate([inp["peer_k1"][l].T, inp["peer_k2"][l].T], axis=1).astype(np.float32)),
            "uT": C(inp["peer_u"][l].T.reshape(8, 128, 16384)), "vv": C(inp["peer_v"][l]), "ident": np.eye(128, dtype=np.float32)}

def prep_lb_acts(x_tok, yfull):
    TB = x_tok.shape[0]
    return {"xT": C(x_tok.T.reshape(8, 128, TB)), "xtok": C(x_tok), "yT": C(yfull.T.reshape(24, 128, TB))}

import time as _time
from concourse.bass_utils import run_bass_kernel_spmd

T_FULL = 16384
TB_FULL = 4096
_PROG = {}


def build_la(T):
    nc = bass.Bass("TRN2", target_bir_lowering=False)
    kb = KB(nc)
    cx = Ctx()
    setup_consts(kb, nc, cx)
    xT_d = din(nc, "xT", [8, 128, T]); g_d = din(nc, "g1", [128, 8])
    rwkv_inputs(nc, cx); ssd_inputs(nc, cx); moba_inputs(nc, cx)
    ya = dout(nc, "yaT", [2, 128, T]); yb = dout(nc, "yb", [2, T, 128]); yc = dout(nc, "ycT", [2, 128, T])
    hT = kb.dram("hT_s", [8, 128, T])
    strm = kb.dram("rw_strm", [5, 2, T, 128], BF16); strm_w = kb.dram("rw_strmw", [2, T, 128]); rfm = kb.dram("rw_fm", [4, 2, 128, T])
    sfm = kb.dram("ss_fm", [4, 2, 128, T]); btok = kb.dram("ss_b", [T, 128]); ctok = kb.dram("ss_c", [T, 128])
    qT = kb.dram("mb_q", [2, 128, T], BF16); kT = kb.dram("mb_k", [2, 128, T], BF16); vd = kb.dram("mb_v", [2, T, 129], BF16); nm = kb.dram("mb_nm", [2, 64, T], BF16)
    phase_norm(kb, nc, cx, T, xT_d, g_d, hT)
    rwkv_phase1(kb, nc, cx, T, hT, strm, rfm, strm_w=strm_w)
    ssd_phase1(kb, nc, cx, T, hT, sfm, btok, ctok)
    moba_phase1(kb, nc, cx, T, hT, qT, kT, vd, nm)
    moba_phase2(kb, nc, cx, T, qT, kT, vd, nm, yb)
    rwkv_phase2(kb, nc, cx, T, strm, rfm, yc, same=False, strm_w=strm_w)
    ssd_phase2(kb, nc, cx, T, sfm, btok, ctok, ya, same=False)
    kb.finish()
    return nc, kb


def build_lb_prog(TB):
    nc = bass.Bass("TRN2", target_bir_lowering=False)
    kb = KB(nc)
    cx = Ctx()
    build_lb(nc, kb, cx, TB)
    return nc, kb


def _prog(kind, n):
    key = (kind, n)
    if key not in _PROG:
        _PROG[key] = (build_la if kind == "A" else build_lb_prog)(n)[0]
    return _PROG[key]


def kernel(**inputs):
    inp = {k: np.asarray(v) for k, v in inputs.items()}
    x = np.ascontiguousarray(inp["x"], dtype=np.float32)
    Bn, T, D = x.shape
    TB = (Bn * T) // 8
    per_b = T // TB
    ncores = 8
    for l in range(2):
        ncA = _prog("A", T)
        in_maps = []
        for c in range(ncores):
            b, hg = c // 4, c % 4
            m = prep_common(x[b], inp["norm1_g"][l])
            m.update(prep_rwkv(inp, l, hg)); m.update(prep_ssd(inp, l, hg)); m.update(prep_moba(inp, l, hg))
            in_maps.append(m)
        resA = run_bass_kernel_spmd(ncA, in_maps, core_ids=list(range(ncores))).results
        del in_maps
        yfull = np.empty((Bn, T, 3072), np.float32)
        for c in range(ncores):
            b, hg = c // 4, c % 4
            r = resA[c]
            yfull[b, :, 256 * hg:256 * hg + 256] = np.asarray(r["yaT"]).reshape(256, T).T
            yfull[b, :, 1024 + 256 * hg:1024 + 256 * hg + 256] = np.asarray(r["yb"]).transpose(1, 0, 2).reshape(T, 256)
            yfull[b, :, 2048 + 256 * hg:2048 + 256 * hg + 256] = np.asarray(r["ycT"]).reshape(256, T).T
        del resA
        ncB = _prog("B", TB)
        wts = prep_lb_weights(inp, l)
        in_maps = []
        for c in range(ncores):
            b, s0 = c // per_b, (c % per_b) * TB
            m = dict(wts)
            m.update(prep_lb_acts(x[b, s0:s0 + TB], yfull[b, s0:s0 + TB]))
            in_maps.append(m)
        resB = run_bass_kernel_spmd(ncB, in_maps, core_ids=list(range(ncores))).results
        del in_maps
        xn = np.empty_like(x)
        for c in range(ncores):
            b, s0 = c // per_b, (c % per_b) * TB
            xn[b, s0:s0 + TB] = np.asarray(resB[c]["xout"])
        x = xn
    return x
```

```python
import contextlib
import numpy as np
import concourse.bass as bass
import concourse.mybir as mybir

F32 = mybir.dt.float32
BF16 = mybir.dt.bfloat16
ALU = mybir.AluOpType
AF = mybir.ActivationFunctionType
AX = mybir.AxisListType

EP = 30000
KD = 24


class Buf:
    __slots__ = ("t", "w", "r", "name")

    def __init__(self, t, name=""):
        self.t = t
        self.w = {}
        self.r = {}
        self.name = name

    def __getitem__(self, idx):
        return self.t[idx]


class KB:
    def __init__(self, nc, same_engine_sync=True):
        self.nc = nc
        self.es = contextlib.ExitStack()
        self.eng = {"pe": nc.tensor, "dve": nc.vector, "act": nc.scalar, "pool": nc.gpsimd, "sp": nc.sync}
        self.cnt = {e: 0 for e in ("pe", "dve", "act", "pool")}
        self.sems = {}
        self.waited = {e: {} for e in self.eng}
        self.dma_i = 0
        self.dma_last = {}
        self.same = same_engine_sync
        self.nbuf = 0
        self.n_wait = 0

    def sem(self, key):
        s = self.sems.get(key)
        if s is None:
            s = self.es.enter_context(self.nc.semaphore("s_%s_%s" % key))
            self.sems[key] = s
        return s

    def sb(self, shape, dt=F32, name=None, stack=None):
        self.nbuf += 1
        name = ("s%d_" % self.nbuf + name) if name else "sb%d" % self.nbuf
        t = (stack or self.es).enter_context(self.nc.sbuf_tensor(name, list(shape), dt))
        return Buf(t, name)

    def ps(self, shape, dt=F32, name=None, stack=None):
        self.nbuf += 1
        name = ("p%d_" % self.nbuf + name) if name else "ps%d" % self.nbuf
        t = (stack or self.es).enter_context(self.nc.psum_tensor(name, list(shape), dt))
        return Buf(t, name)

    def dram(self, name, shape, dt=F32, kind="Internal"):
        t = self.nc.dram_tensor(name, list(shape), dt, kind=kind)
        return Buf(t.ap(), name)

    def _deps(self, reads, writes):
        deps = {}
        for b in reads:
            for k, v in b.w.items():
                if deps.get(k, 0) < v:
                    deps[k] = v
        for b in writes:
            for k, v in b.w.items():
                if deps.get(k, 0) < v:
                    deps[k] = v
            for k, v in b.r.items():
                if deps.get(k, 0) < v:
                    deps[k] = v
        return deps

    def _wait(self, E, deps):
        wd = self.waited[E]
        for k, v in deps.items():
            if k[0] == E:
                if E == "pe" or not self.same:
                    continue
            if wd.get(k, 0) >= v:
                continue
            self.eng[E].wait_ge(self.sem(k), v)
            self.n_wait += 1
            wd[k] = v

    def _mark(self, dep, reads, writes):
        k, v = dep
        for b in reads:
            b.r[k] = v
        for b in writes:
            b.w = {k: v}
            b.r = {}

    def op(self, E, fn, reads=(), writes=()):
        self._wait(E, self._deps(reads, writes))
        ins = fn(self.eng[E])
        n = self.cnt[E] = self.cnt[E] + 1
        key = (E, (n - 1) // EP)
        val = (n - 1) % EP + 1
        ins.then_inc(self.sem(key), 1)
        self._mark((key, val), reads, writes)
        return ins

    def dma(self, out, in_, reads=(), writes=(), Q="sp", **kw):
        self._wait(Q, self._deps(reads, writes))
        i = self.dma_i
        self.dma_i += 1
        s = i % KD
        key = ("dma", s)
        val = 16 * (i // KD + 1)
        if i >= KD:
            wd = self.waited[Q]
            if wd.get(key, 0) < val - 16:
                self.eng[Q].wait_ge(self.sem(key), val - 16)
                wd[key] = val - 16
        self.eng[Q].dma_start(out=out, in_=in_, **kw).then_inc(self.sem(key), 16)
        self.dma_last[s] = val
        self._mark((key, val), reads, writes)

    def barrier(self):
        state = {}
        for e, n in self.cnt.items():
            if n > 0:
                state[(e, (n - 1) // EP)] = (n - 1) % EP + 1
        for s, v in self.dma_last.items():
            state[("dma", s)] = v
        for E in self.eng:
            wd = self.waited[E]
            for k, v in state.items():
                if k[0] == E and E == "pe":
                    continue
                if wd.get(k, 0) >= v:
                    continue
                self.eng[E].wait_ge(self.sem(k), v)
                wd[k] = v

    def finish(self):
        self.barrier()

import math


def din(nc, name, shape, dt=F32):
    return nc.dram_tensor(name, list(shape), dt, kind="ExternalInput").ap()


def dout(nc, name, shape, dt=F32):
    return nc.dram_tensor(name, list(shape), dt, kind="ExternalOutput").ap()


class Ctx:
    pass


def setup_consts(kb, nc, cx):
    cx.ident_d = din(nc, "ident", [128, 128])
    cx.ident = kb.sb([128, 128], name="ident")
    kb.dma(cx.ident[:], cx.ident_d, writes=[cx.ident])
    cx.ones = kb.sb([128, 128], name="ones")
    kb.op("dve", lambda e: e.memset(cx.ones[:], 1.0), writes=[cx.ones])
    cx.bones = kb.sb([128, 128], name="bones")
    kb.op("dve", lambda e: e.memset(cx.bones[:], 0.0), writes=[cx.bones])
    kb.op("dve", lambda e: e.memset(cx.bones[0:64, 0:64], 1.0), writes=[cx.bones])
    kb.op("dve", lambda e: e.memset(cx.bones[64:128, 64:128], 1.0), writes=[cx.bones])
    cx.eps6 = kb.sb([128, 1], name="eps6")
    kb.op("dve", lambda e: e.memset(cx.eps6[:], 1e-6), writes=[cx.eps6])
    cx.psum = [kb.ps([128, 512], name="bank%d" % i) for i in range(8)]
    cx.OUT = Buf(None, "OUT")


def phase_norm(kb, nc, cx, T, xT_d, g_d, hT_dram):
    import contextlib
    with contextlib.ExitStack() as st:
        g = kb.sb([128, 8], name="n_g", stack=st)
        kb.dma(g[:], g_d, writes=[g])
        xs = [kb.sb([128, 8, 512], name="n_x%d" % i, stack=st) for i in range(2)]
        sq = kb.sb([128, 8, 512], name="n_sq", stack=st)
        rstd = kb.sb([128, 512], name="n_rstd", stack=st)
        hs = [kb.sb([128, 8, 512], name="n_h%d" % i, stack=st) for i in range(2)]
        p0 = cx.psum[0]
        for i in range(T // 512):
            x = xs[i % 2]
            h = hs[i % 2]
            kb.dma(x[:], xT_d[:, :, i * 512:(i + 1) * 512].rearrange("k p t -> p k t"), writes=[x])
            kb.op("act", lambda e: e.activation(out=sq[:], in_=x[:], func=AF.Square), reads=[x], writes=[sq])
            for k in range(8):
                kb.op("pe", lambda e: e.matmul(p0[:, :], lhsT=cx.ones[:], rhs=sq[:, k, :], start=(k == 0), stop=(k == 7)),
                      reads=[cx.ones, sq], writes=[p0])
            kb.op("act", lambda e: e.activation(out=rstd[:], in_=p0[:, :], func=AF.Sqrt, scale=1.0 / 1024, bias=cx.eps6[:]),
                  reads=[p0, cx.eps6], writes=[rstd])
            kb.op("dve", lambda e: e.reciprocal(out=rstd[:], in_=rstd[:]), reads=[rstd], writes=[rstd])
            for k in range(8):
                kb.op("dve", lambda e: e.scalar_tensor_tensor(out=h[:, k, :], in0=x[:, k, :], scalar=g[:, k:k + 1], in1=rstd[:],
                                                              op0=ALU.mult, op1=ALU.mult), reads=[x, g, rstd], writes=[h])
            kb.dma(hT_dram.t[:, :, i * 512:(i + 1) * 512].rearrange("k p t -> p k t"), h[:], reads=[h], writes=[hT_dram])
        kb.barrier()


EM05 = math.exp(-0.5)


def rwkv_inputs(nc, cx):
    cx.rw_w = din(nc, "rw_w", [8, 128, 1024])
    cx.rw_mu = din(nc, "rw_mu", [128, 8])
    cx.rw_pc = din(nc, "rw_pc", [128, 14])
    cx.rw_la2 = din(nc, "rw_la2", [128, 256])
    cx.rw_g2 = din(nc, "rw_g2", [128, 256])


def rwkv_phase1(kb, nc, cx, T, hT_dram, strm, fm, strm_w=None):
    with contextlib.ExitStack() as st:
        W = kb.sb([128, 8, 1024], name="rw_W", stack=st)
        kb.dma(W[:], cx.rw_w.rearrange("k p c -> p k c"), writes=[W])
        mu = kb.sb([128, 8], name="rw_mu", stack=st)
        kb.dma(mu[:], cx.rw_mu, writes=[mu])
        pc = kb.sb([128, 14], name="rw_pc", stack=st)
        kb.dma(pc[:], cx.rw_pc, writes=[pc])
        la2 = kb.sb([128, 256], name="rw_la2", stack=st)
        kb.dma(la2[:], cx.rw_la2, writes=[la2])
        g2 = kb.sb([128, 256], name="rw_g2", stack=st)
        kb.dma(g2[:], cx.rw_g2, writes=[g2])
        eps12 = kb.sb([128, 1], name="rw_eps12", stack=st)
        hTs = [kb.sb([128, 8, 512], name="rw_h%d" % i, stack=st) for i in range(2)]
        raw = [kb.sb([128, 513], name="rw_raw%d" % c, stack=st) for c in range(8)]
        xs = [kb.sb([128, 512], name="rw_xs%d" % c, stack=st) for c in range(8)]
        for c in range(8):
            kb.op("dve", lambda e: e.memset(raw[c][:, 0:1], 0.0), writes=[raw[c]])
        tmp = [kb.sb([128, 512], name="rw_tmp%d" % i, stack=st) for i in range(8)]
        sbs = [kb.sb([128, 512], name="rw_s%d" % i, stack=st) for i in range(5)]
        gj = kb.sb([128, 512], name="rw_gj", stack=st)
        tok = [kb.sb([128, 512], BF16, name="rw_tok%d" % i, stack=st) for i in range(2)]
        tokf = [kb.sb([128, 512], name="rw_tokf%d" % i, stack=st) for i in range(2)]
        pp = cx.psum
        ntok = 0
        for i in range(T // 512):
            hT = hTs[i % 2]
            kb.dma(hT[:], hT_dram.t[:, :, i * 512:(i + 1) * 512].rearrange("k p t -> p k t"), writes=[hT])
            for c in range(8):
                ps = pp[c % 2]
                for k in range(8):
                    kb.op("pe", lambda e: e.matmul(ps[:, :], lhsT=W[:, k, c * 128:(c + 1) * 128], rhs=hT[:, k, :], start=(k == 0), stop=(k == 7)),
                          reads=[W, hT], writes=[ps])
                kb.op("act", lambda e: e.copy(out=raw[c][:, 1:513], in_=ps[:, :]), reads=[ps], writes=[raw[c]])
                kb.op("dve", lambda e: e.tensor_tensor(out=xs[c][:], in0=raw[c][:, 0:512], in1=raw[c][:, 1:513], op=ALU.subtract),
                      reads=[raw[c]], writes=[xs[c]])
                kb.op("dve", lambda e: e.scalar_tensor_tensor(out=xs[c][:], in0=xs[c][:], scalar=mu[:, c:c + 1], in1=raw[c][:, 1:513],
                                                              op0=ALU.mult, op1=ALU.add), reads=[xs[c], mu, raw[c]], writes=[xs[c]])
                kb.op("dve", lambda e: e.tensor_copy(out=raw[c][:, 0:1], in_=raw[c][:, 512:513]), reads=[raw[c]], writes=[raw[c]])
            xla, xgl = xs[6], xs[7]
            tanh_wl, sig_gl = tmp[0], tmp[1]
            kb.op("act", lambda e: e.activation(out=tanh_wl[0:64, :], in_=xla[0:64, :], func=AF.Tanh), reads=[xla], writes=[tanh_wl])
            kb.op("act", lambda e: e.activation(out=sig_gl[:], in_=xgl[:], func=AF.Sigmoid), reads=[xgl], writes=[sig_gl])
            for j in range(2):
                R, K, V = xs[0 + j], xs[2 + j], xs[4 + j]
                A_s, W_s, B_s, K_s = sbs[0], sbs[1], sbs[2], sbs[3]
                asig, kx, t1 = tmp[2], tmp[3], tmp[4]
                js = slice(j * 128, (j + 1) * 128)
                kb.op("pe", lambda e: e.matmul(pp[2][:, :], lhsT=la2[0:64, js], rhs=tanh_wl[0:64, :], start=True, stop=True),
                      reads=[la2, tanh_wl], writes=[pp[2]])
                kb.op("act", lambda e: e.activation(out=W_s[:], in_=pp[2][:, :], func=AF.Sigmoid, bias=pc[:, 0 + j:1 + j]),
                      reads=[pp[2], pc], writes=[W_s])
                kb.op("act", lambda e: e.activation(out=W_s[:], in_=W_s[:], func=AF.Exp, scale=-EM05), reads=[W_s], writes=[W_s])
                kb.op("pe", lambda e: e.matmul(pp[3][:, :], lhsT=la2[64:128, js], rhs=xla[64:128, :], start=True, stop=True),
                      reads=[la2, xla], writes=[pp[3]])
                kb.op("act", lambda e: e.activation(out=asig[:], in_=pp[3][:, :], func=AF.Sigmoid, bias=pc[:, 2 + j:3 + j]),
                      reads=[pp[3], pc], writes=[asig])
                kb.op("pe", lambda e: e.matmul(pp[4][:, :], lhsT=g2[:, js], rhs=sig_gl[:], start=True, stop=True),
                      reads=[g2, sig_gl], writes=[pp[4]])
                kb.op("act", lambda e: e.copy(out=gj[:], in_=pp[4][:, :]), reads=[pp[4]], writes=[gj])
                kb.op("dve", lambda e: e.tensor_scalar(out=kx[:], in0=K[:], scalar1=pc[:, 4 + j:5 + j], scalar2=None, op0=ALU.mult),
                      reads=[K, pc], writes=[kx])
                kb.op("act", lambda e: e.activation(out=t1[:], in_=kx[:], func=AF.Square), reads=[kx], writes=[t1])
                kb.op("pe", lambda e: e.matmul(pp[5][:, :], lhsT=cx.bones[:], rhs=t1[:], start=True, stop=True),
                      reads=[cx.bones, t1], writes=[pp[5]])
                kb.op("dve", lambda e: e.tensor_scalar(out=t1[:], in0=pp[5][:, :], scalar1=1e-12, scalar2=None, op0=ALU.max),
                      reads=[pp[5]], writes=[t1])
                kb.op("act", lambda e: e.activation(out=t1[:], in_=t1[:], func=AF.Sqrt), reads=[t1], writes=[t1])
                kb.op("dve", lambda e: e.reciprocal(out=t1[:], in_=t1[:]), reads=[t1], writes=[t1])
                kb.op("dve", lambda e: e.tensor_tensor(out=kx[:], in0=kx[:], in1=t1[:], op=ALU.mult), reads=[kx, t1], writes=[kx])
                kb.op("dve", lambda e: e.tensor_scalar(out=A_s[:], in0=kx[:], scalar1=-1.0, scalar2=None, op0=ALU.mult),
                      reads=[kx], writes=[A_s])
                kb.op("dve", lambda e: e.tensor_tensor(out=B_s[:], in0=kx[:], in1=asig[:], op=ALU.mult), reads=[kx, asig], writes=[B_s])
                kb.op("dve", lambda e: e.tensor_scalar(out=t1[:], in0=asig[:], scalar1=-1.0, scalar2=pc[:, 6 + j:7 + j], op0=ALU.add, op1=ALU.mult),
                      reads=[asig, pc], writes=[t1])
                kb.op("dve", lambda e: e.scalar_tensor_tensor(out=K_s[:], in0=t1[:], scalar=1.0, in1=K[:], op0=ALU.add, op1=ALU.mult),
                      reads=[t1, K], writes=[K_s])
                for q, src in enumerate((R, K_s, V, gj)):
                    kb.dma(fm.t[q, j, :, i * 512:(i + 1) * 512], src[:], reads=[src])
                for q, src in enumerate((A_s, W_s, B_s, K_s, R)):
                    ps = pp[6 + (ntok % 2)]
                    tk = (tokf if q == 1 else tok)[ntok % 2]
                    ntok += 1
                    for s in range(4):
                        kb.op("pe", lambda e: e.transpose(ps[:, s * 128:(s + 1) * 128], src[:, s * 128:(s + 1) * 128], cx.ident[:]),
                              reads=[src, cx.ident], writes=[ps])
                    kb.op("act", lambda e: e.copy(out=tk[:], in_=ps[:, :]), reads=[ps], writes=[tk])
                    for hp in range(2):
                        dd = strm_w.t[hp] if q == 1 else strm.t[q, hp]
                        dst = dd[i * 512:(i + 1) * 512, j * 64:(j + 1) * 64].rearrange("(s p) k -> p s k", p=128)
                        srcap = tk[:].rearrange("p (s h k) -> p s h k", s=4, h=2)[:, :, hp, :]
                        kb.dma(dst, srcap, reads=[tk])
        kb.barrier()


def rwkv_phase2(kb, nc, cx, T, strm, fm, yc_out, CH=16, same=True, strm_w=None):
    with contextlib.ExitStack() as st:
        pc = kb.sb([128, 14], name="r2_pc", stack=st)
        kb.dma(pc[:], cx.rw_pc, writes=[pc])
        eps = kb.sb([128, 1], name="r2_eps", stack=st)
        kb.op("dve", lambda e: e.memset(eps[:], 64e-5), writes=[eps])
        S = kb.sb([128, 128], name="r2_S", stack=st)
        kb.op("dve", lambda e: e.memset(S[:], 0.0), writes=[S])
        T1 = kb.sb([128, 128], name="r2_T1", stack=st)
        T2 = kb.sb([128, 128], name="r2_T2", stack=st)
        T4 = kb.sb([128, 128], name="r2_T4", stack=st)
        T3 = [kb.sb([128, 128], name="r2_T3%d" % i, stack=st) for i in range(4)]
        sa = kb.sb([128, 2], name="r2_sa", stack=st)
        bcs = [[kb.sb([128, CH, 128], (F32 if q == 1 else BF16), name="r2_bc%d_%d" % (q, b), stack=st) for b in range(2)] for q in range(5)]
        fms = [[kb.sb([128, 2, 512], name="r2_fm%d_%d" % (q, b), stack=st) for b in range(2)] for q in range(4)]
        ys = [kb.sb([128, 2, 512], name="r2_y%d" % b, stack=st) for b in range(2)]
        wk = [kb.sb([128, 512], name="r2_wk%d" % b, stack=st) for b in range(3)]
        pp = cx.psum
        v3 = lambda ap: ap.rearrange("p (j k) -> p j k", j=2)
        old_same = kb.same
        for i in range(T // 512):
            fb = [fms[q][i % 2] for q in range(4)]
            for q in range(4):
                kb.dma(fb[q][:], fm.t[q, :, :, i * 512:(i + 1) * 512].rearrange("j p t -> p j t"), writes=[fb[q]])
            Rf, Kf, Vf, Gf = fb
            Y = ys[i % 2]
            kb.same = same
            for cch in range(512 // CH):
                t0 = i * 512 + cch * CH
                cb = [bcs[q][cch % 2] for q in range(5)]
                for q in range(5):
                    for hp in range(2):
                        kb.dma(cb[q][hp * 64:(hp + 1) * 64, :, :],

                               (strm_w.t[hp] if q == 1 else strm.t[q, hp])[t0:t0 + CH, :].unsqueeze(0).broadcast_to([64, CH, 128]),
                               writes=[cb[q]])
                Ab, Wb, Bb, Kb, Rb = cb
                for s in range(CH):
                    tt = cch * CH + s
                    t3 = T3[tt % 4]
                    for j2 in range(2):
                        kb.op("act", lambda e: e.activation(out=t3[:, j2 * 64:(j2 + 1) * 64], in_=Kb[:, s, j2 * 64:(j2 + 1) * 64], func=AF.Copy,
                                                            scale=Vf[:, j2, tt:tt + 1]), reads=[Kb, Vf], writes=[t3])
                    kb.op("dve", lambda e: e.tensor_tensor(out=T1[:], in0=S[:], in1=Ab[:, s, :], op=ALU.mult), reads=[S, Ab], writes=[T1])
                    kb.op("dve", lambda e: e.tensor_reduce(out=sa[:], in_=v3(T1[:]), axis=AX.X, op=ALU.add), reads=[T1], writes=[sa])
                    kb.op("dve", lambda e: e.tensor_tensor(out=S[:], in0=S[:], in1=Wb[:, s, :], op=ALU.mult), reads=[S, Wb], writes=[S])
                    kb.op("dve", lambda e: e.tensor_tensor(out=v3(T2[:]), in0=v3(Bb[:, s, :]), in1=sa[:].unsqueeze(2).broadcast_to([128, 2, 64]),
                                                           op=ALU.mult), reads=[Bb, sa], writes=[T2])
                    kb.op("dve", lambda e: e.tensor_tensor(out=S[:], in0=S[:], in1=T2[:], op=ALU.add), reads=[S, T2], writes=[S])
                    kb.op("dve", lambda e: e.tensor_tensor(out=S[:], in0=S[:], in1=t3[:], op=ALU.add), reads=[S, t3], writes=[S])
                    kb.op("dve", lambda e: e.tensor_tensor(out=T4[:], in0=S[:], in1=Rb[:, s, :], op=ALU.mult), reads=[S, Rb], writes=[T4])
                    kb.op("dve", lambda e: e.tensor_reduce(out=Y[:, :, tt], in_=v3(T4[:]), axis=AX.X, op=ALU.add), reads=[T4], writes=[Y])
            kb.same = old_same
            for j in range(2):
                yj = Y[:, j, :]
                a, b, c = wk
                kb.op("pe", lambda e: e.matmul(pp[0][:, :], lhsT=cx.bones[:], rhs=yj, start=True, stop=True), reads=[cx.bones, Y], writes=[pp[0]])
                kb.op("dve", lambda e: e.scalar_tensor_tensor(out=a[:], in0=pp[0][:, :], scalar=-1.0 / 64, in1=yj, op0=ALU.mult, op1=ALU.add),
                      reads=[pp[0], Y], writes=[a])
                kb.op("act", lambda e: e.activation(out=b[:], in_=a[:], func=AF.Square), reads=[a], writes=[b])
                kb.op("pe", lambda e: e.matmul(pp[1][:, :], lhsT=cx.bones[:], rhs=b[:], start=True, stop=True), reads=[cx.bones, b], writes=[pp[1]])
                kb.op("act", lambda e: e.activation(out=b[:], in_=pp[1][:, :], func=AF.Sqrt, scale=1.0 / 64, bias=eps[:]),
                      reads=[pp[1], eps], writes=[b])
                kb.op("dve", lambda e: e.reciprocal(out=b[:], in_=b[:]), reads=[b], writes=[b])
                kb.op("dve", lambda e: e.tensor_tensor(out=a[:], in0=a[:], in1=b[:], op=ALU.mult), reads=[a, b], writes=[a])
                kb.op("dve", lambda e: e.tensor_scalar(out=a[:], in0=a[:], scalar1=pc[:, 10 + j:11 + j], scalar2=pc[:, 12 + j:13 + j],
                                                       op0=ALU.mult, op1=ALU.add), reads=[a, pc], writes=[a])
                kb.op("dve", lambda e: e.scalar_tensor_tensor(out=b[:], in0=Rf[:, j, :], scalar=pc[:, 8 + j:9 + j], in1=Kf[:, j, :],
                                                              op0=ALU.mult, op1=ALU.mult), reads=[Rf, Kf, pc], writes=[b])
                kb.op("pe", lambda e: e.matmul(pp[2][:, :], lhsT=cx.bones[:], rhs=b[:], start=True, stop=True), reads=[cx.bones, b], writes=[pp[2]])
                kb.op("dve", lambda e: e.tensor_tensor(out=b[:], in0=pp[2][:, :], in1=Vf[:, j, :], op=ALU.mult), reads=[pp[2], Vf], writes=[b])
                kb.op("dve", lambda e: e.tensor_tensor(out=a[:], in0=a[:], in1=b[:], op=ALU.add), reads=[a, b], writes=[a])
                kb.op("dve", lambda e: e.tensor_tensor(out=c[:], in0=a[:], in1=Gf[:, j, :], op=ALU.mult), reads=[a, Gf], writes=[c])
                kb.dma(yc_out[j, :, i * 512:(i + 1) * 512], c[:], reads=[c], writes=[cx.OUT])
        kb.barrier()


def ssd_inputs(nc, cx):
    cx.ss_w = din(nc, "ss_w", [8, 128, 1024])
    cx.ss_pc = din(nc, "ss_pc", [128, 28])


def ssd_phase1(kb, nc, cx, T, hT_dram, fm, btok, ctok):
    with contextlib.ExitStack() as st:
        W = kb.sb([128, 8, 1024], name="ss_W", stack=st)
        kb.dma(W[:], cx.ss_w.rearrange("k p c -> p k c"), writes=[W])
        pc = kb.sb([128, 28], name="ss_pc", stack=st)
        kb.dma(pc[:], cx.ss_pc, writes=[pc])
        nega = kb.sb([128, 2], name="ss_nega", stack=st)
        kb.op("act", lambda e: e.activation(out=nega[:], in_=pc[:, 22:24], func=AF.Exp), reads=[pc], writes=[nega])
        kb.op("dve", lambda e: e.tensor_scalar(out=nega[:], in0=nega[:], scalar1=-1.0, scalar2=None, op0=ALU.mult), reads=[nega], writes=[nega])
        hTs = [kb.sb([128, 8, 512], name="ss_h%d" % i, stack=st) for i in range(2)]
        raw = [kb.sb([128, 515], name="ss_raw%d" % c, stack=st) for c in range(4)]
        for c in range(4):
            kb.op("dve", lambda e: e.memset(raw[c][:, 0:3], 0.0), writes=[raw[c]])
        cv = [kb.sb([128, 512], name="ss_cv%d" % c, stack=st) for c in range(4)]
        sz = kb.sb([128, 512], name="ss_sz", stack=st)
        dt = kb.sb([128, 512], name="ss_dt", stack=st)
        dec = kb.sb([128, 512], name="ss_dec", stack=st)
        xdt = kb.sb([128, 512], name="ss_xdt", stack=st)
        tok = [kb.sb([128, 512], name="ss_tok%d" % i, stack=st) for i in range(2)]
        pp = cx.psum
        for i in range(T // 512):
            hT = hTs[i % 2]
            ts = slice(i * 512, (i + 1) * 512)
            kb.dma(hT[:], hT_dram.t[:, :, ts].rearrange("k p t -> p k t"), writes=[hT])

            def proj(c, ps):
                for k in range(8):
                    kb.op("pe", lambda e: e.matmul(ps[:, :], lhsT=W[:, k, c * 128:(c + 1) * 128], rhs=hT[:, k, :], start=(k == 0), stop=(k == 7)),
                          reads=[W, hT], writes=[ps])
            for j in range(2):
                ps = pp[j]
                proj(j, ps)
                kb.op("act", lambda e: e.activation(out=sz[:], in_=ps[:, :], func=AF.Silu), reads=[ps], writes=[sz])
                kb.dma(fm.t[0, j, :, ts], sz[:], reads=[sz])
            for c in range(4):
                ps = pp[2 + c % 2]
                proj(2 + c, ps)
                r = raw[c]
                kb.op("act", lambda e: e.copy(out=r[:, 3:515], in_=ps[:, :]), reads=[ps], writes=[r])
                kb.op("dve", lambda e: e.tensor_scalar(out=cv[c][:], in0=r[:, 0:512], scalar1=pc[:, c * 4:c * 4 + 1], scalar2=None, op0=ALU.mult),
                      reads=[r, pc], writes=[cv[c]])
                for k in range(1, 4):
                    kb.op("dve", lambda e: e.scalar_tensor_tensor(out=cv[c][:], in0=r[:, k:k + 512], scalar=pc[:, c * 4 + k:c * 4 + k + 1], in1=cv[c][:],
                                                                  op0=ALU.mult, op1=ALU.add), reads=[r, pc, cv[c]], writes=[cv[c]])
                kb.op("act", lambda e: e.activation(out=cv[c][:], in_=cv[c][:], func=AF.Silu, bias=pc[:, 16 + c:17 + c]), reads=[cv[c], pc], writes=[cv[c]])
                kb.op("dve", lambda e: e.tensor_copy(out=r[:, 0:3], in_=r[:, 512:515]), reads=[r], writes=[r])
            for j in range(2):
                ps = pp[4 + j]
                proj(6 + j, ps)
                kb.op("act", lambda e: e.activation(out=dt[:], in_=ps[:, :], func=AF.Exp, bias=pc[:, 20 + j:21 + j]), reads=[ps, pc], writes=[dt])
                kb.op("act", lambda e: e.activation(out=dt[:], in_=dt[:], func=AF.Ln, bias=1.0), reads=[dt], writes=[dt])
                kb.op("act", lambda e: e.activation(out=dec[:], in_=dt[:], func=AF.Exp, scale=nega[:, j:j + 1]), reads=[dt, nega], writes=[dec])
                kb.op("dve", lambda e: e.tensor_tensor(out=xdt[:], in0=cv[j][:], in1=dt[:], op=ALU.mult), reads=[cv[j], dt], writes=[xdt])
                kb.dma(fm.t[1, j, :, ts], cv[j][:], reads=[cv[j]])
                kb.dma(fm.t[2, j, :, ts], xdt[:], reads=[xdt])
                kb.dma(fm.t[3, j, :, ts], dec[:], reads=[dec])
            for q, (src, dst) in enumerate(((cv[2], btok), (cv[3], ctok))):
                ps = pp[6 + q]
                tk = tok[q]
                for s in range(4):
                    kb.op("pe", lambda e: e.transpose(ps[:, s * 128:(s + 1) * 128], src[:, s * 128:(s + 1) * 128], cx.ident[:]),
                          reads=[src, cx.ident], writes=[ps])
                kb.op("act", lambda e: e.copy(out=tk[:], in_=ps[:, :]), reads=[ps], writes=[tk])
                kb.dma(dst.t[ts, :].rearrange("(s p) n -> p s n", p=128), tk[:].rearrange("p (s n) -> p s n", s=4), reads=[tk])
        kb.barrier()


def ssd_phase2(kb, nc, cx, T, fm, btok, ctok, ya_out, CH=16, same=True):
    with contextlib.ExitStack() as st:
        pc = kb.sb([128, 28], name="s2_pc", stack=st)
        kb.dma(pc[:], cx.ss_pc, writes=[pc])
        S = kb.sb([128, 2, 128], name="s2_S", stack=st)
        kb.op("dve", lambda e: e.memset(S[:], 0.0), writes=[S])
        T4 = kb.sb([128, 2, 128], name="s2_T4", stack=st)
        T3 = [kb.sb([128, 2, 128], name="s2_T3%d" % i, stack=st) for i in range(4)]
        bcs = [[kb.sb([128, CH, 128], name="s2_bc%d_%d" % (q, b), stack=st) for b in range(2)] for q in range(2)]
        fms = [[kb.sb([128, 2, 512], name="s2_fm%d_%d" % (q, b), stack=st) for b in range(2)] for q in range(4)]
        ys = [kb.sb([128, 2, 512], name="s2_y%d" % b, stack=st) for b in range(2)]
        wk = [kb.sb([128, 512], name="s2_wk%d" % b, stack=st) for b in range(3)]
        rstd = kb.sb([128, 512], name="s2_rstd", stack=st)
        pp = cx.psum
        old_same = kb.same
        for i in range(T // 512):
            ts = slice(i * 512, (i + 1) * 512)
            fb = [fms[q][i % 2] for q in range(4)]
            for q in range(4):
                kb.dma(fb[q][:], fm.t[q, :, :, ts].rearrange("j p t -> p j t"), writes=[fb[q]])
            SZ, XS, XDT, DEC = fb
            Y = ys[i % 2]
            kb.same = same
            for cch in range(512 // CH):
                t0 = i * 512 + cch * CH
                Bb, Cb = bcs[0][cch % 2], bcs[1][cch % 2]
                kb.dma(Bb[:], btok.t[t0:t0 + CH, :].unsqueeze(0).broadcast_to([128, CH, 128]), writes=[Bb])
                kb.dma(Cb[:], ctok.t[t0:t0 + CH, :].unsqueeze(0).broadcast_to([128, CH, 128]), writes=[Cb])
                for s in range(CH):
                    tt = cch * CH + s
                    t3 = T3[tt % 4]
                    for j2 in range(2):
                        kb.op("act", lambda e: e.activation(out=t3[:, j2, :], in_=Bb[:, s, :], func=AF.Copy, scale=XDT[:, j2, tt:tt + 1]),
                              reads=[Bb, XDT], writes=[t3])
                    kb.op("dve", lambda e: e.tensor_tensor(out=S[:], in0=S[:], in1=DEC[:, :, tt:tt + 1].broadcast_to([128, 2, 128]), op=ALU.mult),
                          reads=[S, DEC], writes=[S])
                    kb.op("dve", lambda e: e.tensor_tensor(out=S[:], in0=S[:], in1=t3[:], op=ALU.add), reads=[S, t3], writes=[S])
                    kb.op("dve", lambda e: e.tensor_tensor(out=T4[:], in0=S[:], in1=Cb[:, s, :].unsqueeze(1).broadcast_to([128, 2, 128]), op=ALU.mult),
                          reads=[S, Cb], writes=[T4])
                    kb.op("dve", lambda e: e.tensor_reduce(out=Y[:, :, tt], in_=T4[:], axis=AX.X, op=ALU.add), reads=[T4], writes=[Y])
            kb.same = old_same
            a, b, c = wk
            ygs = []
            for j in range(2):
                yj = Y[:, j, :]
                kb.op("dve", lambda e: e.scalar_tensor_tensor(out=yj, in0=XS[:, j, :], scalar=pc[:, 24 + j:25 + j], in1=yj, op0=ALU.mult, op1=ALU.add),
                      reads=[XS, pc, Y], writes=[Y])
                kb.op("dve", lambda e: e.tensor_tensor(out=yj, in0=yj, in1=SZ[:, j, :], op=ALU.mult), reads=[Y, SZ], writes=[Y])
                kb.op("act", lambda e: e.activation(out=a[:], in_=yj, func=AF.Square), reads=[Y], writes=[a])
                kb.op("pe", lambda e: e.matmul(pp[0][:, :], lhsT=cx.ones[:], rhs=a[:], start=(j == 0), stop=(j == 1)), reads=[cx.ones, a], writes=[pp[0]])
            kb.op("act", lambda e: e.activation(out=rstd[:], in_=pp[0][:, :], func=AF.Sqrt, scale=1.0 / 256, bias=cx.eps6[:]),
                  reads=[pp[0], cx.eps6], writes=[rstd])
            kb.op("dve", lambda e: e.reciprocal(out=rstd[:], in_=rstd[:]), reads=[rstd], writes=[rstd])
            for j in range(2):
                o = (b, c)[j]
                kb.op("dve", lambda e: e.scalar_tensor_tensor(out=o[:], in0=Y[:, j, :], scalar=pc[:, 26 + j:27 + j], in1=rstd[:], op0=ALU.mult, op1=ALU.mult),
                      reads=[Y, pc, rstd], writes=[o])
                kb.dma(ya_out[j, :, ts], o[:], reads=[o], writes=[cx.OUT])
        kb.barrier()


NEGM = -30000.0


def moba_inputs(nc, cx):
    cx.mb_wqk = din(nc, "mb_wqk", [8, 128, 512])
    cx.mb_wv = din(nc, "mb_wv", [8, 128, 256])
    cx.mb_g = din(nc, "mb_g", [128, 2])
    cx.mb_rel = din(nc, "mb_rel", [1, 64])
    cx.mb_bi = din(nc, "mb_bi", [128, 1280])


def moba_phase1(kb, nc, cx, T, hT_dram, qT_d, kT_d, v_d, nm_d):
    with contextlib.ExitStack() as st:
        Wqk = kb.sb([128, 8, 512], name="mb_Wqk", stack=st)
        kb.dma(Wqk[:], cx.mb_wqk.rearrange("k p c -> p k c"), writes=[Wqk])
        Wv = kb.sb([128, 8, 256], name="mb_Wv", stack=st)
        kb.dma(Wv[:], cx.mb_wv.rearrange("k p c -> p k c"), writes=[Wv])
        gg = kb.sb([128, 2], name="mb_g", stack=st)
        kb.dma(gg[:], cx.mb_g, writes=[gg])
        hTs = [kb.sb([128, 8, 512], name="mb_h%d" % i, stack=st) for i in range(2)]
        kmT = [kb.sb([128, 64], name="mb_km%d" % h, stack=st) for h in range(2)]
        for h in range(2):
            kb.op("dve", lambda e: e.memset(kmT[h][:], 0.0), writes=[kmT[h]])
        rawt = kb.sb([128, 512], name="mb_raw", stack=st)
        sq = kb.sb([128, 512], name="mb_sq", stack=st)
        rstd = kb.sb([128, 512], name="mb_rstd", stack=st)
        QT = kb.sb([128, 512], name="mb_QT", stack=st)
        KT = kb.sb([128, 512], name="mb_KT", stack=st)
        QS = kb.sb([128, 512], BF16, name="mb_QS", stack=st)
        KTb = kb.sb([128, 512], BF16, name="mb_KTb", stack=st)
        gt = kb.sb([128, 64], name="mb_gt", stack=st)
        sel = kb.sb([128, 64], name="mb_sel", stack=st)
        mx = kb.sb([128, 8], name="mb_mx", stack=st)
        nmT = kb.sb([64, 512], BF16, name="mb_nmT", stack=st)
        va = [kb.sb([128, 4, 2, 129], BF16, name="mb_va%d" % i, stack=st) for i in range(2)]
        for i in range(2):
            kb.op("dve", lambda e: e.memset(va[i][:], 1.0), writes=[va[i]])
        pp = cx.psum
        for i in range(T // 512):
            hT = hTs[i % 2]
            ts = slice(i * 512, (i + 1) * 512)
            kb.dma(hT[:], hT_dram.t[:, :, ts].rearrange("k p t -> p k t"), writes=[hT])

            def normed(c, gcol, dst):
                ps = pp[c % 2]
                for k in range(8):
                    kb.op("pe", lambda e: e.matmul(ps[:, :], lhsT=Wqk[:, k, c * 128:(c + 1) * 128], rhs=hT[:, k, :], start=(k == 0), stop=(k == 7)),
                          reads=[Wqk, hT], writes=[ps])
                kb.op("act", lambda e: e.copy(out=rawt[:], in_=ps[:, :]), reads=[ps], writes=[rawt])
                kb.op("act", lambda e: e.activation(out=sq[:], in_=rawt[:], func=AF.Square), reads=[rawt], writes=[sq])
                kb.op("pe", lambda e: e.matmul(pp[2][:, :], lhsT=cx.ones[:], rhs=sq[:], start=True, stop=True), reads=[cx.ones, sq], writes=[pp[2]])
                kb.op("act", lambda e: e.activation(out=rstd[:], in_=pp[2][:, :], func=AF.Sqrt, scale=1.0 / 128, bias=cx.eps6[:]),
                      reads=[pp[2], cx.eps6], writes=[rstd])
                kb.op("dve", lambda e: e.reciprocal(out=rstd[:], in_=rstd[:]), reads=[rstd], writes=[rstd])
                kb.op("dve", lambda e: e.scalar_tensor_tensor(out=dst[:], in0=rawt[:], scalar=gg[:, gcol:gcol + 1], in1=rstd[:], op0=ALU.mult, op1=ALU.mult),
                      reads=[rawt, gg, rstd], writes=[dst])
            for hh in range(2):
                normed(2 + hh, 1, KT)
                kb.op("act", lambda e: e.copy(out=KTb[:], in_=KT[:]), reads=[KT], writes=[KTb])
                kb.dma(kT_d.t[hh, :, ts], KTb[:], reads=[KTb])
                kb.op("dve", lambda e: e.tensor_reduce(out=kmT[hh][:, 2 * i:2 * i + 2], in_=KT[:].rearrange("p (b t) -> p b t", b=2), axis=AX.X, op=ALU.add),
                      reads=[KT], writes=[kmT[hh]])
                normed(hh, 0, QT)
                kb.op("dve", lambda e: e.tensor_scalar(out=QS[:], in0=QT[:], scalar1=128.0 ** -0.5, scalar2=None, op0=ALU.mult), reads=[QT], writes=[QS])
                kb.dma(qT_d.t[hh, :, ts], QS[:], reads=[QS])
                for s in range(4):
                    qb = 2 * i + s // 2
                    if qb == 0:
                        kb.op("dve", lambda e: e.memset(sel[:], 0.0), writes=[sel])
                    else:
                        kb.op("pe", lambda e: e.matmul(pp[3][:, 0:64], lhsT=QT[:, s * 128:(s + 1) * 128], rhs=kmT[hh][:, 0:64], start=True, stop=True),
                              reads=[QT, kmT[hh]], writes=[pp[3]])
                        kb.op("dve", lambda e: e.memset(gt[:], -1e30), writes=[gt])
                        kb.op("dve", lambda e: e.tensor_copy(out=gt[:, 0:qb], in_=pp[3][:, 0:qb]), reads=[pp[3]], writes=[gt])
                        if qb >= 3:
                            kb.op("dve", lambda e: e.max(out=mx[:], in_=gt[:]), reads=[gt], writes=[mx])
                            kb.op("dve", lambda e: e.tensor_scalar(out=sel[:], in0=gt[:], scalar1=mx[:, 2:3], scalar2=None, op0=ALU.is_ge),
                                  reads=[gt, mx], writes=[sel])
                        else:
                            kb.op("dve", lambda e: e.tensor_scalar(out=sel[:], in0=gt[:], scalar1=-1e29, scalar2=None, op0=ALU.is_ge),
                                  reads=[gt], writes=[sel])
                    kb.op("dve", lambda e: e.tensor_scalar(out=sel[:], in0=sel[:], scalar1=-1.0, scalar2=-NEGM, op0=ALU.add, op1=ALU.mult),
                          reads=[sel], writes=[sel])
                    kb.op("pe", lambda e: e.transpose(pp[4][0:64, 0:128], sel[:, 0:64], cx.ident[:]), reads=[sel, cx.ident], writes=[pp[4]])
                    kb.op("act", lambda e: e.copy(out=nmT[0:64, s * 128:(s + 1) * 128], in_=pp[4][0:64, 0:128]), reads=[pp[4]], writes=[nmT])
                kb.dma(nm_d.t[hh, :, ts], nmT[0:64, :], reads=[nmT])
            vt = va[i % 2]
            for s in range(4):
                ps = pp[5 + s % 2]
                for k in range(8):
                    kb.op("pe", lambda e: e.matmul(ps[:, 0:256], lhsT=hT[:, k, s * 128:(s + 1) * 128], rhs=Wv[:, k, :], start=(k == 0), stop=(k == 7)),
                          reads=[hT, Wv], writes=[ps])
                kb.op("act", lambda e: e.copy(out=vt[:, s, :, 0:128], in_=ps[:, 0:256].rearrange("p (h d) -> p h d", h=2)), reads=[ps], writes=[vt])
            for hh in range(2):
                kb.dma(v_d.t[hh, ts, :].rearrange("(s p) c -> p s c", p=128), vt[:, :, hh, :], reads=[vt])
        kb.barrier()


def moba_phase2(kb, nc, cx, T, qT_d, kT_d, v_d, nm_d, yb_out):
    NT = T // 128
    NG = T // 256
    with contextlib.ExitStack() as st:
        KT = kb.sb([128, T], BF16, name="m2_KT", stack=st)
        VA = kb.sb([128, NT, 129], BF16, name="m2_VA", stack=st)
        En = kb.sb([64, 64, 128], BF16, name="m2_En", stack=st)
        identb = kb.sb([128, 128], BF16, name="m2_identb", stack=st)
        kb.op("dve", lambda e: e.tensor_copy(out=identb[:], in_=cx.ident[:]), reads=[cx.ident], writes=[identb])
        kb.op("dve", lambda e: e.tensor_copy(out=En[:], in_=cx.ident[0:64, 0:64].unsqueeze(2).broadcast_to([64, 64, 128])), reads=[cx.ident], writes=[En])
        BI = kb.sb([128, 1280], name="m2_BI", stack=st)
        kb.dma(BI[:], cx.mb_bi, writes=[BI])
        rel = kb.sb([128, 64], name="m2_rel", stack=st)
        kb.dma(rel[:], cx.mb_rel.broadcast_to([128, 64]), writes=[rel])
        BB = kb.sb([128, 1280], name="m2_BB", stack=st)
        BBb = kb.sb([128, 1280], BF16, name="m2_BBb", stack=st)
        tmpb = kb.sb([128, 1280], name="m2_tmpb", stack=st)
        QG = [kb.sb([128, 256], BF16, name="m2_QG%d" % i, stack=st) for i in range(2)]
        NM = [kb.sb([64, 256], BF16, name="m2_NM%d" % i, stack=st) for i in range(2)]
        PT = [kb.sb([128, 256], BF16, name="m2_PT%d" % i, stack=st) for i in range(4)]
        yt = [kb.sb([128, 2, 128], name="m2_yt%d" % i, stack=st) for i in range(2)]
        rinv = kb.sb([128, 1], name="m2_rinv", stack=st)
        pp = cx.psum
        npt = 0
        for hh in range(2):
            kb.dma(KT[:], kT_d.t[hh, :, :], writes=[KT])
            kb.dma(VA[:], v_d.t[hh, :, :].rearrange("(n p) c -> p n c", p=128), writes=[VA])
            kb.op("dve", lambda e: e.tensor_scalar(out=BB[:], in0=BI[:], scalar1=-1.0, scalar2=NEGM, op0=ALU.is_equal, op1=ALU.mult), reads=[BI], writes=[BB])
            for b in range(32):
                kb.op("dve", lambda e: e.tensor_scalar(out=tmpb[:], in0=BI[:], scalar1=float(b), scalar2=rel[:, hh * 32 + b:hh * 32 + b + 1],
                                                       op0=ALU.is_equal, op1=ALU.mult), reads=[BI, rel], writes=[tmpb])
                kb.op("dve", lambda e: e.tensor_tensor(out=BB[:], in0=BB[:], in1=tmpb[:], op=ALU.add), reads=[BB, tmpb], writes=[BB])
            kb.op("act", lambda e: e.copy(out=BBb[:], in_=BB[:]), reads=[BB], writes=[BBb])
            for g in range(NG):
                qg, nm = QG[g % 2], NM[g % 2]
                gs = slice(g * 256, (g + 1) * 256)
                kb.dma(qg[:], qT_d.t[hh, :, gs], writes=[qg])
                kb.dma(nm[:], nm_d.t[hh, :, gs], writes=[nm])
                O = [pp[(g % 2) * 2 + 0], pp[(g % 2) * 2 + 1]]
                for kt in range(2 * g + 2):
                    n = kt // 2
                    past = n < g
                    if kt == 2 * g + 1:
                        q0, w, b0 = 128, 128, 0
                    else:
                        q0, w, b0 = 0, 256, min(2 * g - kt, 8) * 128
                    S = pp[4 + npt % 4]
                    pt = PT[npt % 4]
                    npt += 1
                    kb.op("pe", lambda e: e.matmul(S[:, 0:w], lhsT=KT[:, kt * 128:(kt + 1) * 128], rhs=qg[:, q0:q0 + w], start=True, stop=False),
                          reads=[KT, qg], writes=[S])
                    kb.op("pe", lambda e: e.matmul(S[:, 0:w], lhsT=identb[:], rhs=BBb[:, b0:b0 + w], start=False, stop=(not past)),
                          reads=[identb, BBb], writes=[S])
                    if past:
                        kb.op("pe", lambda e: e.matmul(S[:, 0:w], lhsT=En[0:64, n, :], rhs=nm[0:64, q0:q0 + w], start=False, stop=True),
                              reads=[En, nm], writes=[S])
                    kb.op("act", lambda e: e.activation(out=pt[:, 0:w], in_=S[:, 0:w], func=AF.Exp), reads=[S], writes=[pt])
                    for qq in range(2):
                        if kt == 2 * g + 1 and qq == 0:
                            continue
                        c0 = qq * 128 - q0
                        last = (2 * g) if qq == 0 else (2 * g + 1)
                        kb.op("pe", lambda e: e.matmul(O[qq][:, 0:129], lhsT=pt[:, c0:c0 + 128], rhs=VA[:, kt, :], start=(kt == 0), stop=(kt == last)),
                              reads=[pt, VA], writes=[O[qq]])
                y = yt[g % 2]
                for qq in range(2):
                    kb.op("dve", lambda e: e.reciprocal(out=rinv[:], in_=O[qq][:, 128:129]), reads=[O[qq]], writes=[rinv])
                    kb.op("dve", lambda e: e.tensor_scalar(out=y[:, qq, :], in0=O[qq][:, 0:128], scalar1=rinv[:, 0:1], scalar2=None, op0=ALU.mult),
                          reads=[O[qq], rinv], writes=[y])
                kb.dma(yb_out[hh, gs, :].rearrange("(q p) d -> p q d", p=128), y[:], reads=[y], writes=[cx.OUT])
        kb.barrier()


def lb_inputs(nc, cx, TB):
    cx.xT = din(nc, "xT", [8, 128, TB]); cx.xtok = din(nc, "xtok", [TB, 1024])
    cx.yT = din(nc, "yT", [24, 128, TB])
    cx.g1 = din(nc, "g1", [128, 8]); cx.g2 = din(nc, "g2", [128, 8])
    cx.wg = din(nc, "wg", [8, 128, 3072]); cx.bg = din(nc, "bg", [128, 24])
    cx.pall = din(nc, "pall", [24, 128, 1024])
    cx.wout = din(nc, "wout", [8, 128, 1024])
    cx.wq = din(nc, "wq", [8, 128, 2048])
    cx.k12T = din(nc, "k12T", [128, 256])
    cx.uT = din(nc, "uT", [32, 128, 8, 512])
    cx.vv = din(nc, "vv", [32, 128, 4, 1024])


def lb_gate_merge(kb, nc, cx, TB, hT_d, mT_d):
    with contextlib.ExitStack() as st:
        bg = kb.sb([128, 24], name="b1_bg", stack=st)
        kb.dma(bg[:], cx.bg, writes=[bg])
        Wg = [kb.sb([128, 8, 3, 128], name="b1_wg%d" % i, stack=st) for i in range(2)]
        Pc = [kb.sb([128, 24, 128], name="b1_pc%d" % i, stack=st) for i in range(2)]
        hTs = [kb.sb([128, 8, 512], name="b1_h%d" % i, stack=st) for i in range(2)]
        yTs = [kb.sb([128, 24, 512], name="b1_y%d" % i, stack=st) for i in range(2)]
        gs = [kb.sb([128, 512], name="b1_g%d" % i, stack=st) for i in range(3)]
        m = [kb.sb([128, 512], name="b1_m%d" % i, stack=st) for i in range(2)]
        t = kb.sb([128, 512], name="b1_t", stack=st)
        pp = cx.psum
        it = 0
        for c in range(8):
            wg, pc = Wg[c % 2], Pc[c % 2]
            for br in range(3):
                kb.dma(wg[:, :, br, :], cx.wg[:, :, br * 1024 + c * 128:br * 1024 + (c + 1) * 128].rearrange("k p c -> p k c"), writes=[wg])
            kb.dma(pc[:], cx.pall[:, :, c * 128:(c + 1) * 128].rearrange("k p c -> p k c"), writes=[pc])
            for ti in range(TB // 512):
                ts = slice(ti * 512, (ti + 1) * 512)
                hT, yT = hTs[it % 2], yTs[it % 2]
                mm = m[it % 2]
                it += 1
                kb.dma(hT[:], hT_d.t[:, :, ts].rearrange("k p t -> p k t"), writes=[hT])
                kb.dma(yT[:], cx.yT[:, :, ts].rearrange("k p t -> p k t"), writes=[yT])
                for br in range(3):
                    pg, pj = pp[br], pp[3 + br]
                    for k in range(8):
                        kb.op("pe", lambda e: e.matmul(pg[:, :], lhsT=wg[:, k, br, :], rhs=hT[:, k, :], start=(k == 0), stop=(k == 7)), reads=[wg, hT], writes=[pg])
                    kb.op("act", lambda e: e.activation(out=gs[br][:], in_=pg[:, :], func=AF.Sigmoid, bias=bg[:, br * 8 + c:br * 8 + c + 1]),
                          reads=[pg, bg], writes=[gs[br]])
                    for k in range(8):
                        kb.op("pe", lambda e: e.matmul(pj[:, :], lhsT=pc[:, br * 8 + k, :], rhs=yT[:, br * 8 + k, :], start=(k == 0), stop=(k == 7)), reads=[pc, yT], writes=[pj])
                    if br == 0:
                        kb.op("dve", lambda e: e.tensor_tensor(out=mm[:], in0=gs[br][:], in1=pj[:, :], op=ALU.mult), reads=[gs[br], pj], writes=[mm])
                    else:
                        kb.op("dve", lambda e: e.tensor_tensor(out=t[:], in0=gs[br][:], in1=pj[:, :], op=ALU.mult), reads=[gs[br], pj], writes=[t])
                        kb.op("dve", lambda e: e.tensor_tensor(out=mm[:], in0=mm[:], in1=t[:], op=ALU.add), reads=[mm, t], writes=[mm])
                kb.dma(mT_d.t[c, :, ts], mm[:], reads=[mm])
        kb.barrier()


def lb_outproj(kb, nc, cx, TB, mT_d, xnT_d, xn_tok_d):
    with contextlib.ExitStack() as st:
        Wo = kb.sb([128, 8, 1024], name="b2_wo", stack=st)
        kb.dma(Wo[:], cx.wout.rearrange("k p c -> p k c"), writes=[Wo])
        mTs = [kb.sb([128, 8, 512], name="b2_m%d" % i, stack=st) for i in range(2)]
        xTs = [kb.sb([128, 8, 512], name="b2_x%d" % i, stack=st) for i in range(2)]
        xn = [kb.sb([128, 8, 512], name="b2_xn%d" % i, stack=st) for i in range(2)]
        xt = [kb.sb([128, 1024], name="b2_xt%d" % i, stack=st) for i in range(2)]
        pp = cx.psum
        nx = 0
        for ti in range(TB // 512):
            ts = slice(ti * 512, (ti + 1) * 512)
            mT, xT, xo = mTs[ti % 2], xTs[ti % 2], xn[ti % 2]
            kb.dma(mT[:], mT_d.t[:, :, ts].rearrange("k p t -> p k t"), writes=[mT])
            kb.dma(xT[:], cx.xT[:, :, ts].rearrange("k p t -> p k t"), writes=[xT])
            for c in range(8):
                ps = pp[c % 2]
                for k in range(8):
                    kb.op("pe", lambda e: e.matmul(ps[:, :], lhsT=Wo[:, k, c * 128:(c + 1) * 128], rhs=mT[:, k, :], start=(k == 0), stop=(k == 7)), reads=[Wo, mT], writes=[ps])
                kb.op("dve", lambda e: e.tensor_tensor(out=xo[:, c, :], in0=ps[:, :], in1=xT[:, c, :], op=ALU.add), reads=[ps, xT], writes=[xo])
            kb.dma(xnT_d.t[:, :, ts].rearrange("k p t -> p k t"), xo[:], reads=[xo])
            for s in range(4):
                xk = xt[nx % 2]
                nx += 1
                r0 = ti * 512 + s * 128
                kb.dma(xk[:], cx.xtok[r0:r0 + 128, :], writes=[xk])
                for hf in range(2):
                    ps = pp[2 + hf]
                    for k in range(8):
                        kb.op("pe", lambda e: e.matmul(ps[:, :], lhsT=mT[:, k, s * 128:(s + 1) * 128], rhs=Wo[:, k, hf * 512:(hf + 1) * 512], start=(k == 0), stop=(k == 7)),
                              reads=[Wo, mT], writes=[ps])
                    kb.op("dve", lambda e: e.tensor_tensor(out=xk[:, hf * 512:(hf + 1) * 512], in0=ps[:, :], in1=xk[:, hf * 512:(hf + 1) * 512], op=ALU.add),
                          reads=[ps, xk], writes=[xk])
                kb.dma(xn_tok_d.t[r0:r0 + 128, :], xk[:], reads=[xk])
        kb.barrier()


def lb_peer_scores(kb, nc, cx, TB, h2T_d, s_d):
    with contextlib.ExitStack() as st:
        Wq = kb.sb([128, 8, 2048], name="b3_wq", stack=st)
        kb.dma(Wq[:], cx.wq.rearrange("k p c -> p k c"), writes=[Wq])
        k12 = kb.sb([128, 256], name="b3_k12", stack=st)
        kb.dma(k12[:], cx.k12T, writes=[k12])
        hTs = [kb.sb([128, 8, 512], name="b3_h%d" % i, stack=st) for i in range(2)]
        qT = [kb.sb([128, 512], name="b3_q%d" % i, stack=st) for i in range(2)]
        sc = [kb.sb([128, 4, 128], name="b3_s%d" % i, stack=st) for i in range(2)]
        pp = cx.psum
        for ti in range(TB // 512):
            ts = slice(ti * 512, (ti + 1) * 512)
            hT = hTs[ti % 2]
            kb.dma(hT[:], h2T_d.t[:, :, ts].rearrange("k p t -> p k t"), writes=[hT])
            for ct in range(16):
                ps = pp[ct % 2]
                q = qT[ct % 2]
                so = sc[ct % 2]
                for k in range(8):
                    kb.op("pe", lambda e: e.matmul(ps[:, :], lhsT=Wq[:, k, ct * 128:(ct + 1) * 128], rhs=hT[:, k, :], start=(k == 0), stop=(k == 7)), reads=[Wq, hT], writes=[ps])
                kb.op("act", lambda e: e.copy(out=q[:], in_=ps[:, :]), reads=[ps], writes=[q])
                p2 = pp[2 + ct % 2]
                for s in range(4):
                    kb.op("pe", lambda e: e.matmul(p2[:, s * 128:(s + 1) * 128], lhsT=q[:, s * 128:(s + 1) * 128], rhs=k12[:, (ct % 2) * 128:(ct % 2) * 128 + 128], start=True, stop=True),
                          reads=[q, k12], writes=[p2])
                kb.op("dve", lambda e: e.tensor_copy(out=so[:], in_=p2[:, :].rearrange("p (s n) -> p s n", s=4)), reads=[p2], writes=[so])
                kb.dma(s_d.t[ct % 2, ts, ct // 2, :].rearrange("(s p) n -> p s n", p=128), so[:], reads=[so])
        kb.barrier()


def build_lb(nc, kb, cx, TB):
    setup_consts(kb, nc, cx)
    lb_inputs(nc, cx, TB)
    out_d = dout(nc, "xout", [TB, 1024])
    hT_d = kb.dram("hT_s", [8, 128, TB]); mT_d = kb.dram("mT_s", [8, 128, TB])
    xnT_d = kb.dram("xnT_s", [8, 128, TB]); xn_tok_d = kb.dram("xntok_s", [TB, 1024])
    h2T_d = kb.dram("h2T_s", [8, 128, TB]); s_d = kb.dram("s_s", [2, TB, 8, 128])
    phase_norm(kb, nc, cx, TB, cx.xT, cx.g1, hT_d)
    lb_gate_merge(kb, nc, cx, TB, hT_d, mT_d)
    lb_outproj(kb, nc, cx, TB, mT_d, xnT_d, xn_tok_d)
    phase_norm(kb, nc, cx, TB, xnT_d.t, cx.g2, h2T_d)
    lb_peer_scores(kb, nc, cx, TB, h2T_d, s_d)
    lb_peer_main2(kb, nc, cx, TB, h2T_d, s_d, xn_tok_d, out_d)
    kb.finish()


def lb_peer_main2(kb, nc, cx, TB, h2T_d, s_d, xn_tok_d, out_d, NQ=8, NT=2):
    QI = 128 // NQ
    QE = QI * 128
    NCH = QE // 512
    TP = NT * 128
    with contextlib.ExitStack() as st:
        identb = kb.sb([128, 128], BF16, name="b5_identb", stack=st)
        kb.op("dve", lambda e: e.tensor_copy(out=identb[:], in_=cx.ident[:]), reads=[cx.ident], writes=[identb])
        hf = kb.sb([128, 8, TP], name="b5_hf", stack=st)
        hb = kb.sb([128, 8, TP], BF16, name="b5_hb", stack=st)
        s1 = kb.sb([128, NT, 8, 128], name="b5_s1", stack=st)
        s2 = kb.sb([128, NT, 8, 128], name="b5_s2", stack=st)
        xk = [kb.sb([128, 1024], name="b5_xk%d" % i, stack=st) for i in range(NT)]
        wk = kb.sb([128, 256], name="b5_wk", stack=st)
        v12 = kb.sb([128, 2, 16], name="b5_v12", stack=st)
        cand = kb.sb([128, 16, 16], name="b5_cand", stack=st)
        c16 = kb.sb([128, 16], name="b5_c16", stack=st)
        e16 = kb.sb([128, 16], name="b5_e16", stack=st)
        tau = kb.sb([128, NT, 8], name="b5_tau", stack=st)
        negm = kb.sb([128, NT, 8], name="b5_negm", stack=st)
        rz = kb.sb([128, NT, 8], name="b5_rz", stack=st)
        E = kb.sb([128, QI, 128], name="b5_E", stack=st)
        M = kb.sb([128, QI, 128], name="b5_M", stack=st)
        X = kb.sb([128, QI, 128], name="b5_X", stack=st)
        G = [kb.sb([128, QE], name="b5_G%d" % i, stack=st) for i in range(NT)]
        Ac = [kb.sb([128, 512], name="b5_A%d" % i, stack=st) for i in range(2)]
        GAb = [kb.sb([128, 512], BF16, name="b5_GAb%d" % i, stack=st) for i in range(2)]
        GAT = [kb.sb([128, 4, 128], BF16, name="b5_GAT%d" % i, stack=st) for i in range(2)]
        uc = [kb.sb([128, 8, 512], name="b5_u%d" % i, stack=st) for i in range(2)]
        ub = [kb.sb([128, 8, 512], BF16, name="b5_ub%d" % i, stack=st) for i in range(2)]
        vc = [kb.sb([128, 4, 1024], name="b5_v%d" % i, stack=st) for i in range(2)]
        vb = [kb.sb([128, 4, 1024], BF16, name="b5_vb%d" % i, stack=st) for i in range(2)]
        pp = cx.psum
        nch = 0
        nt = 0
        for tp in range(TB // TP):
            r0 = tp * TP
            kb.dma(hf[:], h2T_d.t[:, :, r0:r0 + TP].rearrange("k p t -> p k t"), writes=[hf])
            kb.op("act", lambda e: e.copy(out=hb[:], in_=hf[:]), reads=[hf], writes=[hb])
            for t in range(NT):
                kb.dma(s1[:, t, :, :], s_d.t[0, r0 + t * 128:r0 + (t + 1) * 128, :, :], writes=[s1])
                kb.dma(s2[:, t, :, :], s_d.t[1, r0 + t * 128:r0 + (t + 1) * 128, :, :], writes=[s2])
                kb.dma(xk[t][:], xn_tok_d.t[r0 + t * 128:r0 + (t + 1) * 128, :], writes=[xk[t]])
            for t in range(NT):
                for hd in range(8):
                    for w, src in enumerate((s1, s2)):
                        kb.op("dve", lambda e: e.max(out=v12[:, w, 0:8], in_=src[:, t, hd, :]), reads=[src], writes=[v12])
                        kb.op("dve", lambda e: e.match_replace(out=wk[:, 0:128], in_to_replace=v12[:, w, 0:8], in_values=src[:, t, hd, :], imm_value=-1e30),
                              reads=[src, v12], writes=[wk])
                        kb.op("dve", lambda e: e.max(out=v12[:, w, 8:16], in_=wk[:, 0:128]), reads=[wk], writes=[v12])
                    kb.op("dve", lambda e: e.tensor_tensor(out=cand[:], in0=v12[:, 0, :].unsqueeze(2).broadcast_to([128, 16, 16]),
                                                           in1=v12[:, 1, :].unsqueeze(1).broadcast_to([128, 16, 16]), op=ALU.add), reads=[v12], writes=[cand])
                    cf = cand[:].rearrange("p a b -> p (a b)")
                    kb.op("dve", lambda e: e.max(out=c16[:, 0:8], in_=cf), reads=[cand], writes=[c16])
                    kb.op("dve", lambda e: e.match_replace(out=wk[:], in_to_replace=c16[:, 0:8], in_values=cf, imm_value=-1e30), reads=[cand, c16], writes=[wk])
                    kb.op("dve", lambda e: e.max(out=c16[:, 8:16], in_=wk[:]), reads=[wk], writes=[c16])
                    kb.op("dve", lambda e: e.tensor_copy(out=tau[:, t, hd:hd + 1], in_=c16[:, 15:16]), reads=[c16], writes=[tau])
                    kb.op("dve", lambda e: e.tensor_scalar(out=negm[:, t, hd:hd + 1], in0=c16[:, 0:1], scalar1=-1.0, scalar2=None, op0=ALU.mult), reads=[c16], writes=[negm])
                    kb.op("act", lambda e: e.activation(out=e16[:], in_=c16[:], func=AF.Exp, bias=negm[:, t, hd:hd + 1]), reads=[c16, negm], writes=[e16])
                    kb.op("dve", lambda e: e.tensor_reduce(out=rz[:, t, hd:hd + 1], in_=e16[:], axis=AX.X, op=ALU.add), reads=[e16], writes=[rz])
            kb.op("dve", lambda e: e.reciprocal(out=rz[:], in_=rz[:]), reads=[rz], writes=[rz])
            O = [[pp[4 + 2 * t + h2] for h2 in range(2)] for t in range(NT)]
            for qi in range(NQ):
                for t in range(NT):
                    G3 = G[t][:].rearrange("p (i j) -> p i j", j=128)
                    for hd in range(8):
                        kb.op("dve", lambda e: e.tensor_tensor(out=E[:], in0=s1[:, t, hd, qi * QI:(qi + 1) * QI].unsqueeze(2).broadcast_to([128, QI, 128]),
                                                               in1=s2[:, t, hd, :].unsqueeze(1).broadcast_to([128, QI, 128]), op=ALU.add), reads=[s1, s2], writes=[E])
                        kb.op("act", lambda e: e.activation(out=X[:], in_=E[:], func=AF.Exp, bias=negm[:, t, hd:hd + 1]), reads=[E, negm], writes=[X])
                        kb.op("dve", lambda e: e.scalar_tensor_tensor(out=M[:], in0=E[:], scalar=tau[:, t, hd:hd + 1], in1=X[:], op0=ALU.is_ge, op1=ALU.mult),
                              reads=[E, tau, X], writes=[M])
                        if hd == 0:
                            kb.op("dve", lambda e: e.tensor_scalar(out=G3, in0=M[:], scalar1=rz[:, t, hd:hd + 1], scalar2=None, op0=ALU.mult),
                                  reads=[M, rz], writes=[G[t]])
                        else:
                            kb.op("dve", lambda e: e.scalar_tensor_tensor(out=G3, in0=M[:], scalar=rz[:, t, hd:hd + 1], in1=G3, op0=ALU.mult, op1=ALU.add),
                                  reads=[M, rz, G[t]], writes=[G[t]])
                for ch in range(NCH):
                    e0 = qi * QE + ch * 512
                    u, v, ubb, vbb = uc[nch % 2], vc[nch % 2], ub[nch % 2], vb[nch % 2]
                    nch += 1
                    kb.dma(u[:], cx.uT[e0 // 512], writes=[u])
                    kb.dma(v[:], cx.vv[e0 // 512], writes=[v])
                    kb.op("act", lambda e: e.copy(out=ubb[:], in_=u[:]), reads=[u], writes=[ubb])
                    kb.op("act", lambda e: e.copy(out=vbb[:], in_=v[:]), reads=[v], writes=[vbb])
                    for t in range(NT):
                        A, gab, gat = Ac[nt % 2], GAb[nt % 2], GAT[nt % 2]
                        pa, ptr = pp[nt % 2], pp[2 + nt % 2]
                        nt += 1
                        ptrb = ptr[:, 0:256].bitcast(BF16)
                        for k in range(8):
                            kb.op("pe", lambda e: e.matmul(pa[:, :], lhsT=hb[:, k, t * 128:(t + 1) * 128], rhs=ubb[:, k, :], start=(k == 0), stop=(k == 7)),
                                  reads=[hb, ubb], writes=[pa])
                        kb.op("act", lambda e: e.activation(out=A[:], in_=pa[:, :], func=AF.Gelu), reads=[pa], writes=[A])
                        kb.op("dve", lambda e: e.tensor_tensor(out=gab[:], in0=A[:], in1=G[t][:, ch * 512:(ch + 1) * 512], op=ALU.mult), reads=[A, G[t]], writes=[gab])
                        for s in range(4):
                            kb.op("pe", lambda e: e.transpose(ptrb[:, s * 128:(s + 1) * 128], gab[:, s * 128:(s + 1) * 128], identb[:]), reads=[gab, identb], writes=[ptr])
                        kb.op("act", lambda e: e.copy(out=gat[:], in_=ptrb.rearrange("p (s t) -> p s t", s=4)), reads=[ptr], writes=[gat])
                        first = (qi == 0 and ch == 0)
                        for s in range(4):
                            lastmm = (qi == NQ - 1 and ch == NCH - 1 and s == 3)
                            for h2 in range(2):
                                kb.op("pe", lambda e: e.matmul(O[t][h2][:, :], lhsT=gat[:, s, :], rhs=vbb[:, s, h2 * 512:(h2 + 1) * 512], start=(first and s == 0), stop=lastmm),
                                      reads=[gat, vbb], writes=[O[t][h2]])
            for t in range(NT):
                for h2 in range(2):
                    kb.op("dve", lambda e: e.tensor_tensor(out=xk[t][:, h2 * 512:(h2 + 1) * 512], in0=O[t][h2][:, :], in1=xk[t][:, h2 * 512:(h2 + 1) * 512], op=ALU.add),
                          reads=[O[t][h2], xk[t]], writes=[xk[t]])
                kb.dma(out_d[r0 + t * 128:r0 + (t + 1) * 128, :], xk[t][:], reads=[xk[t]], writes=[cx.OUT])
        kb.barrier()

C = np.ascontiguousarray
IN_SSD = 1024 + 2048 + 16
B2 = IN_SSD + 3072

def colT(v):
    return C(np.asarray(v, np.float32).reshape(-1, 128).T)

def wtiles(w):
    return C(np.asarray(w, np.float32).reshape(8, 128, -1))

def prep_common(x_b, norm_g):
    T = x_b.shape[0]
    return {"xT": C(x_b.T.reshape(8, 128, T)), "g1": colT(norm_g), "ident": np.eye(128, dtype=np.float32)}

def prep_rwkv(inp, l, hg):
    w_in = inp["w_in"][l]
    cs = slice(256 * hg, 256 * hg + 256)
    r = w_in[:, B2:B2 + 1024][:, cs]; k = w_in[:, B2 + 1024:B2 + 2048][:, cs]; v = w_in[:, B2 + 2048:B2 + 3072][:, cs]
    la = w_in[:, B2 + 3072:B2 + 3200]; gl = w_in[:, B2 + 3200:B2 + 3328]
    W = np.concatenate([r, k, v, la, gl], axis=1)
    mu = inp["rwkv_mu"][l]
    mu_all = np.concatenate([mu[0:1024][cs], mu[1024:2048][cs], mu[2048:3072][cs], mu[3072:3200], mu[3200:3328]])
    pc = np.concatenate([colT(inp[n][l].reshape(-1)[cs]) for n in ("w0", "a0", "k_k", "k_a", "r_k", "lnx_g", "lnx_b")], axis=1)
    la2 = np.concatenate([inp["w_w2"][l][:, cs], inp["w_a2"][l][:, cs]], axis=0)
    return {"rw_w": wtiles(W), "rw_mu": colT(mu_all), "rw_pc": C(pc), "rw_la2": C(la2.astype(np.float32)), "rw_g2": C(inp["w_g2"][l][:, cs])}

def prep_ssd(inp, l, hg):
    w_in = inp["w_in"][l]
    z = w_in[:, 256 * hg:256 * hg + 256]
    x = w_in[:, 1024 + 256 * hg:1024 + 256 * hg + 256]
    Bc = w_in[:, 2048 + 128 * hg:2048 + 128 * hg + 128]
    Cc = w_in[:, 2560 + 128 * hg:2560 + 128 * hg + 128]
    dtc = w_in[:, 3072 + 4 * hg:3072 + 4 * hg + 4]
    dtrep = np.concatenate([np.repeat(dtc[:, 2 * j + ep:2 * j + ep + 1], 64, axis=1) for j in range(2) for ep in range(2)], axis=1)
    W = np.concatenate([z, x, Bc, Cc, dtrep], axis=1)
    cw = inp["conv_w"][l]; cb = inp["conv_b"][l]
    chans = np.concatenate([np.arange(256 * hg, 256 * hg + 256), 1024 + np.arange(128 * hg, 128 * hg + 128), 1536 + np.arange(128 * hg, 128 * hg + 128)])
    convw = np.zeros((128, 16), np.float32); convb = np.zeros((128, 4), np.float32)
    for c in range(4):
        ch = chans[c * 128:(c + 1) * 128]
        for k in range(4):
            convw[:, c * 4 + k] = cw[k, ch]
        convb[:, c] = cb[ch]
    def hrep(v):
        v = np.asarray(v)[4 * hg:4 * hg + 4]
        return np.stack([np.repeat(v[2 * j:2 * j + 2], 64) for j in range(2)], axis=1).astype(np.float32)
    pc = np.concatenate([convw, convb, hrep(inp["dt_bias"][l]), hrep(inp["a_log"][l]), hrep(inp["d_skip"][l]),
                         colT(inp["ssd_norm_g"][l][256 * hg:256 * hg + 256])], axis=1)
    return {"ss_w": wtiles(W), "ss_pc": C(pc)}

def bucket_strip():
    import jax, jax.numpy as jnp, math
    cpu = jax.devices("cpu")[0]
    with jax.default_device(cpu):
        d = jnp.arange(0, 1280, dtype=jnp.int32)
        max_exact = 16
        df = jnp.maximum(d, 1).astype(jnp.float32)
        large = max_exact + (jnp.log(df / max_exact) / math.log(1024 / max_exact) * (32 - max_exact)).astype(jnp.int32)
        large = jnp.minimum(large, 31)
        bk = np.asarray(jnp.where(d < max_exact, d, large))
    m = np.arange(1280)[None, :] - np.arange(128)[:, None]
    out = np.where(m >= 0, bk[np.clip(m, 0, 1279)], -1).astype(np.float32)
    return C(out)

def prep_moba(inp, l, hg):
    w_in = inp["w_in"][l]
    hs = [2 * hg, 2 * hg + 1]
    q = np.concatenate([w_in[:, IN_SSD + h * 128:IN_SSD + (h + 1) * 128] for h in hs], axis=1)
    k = np.concatenate([w_in[:, IN_SSD + 1024 + h * 128:IN_SSD + 1024 + (h + 1) * 128] for h in hs], axis=1)
    v = np.concatenate([w_in[:, IN_SSD + 2048 + h * 128:IN_SSD + 2048 + (h + 1) * 128] for h in hs], axis=1)
    g = np.stack([inp["q_norm_g"][l], inp["k_norm_g"][l]], axis=1).astype(np.float32)
    rel = np.concatenate([inp["rel_bias"][:, h] for h in hs])[None, :].astype(np.float32)
    return {"mb_wqk": wtiles(np.concatenate([q, k], axis=1)), "mb_wv": wtiles(v), "mb_g": C(g), "mb_rel": C(rel), "mb_bi": bucket_strip()}

def prep_lb_weights(inp, l):
    return {"g1": colT(inp["norm1_g"][l]), "g2": colT(inp["norm2_g"][l]),
            "wg": wtiles(inp["w_gate"][l]), "bg": colT(inp["b_gate"][l]),
            "pall": C(np.concatenate([inp["p_ssd"][l], inp["p_moba"][l], inp["p_rwkv"][l]], axis=0).reshape(24, 128, 1024).astype(np.float32)),
            "wout": wtiles(inp["w_out"][l]), "wq": wtiles(inp["peer_wq"][l]),
            "k12T": C(np.concatenate([inp["peer_k1"][l].T, inp["peer_k2"][l].T], axis=1).astype(np.float32)),
            "uT": C(np.asarray(inp["peer_u"][l], np.float32).reshape(32, 512, 8, 128).transpose(0, 3, 2, 1)), "vv": C(np.asarray(inp["peer_v"][l], np.float32).reshape(32, 4, 128, 1024).transpose(0, 2, 1, 3)), "ident": np.eye(128, dtype=np.float32)}

def prep_lb_acts(x_tok, yfull):
    TB = x_tok.shape[0]
    return {"xT": C(x_tok.T.reshape(8, 128, TB)), "xtok": C(x_tok), "yT": C(yfull.T.reshape(24, 128, TB))}

import time as _time
from concourse.bass_utils import run_bass_kernel_spmd

T_FULL = 16384
TB_FULL = 4096
_PROG = {}


def build_la(T):
    nc = bass.Bass("TRN2", target_bir_lowering=False)
    kb = KB(nc)
    cx = Ctx()
    setup_consts(kb, nc, cx)
    xT_d = din(nc, "xT", [8, 128, T]); g_d = din(nc, "g1", [128, 8])
    rwkv_inputs(nc, cx); ssd_inputs(nc, cx); moba_inputs(nc, cx)
    ya = dout(nc, "yaT", [2, 128, T]); yb = dout(nc, "yb", [2, T, 128]); yc = dout(nc, "ycT", [2, 128, T])
    hT = kb.dram("hT_s", [8, 128, T])
    strm = kb.dram("rw_strm", [5, 2, T, 128], BF16); strm_w = kb.dram("rw_strmw", [2, T, 128]); rfm = kb.dram("rw_fm", [4, 2, 128, T])
    sfm = kb.dram("ss_fm", [4, 2, 128, T]); btok = kb.dram("ss_b", [T, 128]); ctok = kb.dram("ss_c", [T, 128])
    qT = kb.dram("mb_q", [2, 128, T], BF16); kT = kb.dram("mb_k", [2, 128, T], BF16); vd = kb.dram("mb_v", [2, T, 129], BF16); nm = kb.dram("mb_nm", [2, 64, T], BF16)
    phase_norm(kb, nc, cx, T, xT_d, g_d, hT)
    rwkv_phase1(kb, nc, cx, T, hT, strm, rfm, strm_w=strm_w)
    ssd_phase1(kb, nc, cx, T, hT, sfm, btok, ctok)
    moba_phase1(kb, nc, cx, T, hT, qT, kT, vd, nm)
    moba_phase2(kb, nc, cx, T, qT, kT, vd, nm, yb)
    rwkv_phase2(kb, nc, cx, T, strm, rfm, yc, same=False, strm_w=strm_w)
    ssd_phase2(kb, nc, cx, T, sfm, btok, ctok, ya, same=False)
    kb.finish()
    return nc, kb


def build_lb_prog(TB):
    nc = bass.Bass("TRN2", target_bir_lowering=False)
    kb = KB(nc)
    cx = Ctx()
    build_lb(nc, kb, cx, TB)
    return nc, kb


def _prog(kind, n):
    key = (kind, n)
    if key not in _PROG:
        _PROG[key] = (build_la if kind == "A" else build_lb_prog)(n)[0]
    return _PROG[key]


def kernel(**inputs):
    inp = {k: np.asarray(v) for k, v in inputs.items()}
    x = np.ascontiguousarray(inp["x"], dtype=np.float32)
    Bn, T, D = x.shape
    TB = (Bn * T) // 8
    per_b = T // TB
    ncores = 8
    for l in range(2):
        ncA = _prog("A", T)
        in_maps = []
        for c in range(ncores):
            b, hg = c // 4, c % 4
            m = prep_common(x[b], inp["norm1_g"][l])
            m.update(prep_rwkv(inp, l, hg)); m.update(prep_ssd(inp, l, hg)); m.update(prep_moba(inp, l, hg))
            in_maps.append(m)
        resA = run_bass_kernel_spmd(ncA, in_maps, core_ids=list(range(ncores))).results
        del in_maps
        yfull = np.empty((Bn, T, 3072), np.float32)
        for c in range(ncores):
            b, hg = c // 4, c % 4
            r = resA[c]
            yfull[b, :, 256 * hg:256 * hg + 256] = np.asarray(r["yaT"]).reshape(256, T).T
            yfull[b, :, 1024 + 256 * hg:1024 + 256 * hg + 256] = np.asarray(r["yb"]).transpose(1, 0, 2).reshape(T, 256)
            yfull[b, :, 2048 + 256 * hg:2048 + 256 * hg + 256] = np.asarray(r["ycT"]).reshape(256, T).T
        del resA
        ncB = _prog("B", TB)
        wts = prep_lb_weights(inp, l)
        in_maps = []
        for c in range(ncores):
            b, s0 = c // per_b, (c % per_b) * TB
            m = dict(wts)
            m.update(prep_lb_acts(x[b, s0:s0 + TB], yfull[b, s0:s0 + TB]))
            in_maps.append(m)
        resB = run_bass_kernel_spmd(ncB, in_maps, core_ids=list(range(ncores))).results
        del in_maps
        xn = np.empty_like(x)
        for c in range(ncores):
            b, s0 = c // per_b, (c % per_b) * TB
            xn[b, s0:s0 + TB] = np.asarray(resB[c]["xout"])
        x = xn
    return x
```

```python
import contextlib
import numpy as np
import concourse.bass as bass
import concourse.mybir as mybir

F32 = mybir.dt.float32
BF16 = mybir.dt.bfloat16
ALU = mybir.AluOpType
AF = mybir.ActivationFunctionType
AX = mybir.AxisListType

EP = 30000
KD = 24


class Buf:
    __slots__ = ("t", "w", "r", "name")

    def __init__(self, t, name=""):
        self.t = t
        self.w = {}
        self.r = {}
        self.name = name

    def __getitem__(self, idx):
        return self.t[idx]


class KB:
    def __init__(self, nc, same_engine_sync=True):
        self.nc = nc
        self.es = contextlib.ExitStack()
        self.eng = {"pe": nc.tensor, "dve": nc.vector, "act": nc.scalar, "pool": nc.gpsimd, "sp": nc.sync}
        self.cnt = {e: 0 for e in ("pe", "dve", "act", "pool")}
        self.sems = {}
        self.waited = {e: {} for e in self.eng}
        self.dma_i = 0
        self.dma_last = {}
        self.same = same_engine_sync
        self.nbuf = 0
        self.n_wait = 0

    def sem(self, key):
        s = self.sems.get(key)
        if s is None:
            s = self.es.enter_context(self.nc.semaphore("s_%s_%s" % key))
            self.sems[key] = s
        return s

    def sb(self, shape, dt=F32, name=None, stack=None):
        self.nbuf += 1
        name = ("s%d_" % self.nbuf + name) if name else "sb%d" % self.nbuf
        t = (stack or self.es).enter_context(self.nc.sbuf_tensor(name, list(shape), dt))
        return Buf(t, name)

    def ps(self, shape, dt=F32, name=None, stack=None):
        self.nbuf += 1
        name = ("p%d_" % self.nbuf + name) if name else "ps%d" % self.nbuf
        t = (stack or self.es).enter_context(self.nc.psum_tensor(name, list(shape), dt))
        return Buf(t, name)

    def dram(self, name, shape, dt=F32, kind="Internal"):
        t = self.nc.dram_tensor(name, list(shape), dt, kind=kind)
        return Buf(t.ap(), name)

    def _deps(self, reads, writes):
        deps = {}
        for b in reads:
            for k, v in b.w.items():
                if deps.get(k, 0) < v:
                    deps[k] = v
        for b in writes:
            for k, v in b.w.items():
                if deps.get(k, 0) < v:
                    deps[k] = v
            for k, v in b.r.items():
                if deps.get(k, 0) < v:
                    deps[k] = v
        return deps

    def _wait(self, E, deps):
        wd = self.waited[E]
        for k, v in deps.items():
            if k[0] == E:
                if E == "pe" or not self.same:
                    continue
            if wd.get(k, 0) >= v:
                continue
            self.eng[E].wait_ge(self.sem(k), v)
            self.n_wait += 1
            wd[k] = v

    def _mark(self, dep, reads, writes):
        k, v = dep
        for b in reads:
            b.r[k] = v
        for b in writes:
            b.w = {k: v}
            b.r = {}

    def op(self, E, fn, reads=(), writes=()):
        self._wait(E, self._deps(reads, writes))
        ins = fn(self.eng[E])
        n = self.cnt[E] = self.cnt[E] + 1
        key = (E, (n - 1) // EP)
        val = (n - 1) % EP + 1
        ins.then_inc(self.sem(key), 1)
        self._mark((key, val), reads, writes)
        return ins

    def dma(self, out, in_, reads=(), writes=(), Q="sp", **kw):
        self._wait(Q, self._deps(reads, writes))
        i = self.dma_i
        self.dma_i += 1
        s = i % KD
        key = ("dma", s)
        val = 16 * (i // KD + 1)
        if i >= KD:
            wd = self.waited[Q]
            if wd.get(key, 0) < val - 16:
                self.eng[Q].wait_ge(self.sem(key), val - 16)
                wd[key] = val - 16
        self.eng[Q].dma_start(out=out, in_=in_, **kw).then_inc(self.sem(key), 16)
        self.dma_last[s] = val
        self._mark((key, val), reads, writes)

    def barrier(self):
        state = {}
        for e, n in self.cnt.items():
            if n > 0:
                state[(e, (n - 1) // EP)] = (n - 1) % EP + 1
        for s, v in self.dma_last.items():
            state[("dma", s)] = v
        for E in self.eng:
            wd = self.waited[E]
            for k, v in state.items():
                if k[0] == E and E == "pe":
                    continue
                if wd.get(k, 0) >= v:
                    continue
                self.eng[E].wait_ge(self.sem(k), v)
                wd[k] = v

    def finish(self):
        self.barrier()

import math


def din(nc, name, shape, dt=F32):
    return nc.dram_tensor(name, list(shape), dt, kind="ExternalInput").ap()


def dout(nc, name, shape, dt=F32):
    return nc.dram_tensor(name, list(shape), dt, kind="ExternalOutput").ap()


class Ctx:
    pass


def setup_consts(kb, nc, cx):
    cx.ident_d = din(nc, "ident", [128, 128])
    cx.ident = kb.sb([128, 128], name="ident")
    kb.dma(cx.ident[:], cx.ident_d, writes=[cx.ident])
    cx.ones = kb.sb([128, 128], name="ones")
    kb.op("dve", lambda e: e.memset(cx.ones[:], 1.0), writes=[cx.ones])
    cx.bones = kb.sb([128, 128], name="bones")
    kb.op("dve", lambda e: e.memset(cx.bones[:], 0.0), writes=[cx.bones])
    kb.op("dve", lambda e: e.memset(cx.bones[0:64, 0:64], 1.0), writes=[cx.bones])
    kb.op("dve", lambda e: e.memset(cx.bones[64:128, 64:128], 1.0), writes=[cx.bones])
    cx.eps6 = kb.sb([128, 1], name="eps6")
    kb.op("dve", lambda e: e.memset(cx.eps6[:], 1e-6), writes=[cx.eps6])
    cx.psum = [kb.ps([128, 512], name="bank%d" % i) for i in range(8)]
    cx.OUT = Buf(None, "OUT")


def phase_norm(kb, nc, cx, T, xT_d, g_d, hT_dram):
    import contextlib
    with contextlib.ExitStack() as st:
        g = kb.sb([128, 8], name="n_g", stack=st)
        kb.dma(g[:], g_d, writes=[g])
        xs = [kb.sb([128, 8, 512], name="n_x%d" % i, stack=st) for i in range(2)]
        sq = kb.sb([128, 8, 512], name="n_sq", stack=st)
        rstd = kb.sb([128, 512], name="n_rstd", stack=st)
        hs = [kb.sb([128, 8, 512], name="n_h%d" % i, stack=st) for i in range(2)]
        p0 = cx.psum[0]
        for i in range(T // 512):
            x = xs[i % 2]
            h = hs[i % 2]
            kb.dma(x[:], xT_d[:, :, i * 512:(i + 1) * 512].rearrange("k p t -> p k t"), writes=[x])
            kb.op("act", lambda e: e.activation(out=sq[:], in_=x[:], func=AF.Square), reads=[x], writes=[sq])
            for k in range(8):
                kb.op("pe", lambda e: e.matmul(p0[:, :], lhsT=cx.ones[:], rhs=sq[:, k, :], start=(k == 0), stop=(k == 7)),
                      reads=[cx.ones, sq], writes=[p0])
            kb.op("act", lambda e: e.activation(out=rstd[:], in_=p0[:, :], func=AF.Sqrt, scale=1.0 / 1024, bias=cx.eps6[:]),
                  reads=[p0, cx.eps6], writes=[rstd])
            kb.op("dve", lambda e: e.reciprocal(out=rstd[:], in_=rstd[:]), reads=[rstd], writes=[rstd])
            for k in range(8):
                kb.op("dve", lambda e: e.scalar_tensor_tensor(out=h[:, k, :], in0=x[:, k, :], scalar=g[:, k:k + 1], in1=rstd[:],
                                                              op0=ALU.mult, op1=ALU.mult), reads=[x, g, rstd], writes=[h])
            kb.dma(hT_dram.t[:, :, i * 512:(i + 1) * 512].rearrange("k p t -> p k t"), h[:], reads=[h], writes=[hT_dram])
        kb.barrier()


EM05 = math.exp(-0.5)


def rwkv_inputs(nc, cx):
    cx.rw_w = din(nc, "rw_w", [8, 128, 1024])
    cx.rw_mu = din(nc, "rw_mu", [128, 8])
    cx.rw_pc = din(nc, "rw_pc", [128, 14])
    cx.rw_la2 = din(nc, "rw_la2", [128, 256])
    cx.rw_g2 = din(nc, "rw_g2", [128, 256])


def rwkv_phase1(kb, nc, cx, T, hT_dram, strm, fm, strm_w=None):
    with contextlib.ExitStack() as st:
        W = kb.sb([128, 8, 1024], name="rw_W", stack=st)
        kb.dma(W[:], cx.rw_w.rearrange("k p c -> p k c"), writes=[W])
        mu = kb.sb([128, 8], name="rw_mu", stack=st)
        kb.dma(mu[:], cx.rw_mu, writes=[mu])
        pc = kb.sb([128, 14], name="rw_pc", stack=st)
        kb.dma(pc[:], cx.rw_pc, writes=[pc])
        la2 = kb.sb([128, 256], name="rw_la2", stack=st)
        kb.dma(la2[:], cx.rw_la2, writes=[la2])
        g2 = kb.sb([128, 256], name="rw_g2", stack=st)
        kb.dma(g2[:], cx.rw_g2, writes=[g2])
        eps12 = kb.sb([128, 1], name="rw_eps12", stack=st)
        hTs = [kb.sb([128, 8, 512], name="rw_h%d" % i, stack=st) for i in range(2)]
        raw = [kb.sb([128, 513], name="rw_raw%d" % c, stack=st) for c in range(8)]
        xs = [kb.sb([128, 512], name="rw_xs%d" % c, stack=st) for c in range(8)]
        for c in range(8):
            kb.op("dve", lambda e: e.memset(raw[c][:, 0:1], 0.0), writes=[raw[c]])
        tmp = [kb.sb([128, 512], name="rw_tmp%d" % i, stack=st) for i in range(8)]
        sbs = [kb.sb([128, 512], name="rw_s%d" % i, stack=st) for i in range(5)]
        gj = kb.sb([128, 512], name="rw_gj", stack=st)
        tok = [kb.sb([128, 512], BF16, name="rw_tok%d" % i, stack=st) for i in range(2)]
        tokf = [kb.sb([128, 512], name="rw_tokf%d" % i, stack=st) for i in range(2)]
        pp = cx.psum
        ntok = 0
        for i in range(T // 512):
            hT = hTs[i % 2]
            kb.dma(hT[:], hT_dram.t[:, :, i * 512:(i + 1) * 512].rearrange("k p t -> p k t"), writes=[hT])
            for c in range(8):
                ps = pp[c % 2]
                for k in range(8):
                    kb.op("pe", lambda e: e.matmul(ps[:, :], lhsT=W[:, k, c * 128:(c + 1) * 128], rhs=hT[:, k, :], start=(k == 0), stop=(k == 7)),
                          reads=[W, hT], writes=[ps])
                kb.op("act", lambda e: e.copy(out=raw[c][:, 1:513], in_=ps[:, :]), reads=[ps], writes=[raw[c]])
                kb.op("dve", lambda e: e.tensor_tensor(out=xs[c][:], in0=raw[c][:, 0:512], in1=raw[c][:, 1:513], op=ALU.subtract),
                      reads=[raw[c]], writes=[xs[c]])
                kb.op("dve", lambda e: e.scalar_tensor_tensor(out=xs[c][:], in0=xs[c][:], scalar=mu[:, c:c + 1], in1=raw[c][:, 1:513],
                                                              op0=ALU.mult, op1=ALU.add), reads=[xs[c], mu, raw[c]], writes=[xs[c]])
                kb.op("dve", lambda e: e.tensor_copy(out=raw[c][:, 0:1], in_=raw[c][:, 512:513]), reads=[raw[c]], writes=[raw[c]])
            xla, xgl = xs[6], xs[7]
            tanh_wl, sig_gl = tmp[0], tmp[1]
            kb.op("act", lambda e: e.activation(out=tanh_wl[0:64, :], in_=xla[0:64, :], func=AF.Tanh), reads=[xla], writes=[tanh_wl])
            kb.op("act", lambda e: e.activation(out=sig_gl[:], in_=xgl[:], func=AF.Sigmoid), reads=[xgl], writes=[sig_gl])
            for j in range(2):
                R, K, V = xs[0 + j], xs[2 + j], xs[4 + j]
                A_s, W_s, B_s, K_s = sbs[0], sbs[1], sbs[2], sbs[3]
                asig, kx, t1 = tmp[2], tmp[3], tmp[4]
                js = slice(j * 128, (j + 1) * 128)
                kb.op("pe", lambda e: e.matmul(pp[2][:, :], lhsT=la2[0:64, js], rhs=tanh_wl[0:64, :], start=True, stop=True),
                      reads=[la2, tanh_wl], writes=[pp[2]])
                kb.op("act", lambda e: e.activation(out=W_s[:], in_=pp[2][:, :], func=AF.Sigmoid, bias=pc[:, 0 + j:1 + j]),
                      reads=[pp[2], pc], writes=[W_s])
                kb.op("act", lambda e: e.activation(out=W_s[:], in_=W_s[:], func=AF.Exp, scale=-EM05), reads=[W_s], writes=[W_s])
                kb.op("pe", lambda e: e.matmul(pp[3][:, :], lhsT=la2[64:128, js], rhs=xla[64:128, :], start=True, stop=True),
                      reads=[la2, xla], writes=[pp[3]])
                kb.op("act", lambda e: e.activation(out=asig[:], in_=pp[3][:, :], func=AF.Sigmoid, bias=pc[:, 2 + j:3 + j]),
                      reads=[pp[3], pc], writes=[asig])
                kb.op("pe", lambda e: e.matmul(pp[4][:, :], lhsT=g2[:, js], rhs=sig_gl[:], start=True, stop=True),
                      reads=[g2, sig_gl], writes=[pp[4]])
                kb.op("act", lambda e: e.copy(out=gj[:], in_=pp[4][:, :]), reads=[pp[4]], writes=[gj])
                kb.op("dve", lambda e: e.tensor_scalar(out=kx[:], in0=K[:], scalar1=pc[:, 4 + j:5 + j], scalar2=None, op0=ALU.mult),
                      reads=[K, pc], writes=[kx])
                kb.op("act", lambda e: e.activation(out=t1[:], in_=kx[:], func=AF.Square), reads=[kx], writes=[t1])
                kb.op("pe", lambda e: e.matmul(pp[5][:, :], lhsT=cx.bones[:], rhs=t1[:], start=True, stop=True),
                      reads=[cx.bones, t1], writes=[pp[5]])
                kb.op("dve", lambda e: e.tensor_scalar(out=t1[:], in0=pp[5][:, :], scalar1=1e-12, scalar2=None, op0=ALU.max),
                      reads=[pp[5]], writes=[t1])
                kb.op("act", lambda e: e.activation(out=t1[:], in_=t1[:], func=AF.Sqrt), reads=[t1], writes=[t1])
                kb.op("dve", lambda e: e.reciprocal(out=t1[:], in_=t1[:]), reads=[t1], writes=[t1])
                kb.op("dve", lambda e: e.tensor_tensor(out=kx[:], in0=kx[:], in1=t1[:], op=ALU.mult), reads=[kx, t1], writes=[kx])
                kb.op("dve", lambda e: e.tensor_scalar(out=A_s[:], in0=kx[:], scalar1=-1.0, scalar2=None, op0=ALU.mult),
                      reads=[kx], writes=[A_s])
                kb.op("dve", lambda e: e.tensor_tensor(out=B_s[:], in0=kx[:], in1=asig[:], op=ALU.mult), reads=[kx, asig], writes=[B_s])
                kb.op("dve", lambda e: e.tensor_scalar(out=t1[:], in0=asig[:], scalar1=-1.0, scalar2=pc[:, 6 + j:7 + j], op0=ALU.add, op1=ALU.mult),
                      reads=[asig, pc], writes=[t1])
                kb.op("dve", lambda e: e.scalar_tensor_tensor(out=K_s[:], in0=t1[:], scalar=1.0, in1=K[:], op0=ALU.add, op1=ALU.mult),
                      reads=[t1, K], writes=[K_s])
                for q, src in enumerate((R, K_s, V, gj)):
                    kb.dma(fm.t[q, j, :, i * 512:(i + 1) * 512], src[:], reads=[src])
                for q, src in enumerate((A_s, W_s, B_s, K_s, R)):
                    ps = pp[6 + (ntok % 2)]
                    tk = (tokf if q == 1 else tok)[ntok % 2]
                    ntok += 1
                    for s in range(4):
                        kb.op("pe", lambda e: e.transpose(ps[:, s * 128:(s + 1) * 128], src[:, s * 128:(s + 1) * 128], cx.ident[:]),
                              reads=[src, cx.ident], writes=[ps])
                    kb.op("act", lambda e: e.copy(out=tk[:], in_=ps[:, :]), reads=[ps], writes=[tk])
                    for hp in range(2):
                        dd = strm_w.t[hp] if q == 1 else strm.t[q, hp]
                        dst = dd[i * 512:(i + 1) * 512, j * 64:(j + 1) * 64].rearrange("(s p) k -> p s k", p=128)
                        srcap = tk[:].rearrange("p (s h k) -> p s h k", s=4, h=2)[:, :, hp, :]
                        kb.dma(dst, srcap, reads=[tk])
        kb.barrier()


def rwkv_phase2(kb, nc, cx, T, strm, fm, yc_out, CH=16, same=True, strm_w=None):
    with contextlib.ExitStack() as st:
        pc = kb.sb([128, 14], name="r2_pc", stack=st)
        kb.dma(pc[:], cx.rw_pc, writes=[pc])
        eps = kb.sb([128, 1], name="r2_eps", stack=st)
        kb.op("dve", lambda e: e.memset(eps[:], 64e-5), writes=[eps])
        S = kb.sb([128, 128], name="r2_S", stack=st)
        kb.op("dve", lambda e: e.memset(S[:], 0.0), writes=[S])
        T1 = kb.sb([128, 128], name="r2_T1", stack=st)
        T2 = kb.sb([128, 128], name="r2_T2", stack=st)
        T4 = kb.sb([128, 128], name="r2_T4", stack=st)
        T3 = [kb.sb([128, 128], name="r2_T3%d" % i, stack=st) for i in range(4)]
        sa = kb.sb([128, 2], name="r2_sa", stack=st)
        bcs = [[kb.sb([128, CH, 128], (F32 if q == 1 else BF16), name="r2_bc%d_%d" % (q, b), stack=st) for b in range(2)] for q in range(5)]
        fms = [[kb.sb([128, 2, 512], name="r2_fm%d_%d" % (q, b), stack=st) for b in range(2)] for q in range(4)]
        ys = [kb.sb([128, 2, 512], name="r2_y%d" % b, stack=st) for b in range(2)]
        wk = [kb.sb([128, 512], name="r2_wk%d" % b, stack=st) for b in range(3)]
        pp = cx.psum
        v3 = lambda ap: ap.rearrange("p (j k) -> p j k", j=2)
        old_same = kb.same
        for i in range(T // 512):
            fb = [fms[q][i % 2] for q in range(4)]
            for q in range(4):
                kb.dma(fb[q][:], fm.t[q, :, :, i * 512:(i + 1) * 512].rearrange("j p t -> p j t"), writes=[fb[q]])
            Rf, Kf, Vf, Gf = fb
            Y = ys[i % 2]
            kb.same = same
            for cch in range(512 // CH):
                t0 = i * 512 + cch * CH
                cb = [bcs[q][cch % 2] for q in range(5)]
                for q in range(5):
                    for hp in range(2):
                        kb.dma(cb[q][hp * 64:(hp + 1) * 64, :, :],

                               (strm_w.t[hp] if q == 1 else strm.t[q, hp])[t0:t0 + CH, :].unsqueeze(0).broadcast_to([64, CH, 128]),
                               writes=[cb[q]])
                Ab, Wb, Bb, Kb, Rb = cb
                for s in range(CH):
                    tt = cch * CH + s
                    t3 = T3[tt % 4]
                    for j2 in range(2):
                        kb.op("act", lambda e: e.activation(out=t3[:, j2 * 64:(j2 + 1) * 64], in_=Kb[:, s, j2 * 64:(j2 + 1) * 64], func=AF.Copy,
                                                            scale=Vf[:, j2, tt:tt + 1]), reads=[Kb, Vf], writes=[t3])
                    kb.op("dve", lambda e: e.tensor_tensor(out=T1[:], in0=S[:], in1=Ab[:, s, :], op=ALU.mult), reads=[S, Ab], writes=[T1])
                    kb.op("dve", lambda e: e.tensor_reduce(out=sa[:], in_=v3(T1[:]), axis=AX.X, op=ALU.add), reads=[T1], writes=[sa])
                    kb.op("dve", lambda e: e.tensor_tensor(out=S[:], in0=S[:], in1=Wb[:, s, :], op=ALU.mult), reads=[S, Wb], writes=[S])
                    kb.op("dve", lambda e: e.tensor_tensor(out=v3(T2[:]), in0=v3(Bb[:, s, :]), in1=sa[:].unsqueeze(2).broadcast_to([128, 2, 64]),
                                                           op=ALU.mult), reads=[Bb, sa], writes=[T2])
                    kb.op("dve", lambda e: e.tensor_tensor(out=S[:], in0=S[:], in1=T2[:], op=ALU.add), reads=[S, T2], writes=[S])
                    kb.op("dve", lambda e: e.tensor_tensor(out=S[:], in0=S[:], in1=t3[:], op=ALU.add), reads=[S, t3], writes=[S])
                    kb.op("dve", lambda e: e.tensor_tensor(out=T4[:], in0=S[:], in1=Rb[:, s, :], op=ALU.mult), reads=[S, Rb], writes=[T4])
                    kb.op("dve", lambda e: e.tensor_reduce(out=Y[:, :, tt], in_=v3(T4[:]), axis=AX.X, op=ALU.add), reads=[T4], writes=[Y])
            kb.same = old_same
            for j in range(2):
                yj = Y[:, j, :]
                a, b, c = wk
                kb.op("pe", lambda e: e.matmul(pp[0][:, :], lhsT=cx.bones[:], rhs=yj, start=True, stop=True), reads=[cx.bones, Y], writes=[pp[0]])
                kb.op("dve", lambda e: e.scalar_tensor_tensor(out=a[:], in0=pp[0][:, :], scalar=-1.0 / 64, in1=yj, op0=ALU.mult, op1=ALU.add),
                      reads=[pp[0], Y], writes=[a])
                kb.op("act", lambda e: e.activation(out=b[:], in_=a[:], func=AF.Square), reads=[a], writes=[b])
                kb.op("pe", lambda e: e.matmul(pp[1][:, :], lhsT=cx.bones[:], rhs=b[:], start=True, stop=True), reads=[cx.bones, b], writes=[pp[1]])
                kb.op("act", lambda e: e.activation(out=b[:], in_=pp[1][:, :], func=AF.Sqrt, scale=1.0 / 64, bias=eps[:]),
                      reads=[pp[1], eps], writes=[b])
                kb.op("dve", lambda e: e.reciprocal(out=b[:], in_=b[:]), reads=[b], writes=[b])
                kb.op("dve", lambda e: e.tensor_tensor(out=a[:], in0=a[:], in1=b[:], op=ALU.mult), reads=[a, b], writes=[a])
                kb.op("dve", lambda e: e.tensor_scalar(out=a[:], in0=a[:], scalar1=pc[:, 10 + j:11 + j], scalar2=pc[:, 12 + j:13 + j],
                                                       op0=ALU.mult, op1=ALU.add), reads=[a, pc], writes=[a])
                kb.op("dve", lambda e: e.scalar_tensor_tensor(out=b[:], in0=Rf[:, j, :], scalar=pc[:, 8 + j:9 + j], in1=Kf[:, j, :],
                                                              op0=ALU.mult, op1=ALU.mult), reads=[Rf, Kf, pc], writes=[b])
                kb.op("pe", lambda e: e.matmul(pp[2][:, :], lhsT=cx.bones[:], rhs=b[:], start=True, stop=True), reads=[cx.bones, b], writes=[pp[2]])
                kb.op("dve", lambda e: e.tensor_tensor(out=b[:], in0=pp[2][:, :], in1=Vf[:, j, :], op=ALU.mult), reads=[pp[2], Vf], writes=[b])
                kb.op("dve", lambda e: e.tensor_tensor(out=a[:], in0=a[:], in1=b[:], op=ALU.add), reads=[a, b], writes=[a])
                kb.op("dve", lambda e: e.tensor_tensor(out=c[:], in0=a[:], in1=Gf[:, j, :], op=ALU.mult), reads=[a, Gf], writes=[c])
                kb.dma(yc_out[j, :, i * 512:(i + 1) * 512], c[:], reads=[c], writes=[cx.OUT])
        kb.barrier()


def ssd_inputs(nc, cx):
    cx.ss_w = din(nc, "ss_w", [8, 128, 1024])
    cx.ss_pc = din(nc, "ss_pc", [128, 28])
    cx.ss_mk = din(nc, "ss_mk", [3, 128, 128])


def ssd_phase1(kb, nc, cx, T, hT_dram, fm, bc_fm):
    with contextlib.ExitStack() as st:
        W = kb.sb([128, 8, 1024], name="ss_W", stack=st)
        kb.dma(W[:], cx.ss_w.rearrange("k p c -> p k c"), writes=[W])
        pc = kb.sb([128, 28], name="ss_pc", stack=st)
        kb.dma(pc[:], cx.ss_pc, writes=[pc])
        nega = kb.sb([128, 2], name="ss_nega", stack=st)
        kb.op("act", lambda e: e.activation(out=nega[:], in_=pc[:, 22:24], func=AF.Exp), reads=[pc], writes=[nega])
        kb.op("dve", lambda e: e.tensor_scalar(out=nega[:], in0=nega[:], scalar1=-1.0, scalar2=None, op0=ALU.mult), reads=[nega], writes=[nega])
        hTs = [kb.sb([128, 8, 512], name="ss_h%d" % i, stack=st) for i in range(2)]
        raw = [kb.sb([128, 515], name="ss_raw%d" % c, stack=st) for c in range(4)]
        for c in range(4):
            kb.op("dve", lambda e: e.memset(raw[c][:, 0:3], 0.0), writes=[raw[c]])
        cv = [kb.sb([128, 512], name="ss_cv%d" % c, stack=st) for c in range(4)]
        sz = kb.sb([128, 512], name="ss_sz", stack=st)
        dt = kb.sb([128, 512], name="ss_dt", stack=st)
        dA = kb.sb([128, 512], name="ss_dA", stack=st)
        xdt = kb.sb([128, 512], name="ss_xdt", stack=st)
        pp = cx.psum
        for i in range(T // 512):
            hT = hTs[i % 2]
            ts = slice(i * 512, (i + 1) * 512)
            kb.dma(hT[:], hT_dram.t[:, :, ts].rearrange("k p t -> p k t"), writes=[hT])

            def proj(c, ps):
                for k in range(8):
                    kb.op("pe", lambda e: e.matmul(ps[:, :], lhsT=W[:, k, c * 128:(c + 1) * 128], rhs=hT[:, k, :], start=(k == 0), stop=(k == 7)),
                          reads=[W, hT], writes=[ps])
            for j in range(2):
                ps = pp[j]
                proj(j, ps)
                kb.op("act", lambda e: e.activation(out=sz[:], in_=ps[:, :], func=AF.Silu), reads=[ps], writes=[sz])
                kb.dma(fm.t[0, j, :, ts], sz[:], reads=[sz])
            for c in range(4):
                ps = pp[2 + c % 2]
                proj(2 + c, ps)
                r = raw[c]
                kb.op("act", lambda e: e.copy(out=r[:, 3:515], in_=ps[:, :]), reads=[ps], writes=[r])
                kb.op("dve", lambda e: e.tensor_scalar(out=cv[c][:], in0=r[:, 0:512], scalar1=pc[:, c * 4:c * 4 + 1], scalar2=None, op0=ALU.mult),
                      reads=[r, pc], writes=[cv[c]])
                for k in range(1, 4):
                    kb.op("dve", lambda e: e.scalar_tensor_tensor(out=cv[c][:], in0=r[:, k:k + 512], scalar=pc[:, c * 4 + k:c * 4 + k + 1], in1=cv[c][:],
                                                                  op0=ALU.mult, op1=ALU.add), reads=[r, pc, cv[c]], writes=[cv[c]])
                kb.op("act", lambda e: e.activation(out=cv[c][:], in_=cv[c][:], func=AF.Silu, bias=pc[:, 16 + c:17 + c]), reads=[cv[c], pc], writes=[cv[c]])
                kb.op("dve", lambda e: e.tensor_copy(out=r[:, 0:3], in_=r[:, 512:515]), reads=[r], writes=[r])
                if c >= 2:
                    kb.dma(bc_fm.t[c - 2, :, ts], cv[c][:], reads=[cv[c]])
            for j in range(2):
                ps = pp[4 + j]
                proj(6 + j, ps)
                kb.op("act", lambda e: e.activation(out=dt[:], in_=ps[:, :], func=AF.Exp, bias=pc[:, 20 + j:21 + j]), reads=[ps, pc], writes=[dt])
                kb.op("act", lambda e: e.activation(out=dt[:], in_=dt[:], func=AF.Ln, bias=1.0), reads=[dt], writes=[dt])
                kb.op("dve", lambda e: e.tensor_scalar(out=dA[:], in0=dt[:], scalar1=nega[:, j:j + 1], scalar2=None, op0=ALU.mult), reads=[dt, nega], writes=[dA])
                kb.op("dve", lambda e: e.tensor_tensor(out=xdt[:], in0=cv[j][:], in1=dt[:], op=ALU.mult), reads=[cv[j], dt], writes=[xdt])
                kb.dma(fm.t[1, j, :, ts], cv[j][:], reads=[cv[j]])
                kb.dma(fm.t[2, j, :, ts], xdt[:], reads=[xdt])
                kb.dma(fm.t[3, j, :, ts], dA[:], reads=[dA])
        kb.barrier()


def ssd_phase2(kb, nc, cx, T, fm, bc_fm, ya_out):
    with contextlib.ExitStack() as st:
        pc = kb.sb([128, 28], name="s2_pc", stack=st)
        kb.dma(pc[:], cx.ss_pc, writes=[pc])
        mk = kb.sb([128, 3, 128], name="s2_mk", stack=st)
        kb.dma(mk[:], cx.ss_mk.rearrange("m p c -> p m c"), writes=[mk])
        Uin, Lst, CNeg = mk[:, 0, :], mk[:, 1, :], mk[:, 2, :]
        H = kb.sb([128, 256], name="s2_H", stack=st)
        kb.op("dve", lambda e: e.memset(H[:], 0.0), writes=[H])
        fms = [[kb.sb([128, 2, 512], name="s2_fm%d_%d" % (q, b), stack=st) for b in range(2)] for q in range(4)]
        bcs = [[kb.sb([128, 512], name="s2_bc%d_%d" % (q, b), stack=st) for b in range(2)] for q in range(2)]
        ys = [kb.sb([128, 2, 512], name="s2_y%d" % b, stack=st) for b in range(2)]
        wk = [kb.sb([128, 512], name="s2_wk%d" % b, stack=st) for b in range(3)]
        rstd = kb.sb([128, 512], name="s2_rstd", stack=st)
        dA_tok = kb.sb([128, 4], name="s2_dAtok", stack=st)
        sc = kb.sb([128, 12], name="s2_sc", stack=st)
        lh = kb.sb([128, 4, 128], name="s2_lh", stack=st)
        dec4 = kb.sb([128, 4, 128], name="s2_dec4", stack=st)
        M4 = kb.sb([128, 4, 128], name="s2_M4", stack=st)
        xtok = kb.sb([128, 256], name="s2_xtok", stack=st)
        xdec = kb.sb([128, 256], name="s2_xdec", stack=st)
        Btok = kb.sb([128, 128], name="s2_Btok", stack=st)
        ytmp = kb.sb([128, 256], name="s2_ytmp", stack=st)
        ytok = kb.sb([128, 256], name="s2_ytok", stack=st)
        Ht = kb.sb([128, 256], name="s2_Ht", stack=st)
        pp = cx.psum
        e4 = lambda ap: ap.unsqueeze(2).broadcast_to([128, 4, 64])
        v4 = lambda ap: ap.rearrange("p (e q) -> p e q", e=4)
        for i in range(T // 512):
            ts = slice(i * 512, (i + 1) * 512)
            fb = [fms[q][i % 2] for q in range(4)]
            for q in range(4):
                kb.dma(fb[q][:], fm.t[q, :, :, ts].rearrange("j p t -> p j t"), writes=[fb[q]])
            SZ, XS, XDT, DA = fb
            BT, CT = bcs[0][i % 2], bcs[1][i % 2]
            kb.dma(BT[:], bc_fm.t[0, :, ts], writes=[BT])
            kb.dma(CT[:], bc_fm.t[1, :, ts], writes=[CT])
            Y = ys[i % 2]
            for c in range(4):
                cs = slice(c * 128, (c + 1) * 128)
                for j in range(2):
                    kb.op("pe", lambda e: e.transpose(pp[0][:, j * 128:(j + 1) * 128], XDT[:, j, cs], cx.ident[:]), reads=[XDT, cx.ident], writes=[pp[0]])
                kb.op("pe", lambda e: e.transpose(pp[0][:, 256:384], BT[:, cs], cx.ident[:]), reads=[BT, cx.ident], writes=[pp[0]])
                for j in range(2):
                    kb.op("pe", lambda e: e.transpose(pp[1][:, j * 128:(j + 1) * 128], DA[:, j, cs], cx.ident[:]), reads=[DA, cx.ident], writes=[pp[1]])
                kb.op("act", lambda e: e.copy(out=xtok[:], in_=pp[0][:, 0:256]), reads=[pp[0]], writes=[xtok])
                kb.op("act", lambda e: e.copy(out=Btok[:], in_=pp[0][:, 256:384]), reads=[pp[0]], writes=[Btok])
                kb.op("dve", lambda e: e.tensor_copy(out=dA_tok[:], in_=pp[1][:, 0:256:64]), reads=[pp[1]], writes=[dA_tok])
                kb.op("pe", lambda e: e.matmul(pp[2][:, 0:4], lhsT=Uin, rhs=dA_tok[:], start=True, stop=True), reads=[mk, dA_tok], writes=[pp[2]])
                kb.op("pe", lambda e: e.matmul(pp[2][:, 4:8], lhsT=Lst, rhs=dA_tok[:], start=True, stop=True), reads=[mk, dA_tok], writes=[pp[2]])
                kb.op("pe", lambda e: e.matmul(pp[2][:, 8:12], lhsT=cx.ones[:], rhs=dA_tok[:], start=True, stop=True), reads=[cx.ones, dA_tok], writes=[pp[2]])
                kb.op("act", lambda e: e.activation(out=sc[:], in_=pp[2][:, 0:12], func=AF.Exp), reads=[pp[2]], writes=[sc])
                for e_ in range(4):
                    kb.op("dve", lambda e: e.tensor_scalar(out=lh[:, e_, :], in0=Lst, scalar1=dA_tok[:, e_:e_ + 1], scalar2=None, op0=ALU.mult),
                          reads=[mk, dA_tok], writes=[lh])
                for e_ in range(4):
                    kb.op("pe", lambda e: e.matmul(pp[3][:, e_ * 128:(e_ + 1) * 128], lhsT=lh[:, e_, :], rhs=Uin, start=True, stop=False), reads=[lh, mk], writes=[pp[3]])
                    kb.op("pe", lambda e: e.matmul(pp[3][:, e_ * 128:(e_ + 1) * 128], lhsT=cx.ident[:], rhs=CNeg, start=False, stop=True), reads=[cx.ident, mk], writes=[pp[3]])
                kb.op("act", lambda e: e.activation(out=dec4[:], in_=pp[3][:, :].rearrange("p (e l) -> p e l", e=4), func=AF.Exp), reads=[pp[3]], writes=[dec4])
                kb.op("pe", lambda e: e.matmul(pp[4][:, 0:128], lhsT=BT[:, cs], rhs=CT[:, cs], start=True, stop=True), reads=[BT, CT], writes=[pp[4]])
                kb.op("dve", lambda e: e.tensor_tensor(out=M4[:], in0=dec4[:], in1=pp[4][:, 0:128].unsqueeze(1).broadcast_to([128, 4, 128]), op=ALU.mult),
                      reads=[dec4, pp[4]], writes=[M4])
                for e_ in range(4):
                    kb.op("pe", lambda e: e.matmul(pp[5][:, e_ * 64:(e_ + 1) * 64], lhsT=M4[:, e_, :], rhs=xtok[:, e_ * 64:(e_ + 1) * 64], start=True, stop=True),
                          reads=[M4, xtok], writes=[pp[5]])
                kb.op("pe", lambda e: e.matmul(pp[6][:, 0:256], lhsT=CT[:, cs], rhs=H[:], start=True, stop=True), reads=[CT, H], writes=[pp[6]])
                kb.op("dve", lambda e: e.tensor_tensor(out=v4(ytmp[:]), in0=v4(pp[6][:, 0:256]), in1=e4(sc[:, 0:4]), op=ALU.mult), reads=[pp[6], sc], writes=[ytmp])
                kb.op("dve", lambda e: e.tensor_tensor(out=ytok[:], in0=ytmp[:], in1=pp[5][:, 0:256], op=ALU.add), reads=[ytmp, pp[5]], writes=[ytok])
                kb.op("dve", lambda e: e.tensor_tensor(out=v4(xdec[:]), in0=v4(xtok[:]), in1=e4(sc[:, 4:8]), op=ALU.mult), reads=[xtok, sc], writes=[xdec])
                kb.op("pe", lambda e: e.matmul(pp[7][:, 0:256], lhsT=Btok[:], rhs=xdec[:], start=True, stop=True), reads=[Btok, xdec], writes=[pp[7]])
                kb.op("dve", lambda e: e.tensor_tensor(out=v4(Ht[:]), in0=v4(H[:]), in1=e4(sc[:, 8:12]), op=ALU.mult), reads=[H, sc], writes=[Ht])
                kb.op("dve", lambda e: e.tensor_tensor(out=H[:], in0=Ht[:], in1=pp[7][:, 0:256], op=ALU.add), reads=[Ht, pp[7]], writes=[H])
                for j in range(2):
                    kb.op("pe", lambda e: e.transpose(pp[2][:, 128 + j * 128:256 + j * 128], ytok[:, j * 128:(j + 1) * 128], cx.ident[:]), reads=[ytok, cx.ident], writes=[pp[2]])
                kb.op("act", lambda e: e.copy(out=Y[:, :, cs], in_=pp[2][:, 128:384].rearrange("p (j l) -> p j l", j=2)), reads=[pp[2]], writes=[Y])
            a, b, c_ = wk
            for j in range(2):
                yj = Y[:, j, :]
                kb.op("dve", lambda e: e.scalar_tensor_tensor(out=yj, in0=XS[:, j, :], scalar=pc[:, 24 + j:25 + j], in1=yj, op0=ALU.mult, op1=ALU.add),
                      reads=[XS, pc, Y], writes=[Y])
                kb.op("dve", lambda e: e.tensor_tensor(out=yj, in0=yj, in1=SZ[:, j, :], op=ALU.mult), reads=[Y, SZ], writes=[Y])
                kb.op("act", lambda e: e.activation(out=a[:], in_=yj, func=AF.Square), reads=[Y], writes=[a])
                kb.op("pe", lambda e: e.matmul(pp[0][:, :], lhsT=cx.ones[:], rhs=a[:], start=(j == 0), stop=(j == 1)), reads=[cx.ones, a], writes=[pp[0]])
            kb.op("act", lambda e: e.activation(out=rstd[:], in_=pp[0][:, :], func=AF.Sqrt, scale=1.0 / 256, bias=cx.eps6[:]),
                  reads=[pp[0], cx.eps6], writes=[rstd])
            kb.op("dve", lambda e: e.reciprocal(out=rstd[:], in_=rstd[:]), reads=[rstd], writes=[rstd])
            for j in range(2):
                o = (b, c_)[j]
                kb.op("dve", lambda e: e.scalar_tensor_tensor(out=o[:], in0=Y[:, j, :], scalar=pc[:, 26 + j:27 + j], in1=rstd[:], op0=ALU.mult, op1=ALU.mult),
                      reads=[Y, pc, rstd], writes=[o])
                kb.dma(ya_out[j, :, ts], o[:], reads=[o], writes=[cx.OUT])
        kb.barrier()


NEGM = -30000.0


def moba_inputs(nc, cx):
    cx.mb_wqk = din(nc, "mb_wqk", [8, 128, 512])
    cx.mb_wv = din(nc, "mb_wv", [8, 128, 256])
    cx.mb_g = din(nc, "mb_g", [128, 2])
    cx.mb_rel = din(nc, "mb_rel", [1, 64])
    cx.mb_bi = din(nc, "mb_bi", [128, 1280])


def moba_phase1(kb, nc, cx, T, hT_dram, qT_d, kT_d, v_d, nm_d):
    with contextlib.ExitStack() as st:
        Wqk = kb.sb([128, 8, 512], name="mb_Wqk", stack=st)
        kb.dma(Wqk[:], cx.mb_wqk.rearrange("k p c -> p k c"), writes=[Wqk])
        Wv = kb.sb([128, 8, 256], name="mb_Wv", stack=st)
        kb.dma(Wv[:], cx.mb_wv.rearrange("k p c -> p k c"), writes=[Wv])
        gg = kb.sb([128, 2], name="mb_g", stack=st)
        kb.dma(gg[:], cx.mb_g, writes=[gg])
        hTs = [kb.sb([128, 8, 512], name="mb_h%d" % i, stack=st) for i in range(2)]
        kmT = [kb.sb([128, 64], name="mb_km%d" % h, stack=st) for h in range(2)]
        for h in range(2):
            kb.op("dve", lambda e: e.memset(kmT[h][:], 0.0), writes=[kmT[h]])
        rawt = kb.sb([128, 512], name="mb_raw", stack=st)
        sq = kb.sb([128, 512], name="mb_sq", stack=st)
        rstd = kb.sb([128, 512], name="mb_rstd", stack=st)
        QT = kb.sb([128, 512], name="mb_QT", stack=st)
        KT = kb.sb([128, 512], name="mb_KT", stack=st)
        QS = kb.sb([128, 512], BF16, name="mb_QS", stack=st)
        KTb = kb.sb([128, 512], BF16, name="mb_KTb", stack=st)
        gt = kb.sb([128, 64], name="mb_gt", stack=st)
        sel = kb.sb([128, 64], name="mb_sel", stack=st)
        mx = kb.sb([128, 8], name="mb_mx", stack=st)
        nmT = kb.sb([64, 512], BF16, name="mb_nmT", stack=st)
        va = [kb.sb([128, 4, 2, 129], BF16, name="mb_va%d" % i, stack=st) for i in range(2)]
        for i in range(2):
            kb.op("dve", lambda e: e.memset(va[i][:], 1.0), writes=[va[i]])
        pp = cx.psum
        for i in range(T // 512):
            hT = hTs[i % 2]
            ts = slice(i * 512, (i + 1) * 512)
            kb.dma(hT[:], hT_dram.t[:, :, ts].rearrange("k p t -> p k t"), writes=[hT])

            def normed(c, gcol, dst):
                ps = pp[c % 2]
                for k in range(8):
                    kb.op("pe", lambda e: e.matmul(ps[:, :], lhsT=Wqk[:, k, c * 128:(c + 1) * 128], rhs=hT[:, k, :], start=(k == 0), stop=(k == 7)),
                          reads=[Wqk, hT], writes=[ps])
                kb.op("act", lambda e: e.copy(out=rawt[:], in_=ps[:, :]), reads=[ps], writes=[rawt])
                kb.op("act", lambda e: e.activation(out=sq[:], in_=rawt[:], func=AF.Square), reads=[rawt], writes=[sq])
                kb.op("pe", lambda e: e.matmul(pp[2][:, :], lhsT=cx.ones[:], rhs=sq[:], start=True, stop=True), reads=[cx.ones, sq], writes=[pp[2]])
                kb.op("act", lambda e: e.activation(out=rstd[:], in_=pp[2][:, :], func=AF.Sqrt, scale=1.0 / 128, bias=cx.eps6[:]),
                      reads=[pp[2], cx.eps6], writes=[rstd])
                kb.op("dve", lambda e: e.reciprocal(out=rstd[:], in_=rstd[:]), reads=[rstd], writes=[rstd])
                kb.op("dve", lambda e: e.scalar_tensor_tensor(out=dst[:], in0=rawt[:], scalar=gg[:, gcol:gcol + 1], in1=rstd[:], op0=ALU.mult, op1=ALU.mult),
                      reads=[rawt, gg, rstd], writes=[dst])
            for hh in range(2):
                normed(2 + hh, 1, KT)
                kb.op("act", lambda e: e.copy(out=KTb[:], in_=KT[:]), reads=[KT], writes=[KTb])
                kb.dma(kT_d.t[hh, :, ts], KTb[:], reads=[KTb])
                kb.op("dve", lambda e: e.tensor_reduce(out=kmT[hh][:, 2 * i:2 * i + 2], in_=KT[:].rearrange("p (b t) -> p b t", b=2), axis=AX.X, op=ALU.add),
                      reads=[KT], writes=[kmT[hh]])
                normed(hh, 0, QT)
                kb.op("dve", lambda e: e.tensor_scalar(out=QS[:], in0=QT[:], scalar1=128.0 ** -0.5, scalar2=None, op0=ALU.mult), reads=[QT], writes=[QS])
                kb.dma(qT_d.t[hh, :, ts], QS[:], reads=[QS])
                for s in range(4):
                    qb = 2 * i + s // 2
                    if qb == 0:
                        kb.op("dve", lambda e: e.memset(sel[:], 0.0), writes=[sel])
                    else:
                        kb.op("pe", lambda e: e.matmul(pp[3][:, 0:64], lhsT=QT[:, s * 128:(s + 1) * 128], rhs=kmT[hh][:, 0:64], start=True, stop=True),
                              reads=[QT, kmT[hh]], writes=[pp[3]])
                        kb.op("dve", lambda e: e.memset(gt[:], -1e30), writes=[gt])
                        kb.op("dve", lambda e: e.tensor_copy(out=gt[:, 0:qb], in_=pp[3][:, 0:qb]), reads=[pp[3]], writes=[gt])
                        if qb >= 3:
                            kb.op("dve", lambda e: e.max(out=mx[:], in_=gt[:]), reads=[gt], writes=[mx])
                            kb.op("dve", lambda e: e.tensor_scalar(out=sel[:], in0=gt[:], scalar1=mx[:, 2:3], scalar2=None, op0=ALU.is_ge),
                                  reads=[gt, mx], writes=[sel])
                        else:
                            kb.op("dve", lambda e: e.tensor_scalar(out=sel[:], in0=gt[:], scalar1=-1e29, scalar2=None, op0=ALU.is_ge),
                                  reads=[gt], writes=[sel])
                    kb.op("dve", lambda e: e.tensor_scalar(out=sel[:], in0=sel[:], scalar1=-1.0, scalar2=-NEGM, op0=ALU.add, op1=ALU.mult),
                          reads=[sel], writes=[sel])
                    kb.op("pe", lambda e: e.transpose(pp[4][0:64, 0:128], sel[:, 0:64], cx.ident[:]), reads=[sel, cx.ident], writes=[pp[4]])
                    kb.op("act", lambda e: e.copy(out=nmT[0:64, s * 128:(s + 1) * 128], in_=pp[4][0:64, 0:128]), reads=[pp[4]], writes=[nmT])
                kb.dma(nm_d.t[hh, :, ts], nmT[0:64, :], reads=[nmT])
            vt = va[i % 2]
            for s in range(4):
                ps = pp[5 + s % 2]
                for k in range(8):
                    kb.op("pe", lambda e: e.matmul(ps[:, 0:256], lhsT=hT[:, k, s * 128:(s + 1) * 128], rhs=Wv[:, k, :], start=(k == 0), stop=(k == 7)),
                          reads=[hT, Wv], writes=[ps])
                kb.op("act", lambda e: e.copy(out=vt[:, s, :, 0:128], in_=ps[:, 0:256].rearrange("p (h d) -> p h d", h=2)), reads=[ps], writes=[vt])
            for hh in range(2):
                kb.dma(v_d.t[hh, ts, :].rearrange("(s p) c -> p s c", p=128), vt[:, :, hh, :], reads=[vt])
        kb.barrier()


def moba_phase2(kb, nc, cx, T, qT_d, kT_d, v_d, nm_d, yb_out):
    NT = T // 128
    NG = T // 256
    with contextlib.ExitStack() as st:
        KT = kb.sb([128, T], BF16, name="m2_KT", stack=st)
        VA = kb.sb([128, NT, 129], BF16, name="m2_VA", stack=st)
        En = kb.sb([64, 64, 128], BF16, name="m2_En", stack=st)
        identb = kb.sb([128, 128], BF16, name="m2_identb", stack=st)
        kb.op("dve", lambda e: e.tensor_copy(out=identb[:], in_=cx.ident[:]), reads=[cx.ident], writes=[identb])
        kb.op("dve", lambda e: e.tensor_copy(out=En[:], in_=cx.ident[0:64, 0:64].unsqueeze(2).broadcast_to([64, 64, 128])), reads=[cx.ident], writes=[En])
        BI = kb.sb([128, 1280], name="m2_BI", stack=st)
        kb.dma(BI[:], cx.mb_bi, writes=[BI])
        rel = kb.sb([128, 64], name="m2_rel", stack=st)
        kb.dma(rel[:], cx.mb_rel.broadcast_to([128, 64]), writes=[rel])
        BB = kb.sb([128, 1280], name="m2_BB", stack=st)
        BBb = kb.sb([128, 1280], BF16, name="m2_BBb", stack=st)
        tmpb = kb.sb([128, 1280], name="m2_tmpb", stack=st)
        QG = [kb.sb([128, 256], BF16, name="m2_QG%d" % i, stack=st) for i in range(2)]
        NM = [kb.sb([64, 256], BF16, name="m2_NM%d" % i, stack=st) for i in range(2)]
        PT = [kb.sb([128, 256], BF16, name="m2_PT%d" % i, stack=st) for i in range(4)]
        yt = [kb.sb([128, 2, 128], name="m2_yt%d" % i, stack=st) for i in range(2)]
        rinv = kb.sb([128, 1], name="m2_rinv", stack=st)
        pp = cx.psum
        npt = 0
        for hh in range(2):
            kb.dma(KT[:], kT_d.t[hh, :, :], writes=[KT])
            kb.dma(VA[:], v_d.t[hh, :, :].rearrange("(n p) c -> p n c", p=128), writes=[VA])
            kb.op("dve", lambda e: e.tensor_scalar(out=BB[:], in0=BI[:], scalar1=-1.0, scalar2=NEGM, op0=ALU.is_equal, op1=ALU.mult), reads=[BI], writes=[BB])
            for b in range(32):
                kb.op("dve", lambda e: e.tensor_scalar(out=tmpb[:], in0=BI[:], scalar1=float(b), scalar2=rel[:, hh * 32 + b:hh * 32 + b + 1],
                                                       op0=ALU.is_equal, op1=ALU.mult), reads=[BI, rel], writes=[tmpb])
                kb.op("dve", lambda e: e.tensor_tensor(out=BB[:], in0=BB[:], in1=tmpb[:], op=ALU.add), reads=[BB, tmpb], writes=[BB])
            kb.op("act", lambda e: e.copy(out=BBb[:], in_=BB[:]), reads=[BB], writes=[BBb])
            for g in range(NG):
                qg, nm = QG[g % 2], NM[g % 2]
                gs = slice(g * 256, (g + 1) * 256)
                kb.dma(qg[:], qT_d.t[hh, :, gs], writes=[qg])
                kb.dma(nm[:], nm_d.t[hh, :, gs], writes=[nm])
                O = [pp[(g % 2) * 2 + 0], pp[(g % 2) * 2 + 1]]
                for kt in range(2 * g + 2):
                    n = kt // 2
                    past = n < g
                    if kt == 2 * g + 1:
                        q0, w, b0 = 128, 128, 0
                    else:
                        q0, w, b0 = 0, 256, min(2 * g - kt, 8) * 128
                    S = pp[4 + npt % 4]
                    pt = PT[npt % 4]
                    npt += 1
                    kb.op("pe", lambda e: e.matmul(S[:, 0:w], lhsT=KT[:, kt * 128:(kt + 1) * 128], rhs=qg[:, q0:q0 + w], start=True, stop=False),
                          reads=[KT, qg], writes=[S])
                    kb.op("pe", lambda e: e.matmul(S[:, 0:w], lhsT=identb[:], rhs=BBb[:, b0:b0 + w], start=False, stop=(not past)),
                          reads=[identb, BBb], writes=[S])
                    if past:
                        kb.op("pe", lambda e: e.matmul(S[:, 0:w], lhsT=En[0:64, n, :], rhs=nm[0:64, q0:q0 + w], start=False, stop=True),
                              reads=[En, nm], writes=[S])
                    kb.op("act", lambda e: e.activation(out=pt[:, 0:w], in_=S[:, 0:w], func=AF.Exp), reads=[S], writes=[pt])
                    for qq in range(2):
                        if kt == 2 * g + 1 and qq == 0:
                            continue
                        c0 = qq * 128 - q0
                        last = (2 * g) if qq == 0 else (2 * g + 1)
                        kb.op("pe", lambda e: e.matmul(O[qq][:, 0:129], lhsT=pt[:, c0:c0 + 128], rhs=VA[:, kt, :], start=(kt == 0), stop=(kt == last)),
                              reads=[pt, VA], writes=[O[qq]])
                y = yt[g % 2]
                for qq in range(2):
                    kb.op("dve", lambda e: e.reciprocal(out=rinv[:], in_=O[qq][:, 128:129]), reads=[O[qq]], writes=[rinv])
                    kb.op("dve", lambda e: e.tensor_scalar(out=y[:, qq, :], in0=O[qq][:, 0:128], scalar1=rinv[:, 0:1], scalar2=None, op0=ALU.mult),
                          reads=[O[qq], rinv], writes=[y])
                kb.dma(yb_out[hh, gs, :].rearrange("(q p) d -> p q d", p=128), y[:], reads=[y], writes=[cx.OUT])
        kb.barrier()


def lb_inputs(nc, cx, TB):
    cx.xT = din(nc, "xT", [8, 128, TB]); cx.xtok = din(nc, "xtok", [TB, 1024])
    cx.yT = din(nc, "yT", [24, 128, TB])
    cx.g1 = din(nc, "g1", [128, 8]); cx.g2 = din(nc, "g2", [128, 8])
    cx.wg = din(nc, "wg", [8, 128, 3072]); cx.bg = din(nc, "bg", [128, 24])
    cx.pall = din(nc, "pall", [24, 128, 1024])
    cx.wout = din(nc, "wout", [8, 128, 1024])
    cx.wq = din(nc, "wq", [8, 128, 2048])
    cx.k12T = din(nc, "k12T", [128, 256])
    cx.uT = din(nc, "uT", [32, 128, 8, 512])
    cx.vv = din(nc, "vv", [32, 128, 4, 1024])


def lb_gate_merge(kb, nc, cx, TB, hT_d, mT_d):
    with contextlib.ExitStack() as st:
        bg = kb.sb([128, 24], name="b1_bg", stack=st)
        kb.dma(bg[:], cx.bg, writes=[bg])
        Wg = [kb.sb([128, 8, 3, 128], name="b1_wg%d" % i, stack=st) for i in range(2)]
        Pc = [kb.sb([128, 24, 128], name="b1_pc%d" % i, stack=st) for i in range(2)]
        hTs = [kb.sb([128, 8, 512], name="b1_h%d" % i, stack=st) for i in range(2)]
        yTs = [kb.sb([128, 24, 512], name="b1_y%d" % i, stack=st) for i in range(2)]
        gs = [kb.sb([128, 512], name="b1_g%d" % i, stack=st) for i in range(3)]
        m = [kb.sb([128, 512], name="b1_m%d" % i, stack=st) for i in range(2)]
        t = kb.sb([128, 512], name="b1_t", stack=st)
        pp = cx.psum
        it = 0
        for c in range(8):
            wg, pc = Wg[c % 2], Pc[c % 2]
            for br in range(3):
                kb.dma(wg[:, :, br, :], cx.wg[:, :, br * 1024 + c * 128:br * 1024 + (c + 1) * 128].rearrange("k p c -> p k c"), writes=[wg])
            kb.dma(pc[:], cx.pall[:, :, c * 128:(c + 1) * 128].rearrange("k p c -> p k c"), writes=[pc])
            for ti in range(TB // 512):
                ts = slice(ti * 512, (ti + 1) * 512)
                hT, yT = hTs[it % 2], yTs[it % 2]
                mm = m[it % 2]
                it += 1
                kb.dma(hT[:], hT_d.t[:, :, ts].rearrange("k p t -> p k t"), writes=[hT])
                kb.dma(yT[:], cx.yT[:, :, ts].rearrange("k p t -> p k t"), writes=[yT])
                for br in range(3):
                    pg, pj = pp[br], pp[3 + br]
                    for k in range(8):
                        kb.op("pe", lambda e: e.matmul(pg[:, :], lhsT=wg[:, k, br, :], rhs=hT[:, k, :], start=(k == 0), stop=(k == 7)), reads=[wg, hT], writes=[pg])
                    kb.op("act", lambda e: e.activation(out=gs[br][:], in_=pg[:, :], func=AF.Sigmoid, bias=bg[:, br * 8 + c:br * 8 + c + 1]),
                          reads=[pg, bg], writes=[gs[br]])
                    for k in range(8):
                        kb.op("pe", lambda e: e.matmul(pj[:, :], lhsT=pc[:, br * 8 + k, :], rhs=yT[:, br * 8 + k, :], start=(k == 0), stop=(k == 7)), reads=[pc, yT], writes=[pj])
                    if br == 0:
                        kb.op("dve", lambda e: e.tensor_tensor(out=mm[:], in0=gs[br][:], in1=pj[:, :], op=ALU.mult), reads=[gs[br], pj], writes=[mm])
                    else:
                        kb.op("dve", lambda e: e.tensor_tensor(out=t[:], in0=gs[br][:], in1=pj[:, :], op=ALU.mult), reads=[gs[br], pj], writes=[t])
                        kb.op("dve", lambda e: e.tensor_tensor(out=mm[:], in0=mm[:], in1=t[:], op=ALU.add), reads=[mm, t], writes=[mm])
                kb.dma(mT_d.t[c, :, ts], mm[:], reads=[mm])
        kb.barrier()


def lb_outproj(kb, nc, cx, TB, mT_d, xnT_d, xn_tok_d):
    with contextlib.ExitStack() as st:
        Wo = kb.sb([128, 8, 1024], name="b2_wo", stack=st)
        kb.dma(Wo[:], cx.wout.rearrange("k p c -> p k c"), writes=[Wo])
        mTs = [kb.sb([128, 8, 512], name="b2_m%d" % i, stack=st) for i in range(2)]
        xTs = [kb.sb([128, 8, 512], name="b2_x%d" % i, stack=st) for i in range(2)]
        xn = [kb.sb([128, 8, 512], name="b2_xn%d" % i, stack=st) for i in range(2)]
        xt = [kb.sb([128, 1024], name="b2_xt%d" % i, stack=st) for i in range(2)]
        pp = cx.psum
        nx = 0
        for ti in range(TB // 512):
            ts = slice(ti * 512, (ti + 1) * 512)
            mT, xT, xo = mTs[ti % 2], xTs[ti % 2], xn[ti % 2]
            kb.dma(mT[:], mT_d.t[:, :, ts].rearrange("k p t -> p k t"), writes=[mT])
            kb.dma(xT[:], cx.xT[:, :, ts].rearrange("k p t -> p k t"), writes=[xT])
            for c in range(8):
                ps = pp[c % 2]
                for k in range(8):
                    kb.op("pe", lambda e: e.matmul(ps[:, :], lhsT=Wo[:, k, c * 128:(c + 1) * 128], rhs=mT[:, k, :], start=(k == 0), stop=(k == 7)), reads=[Wo, mT], writes=[ps])
                kb.op("dve", lambda e: e.tensor_tensor(out=xo[:, c, :], in0=ps[:, :], in1=xT[:, c, :], op=ALU.add), reads=[ps, xT], writes=[xo])
            kb.dma(xnT_d.t[:, :, ts].rearrange("k p t -> p k t"), xo[:], reads=[xo])
            for s in range(4):
                xk = xt[nx % 2]
                nx += 1
                r0 = ti * 512 + s * 128
                kb.dma(xk[:], cx.xtok[r0:r0 + 128, :], writes=[xk])
                for hf in range(2):
                    ps = pp[2 + hf]
                    for k in range(8):
                        kb.op("pe", lambda e: e.matmul(ps[:, :], lhsT=mT[:, k, s * 128:(s + 1) * 128], rhs=Wo[:, k, hf * 512:(hf + 1) * 512], start=(k == 0), stop=(k == 7)),
                              reads=[Wo, mT], writes=[ps])
                    kb.op("dve", lambda e: e.tensor_tensor(out=xk[:, hf * 512:(hf + 1) * 512], in0=ps[:, :], in1=xk[:, hf * 512:(hf + 1) * 512], op=ALU.add),
                          reads=[ps, xk], writes=[xk])
                kb.dma(xn_tok_d.t[r0:r0 + 128, :], xk[:], reads=[xk])
        kb.barrier()


def lb_peer_scores(kb, nc, cx, TB, h2T_d, s_d):
    with contextlib.ExitStack() as st:
        Wq = kb.sb([128, 8, 2048], name="b3_wq", stack=st)
        kb.dma(Wq[:], cx.wq.rearrange("k p c -> p k c"), writes=[Wq])
        k12 = kb.sb([128, 256], name="b3_k12", stack=st)
        kb.dma(k12[:], cx.k12T, writes=[k12])
        hTs = [kb.sb([128, 8, 512], name="b3_h%d" % i, stack=st) for i in range(2)]
        qT = [kb.sb([128, 512], name="b3_q%d" % i, stack=st) for i in range(2)]
        sc = [kb.sb([128, 4, 128], name="b3_s%d" % i, stack=st) for i in range(2)]
        pp = cx.psum
        for ti in range(TB // 512):
            ts = slice(ti * 512, (ti + 1) * 512)
            hT = hTs[ti % 2]
            kb.dma(hT[:], h2T_d.t[:, :, ts].rearrange("k p t -> p k t"), writes=[hT])
            for ct in range(16):
                ps = pp[ct % 2]
                q = qT[ct % 2]
                so = sc[ct % 2]
                for k in range(8):
                    kb.op("pe", lambda e: e.matmul(ps[:, :], lhsT=Wq[:, k, ct * 128:(ct + 1) * 128], rhs=hT[:, k, :], start=(k == 0), stop=(k == 7)), reads=[Wq, hT], writes=[ps])
                kb.op("act", lambda e: e.copy(out=q[:], in_=ps[:, :]), reads=[ps], writes=[q])
                p2 = pp[2 + ct % 2]
                for s in range(4):
                    kb.op("pe", lambda e: e.matmul(p2[:, s * 128:(s + 1) * 128], lhsT=q[:, s * 128:(s + 1) * 128], rhs=k12[:, (ct % 2) * 128:(ct % 2) * 128 + 128], start=True, stop=True),
                          reads=[q, k12], writes=[p2])
                kb.op("dve", lambda e: e.tensor_copy(out=so[:], in_=p2[:, :].rearrange("p (s n) -> p s n", s=4)), reads=[p2], writes=[so])
                kb.dma(s_d.t[ct % 2, ts, ct // 2, :].rearrange("(s p) n -> p s n", p=128), so[:], reads=[so])
        kb.barrier()


def build_lb(nc, kb, cx, TB):
    setup_consts(kb, nc, cx)
    lb_inputs(nc, cx, TB)
    out_d = dout(nc, "xout", [TB, 1024])
    hT_d = kb.dram("hT_s", [8, 128, TB]); mT_d = kb.dram("mT_s", [8, 128, TB])
    xnT_d = kb.dram("xnT_s", [8, 128, TB]); xn_tok_d = kb.dram("xntok_s", [TB, 1024])
    h2T_d = kb.dram("h2T_s", [8, 128, TB]); s_d = kb.dram("s_s", [2, TB, 8, 128])
    phase_norm(kb, nc, cx, TB, cx.xT, cx.g1, hT_d)
    lb_gate_merge(kb, nc, cx, TB, hT_d, mT_d)
    lb_outproj(kb, nc, cx, TB, mT_d, xnT_d, xn_tok_d)
    phase_norm(kb, nc, cx, TB, xnT_d.t, cx.g2, h2T_d)
    lb_peer_scores(kb, nc, cx, TB, h2T_d, s_d)
    lb_peer_main2(kb, nc, cx, TB, h2T_d, s_d, xn_tok_d, out_d)
    kb.finish()


def lb_peer_main2(kb, nc, cx, TB, h2T_d, s_d, xn_tok_d, out_d, NQ=8, NT=2):
    QI = 128 // NQ
    QE = QI * 128
    NCH = QE // 512
    TP = NT * 128
    with contextlib.ExitStack() as st:
        identb = kb.sb([128, 128], BF16, name="b5_identb", stack=st)
        kb.op("dve", lambda e: e.tensor_copy(out=identb[:], in_=cx.ident[:]), reads=[cx.ident], writes=[identb])
        hf = kb.sb([128, 8, TP], name="b5_hf", stack=st)
        hb = kb.sb([128, 8, TP], BF16, name="b5_hb", stack=st)
        s1 = kb.sb([128, NT, 8, 128], name="b5_s1", stack=st)
        s2 = kb.sb([128, NT, 8, 128], name="b5_s2", stack=st)
        xk = [kb.sb([128, 1024], name="b5_xk%d" % i, stack=st) for i in range(NT)]
        wk = kb.sb([128, 256], name="b5_wk", stack=st)
        v12 = kb.sb([128, 2, 16], name="b5_v12", stack=st)
        cand = kb.sb([128, 16, 16], name="b5_cand", stack=st)
        c16 = kb.sb([128, 16], name="b5_c16", stack=st)
        e16 = kb.sb([128, 16], name="b5_e16", stack=st)
        tau = kb.sb([128, NT, 8], name="b5_tau", stack=st)
        negm = kb.sb([128, NT, 8], name="b5_negm", stack=st)
        rz = kb.sb([128, NT, 8], name="b5_rz", stack=st)
        E = kb.sb([128, QI, 128], name="b5_E", stack=st)
        M = kb.sb([128, QI, 128], name="b5_M", stack=st)
        X = kb.sb([128, QI, 128], name="b5_X", stack=st)
        G = [kb.sb([128, QE], name="b5_G%d" % i, stack=st) for i in range(NT)]
        Ac = [kb.sb([128, 512], name="b5_A%d" % i, stack=st) for i in range(2)]
        GAb = [kb.sb([128, 512], BF16, name="b5_GAb%d" % i, stack=st) for i in range(2)]
        GAT = [kb.sb([128, 4, 128], BF16, name="b5_GAT%d" % i, stack=st) for i in range(2)]
        uc = [kb.sb([128, 8, 512], name="b5_u%d" % i, stack=st) for i in range(2)]
        ub = [kb.sb([128, 8, 512], BF16, name="b5_ub%d" % i, stack=st) for i in range(2)]
        vc = [kb.sb([128, 4, 1024], name="b5_v%d" % i, stack=st) for i in range(2)]
        vb = [kb.sb([128, 4, 1024], BF16, name="b5_vb%d" % i, stack=st) for i in range(2)]
        pp = cx.psum
        nch = 0
        nt = 0
        for tp in range(TB // TP):
            r0 = tp * TP
            kb.dma(hf[:], h2T_d.t[:, :, r0:r0 + TP].rearrange("k p t -> p k t"), writes=[hf])
            kb.op("act", lambda e: e.copy(out=hb[:], in_=hf[:]), reads=[hf], writes=[hb])
            for t in range(NT):
                kb.dma(s1[:, t, :, :], s_d.t[0, r0 + t * 128:r0 + (t + 1) * 128, :, :], writes=[s1])
                kb.dma(s2[:, t, :, :], s_d.t[1, r0 + t * 128:r0 + (t + 1) * 128, :, :], writes=[s2])
                kb.dma(xk[t][:], xn_tok_d.t[r0 + t * 128:r0 + (t + 1) * 128, :], writes=[xk[t]])
            for t in range(NT):
                for hd in range(8):
                    for w, src in enumerate((s1, s2)):
                        kb.op("dve", lambda e: e.max(out=v12[:, w, 0:8], in_=src[:, t, hd, :]), reads=[src], writes=[v12])
                        kb.op("dve", lambda e: e.match_replace(out=wk[:, 0:128], in_to_replace=v12[:, w, 0:8], in_values=src[:, t, hd, :], imm_value=-1e30),
                              reads=[src, v12], writes=[wk])
                        kb.op("dve", lambda e: e.max(out=v12[:, w, 8:16], in_=wk[:, 0:128]), reads=[wk], writes=[v12])
                    kb.op("dve", lambda e: e.tensor_tensor(out=cand[:], in0=v12[:, 0, :].unsqueeze(2).broadcast_to([128, 16, 16]),
                                                           in1=v12[:, 1, :].unsqueeze(1).broadcast_to([128, 16, 16]), op=ALU.add), reads=[v12], writes=[cand])
                    cf = cand[:].rearrange("p a b -> p (a b)")
                    kb.op("dve", lambda e: e.max(out=c16[:, 0:8], in_=cf), reads=[cand], writes=[c16])
                    kb.op("dve", lambda e: e.match_replace(out=wk[:], in_to_replace=c16[:, 0:8], in_values=cf, imm_value=-1e30), reads=[cand, c16], writes=[wk])
                    kb.op("dve", lambda e: e.max(out=c16[:, 8:16], in_=wk[:]), reads=[wk], writes=[c16])
                    kb.op("dve", lambda e: e.tensor_copy(out=tau[:, t, hd:hd + 1], in_=c16[:, 15:16]), reads=[c16], writes=[tau])
                    kb.op("dve", lambda e: e.tensor_scalar(out=negm[:, t, hd:hd + 1], in0=c16[:, 0:1], scalar1=-1.0, scalar2=None, op0=ALU.mult), reads=[c16], writes=[negm])
                    kb.op("act", lambda e: e.activation(out=e16[:], in_=c16[:], func=AF.Exp, bias=negm[:, t, hd:hd + 1]), reads=[c16, negm], writes=[e16])
                    kb.op("dve", lambda e: e.tensor_reduce(out=rz[:, t, hd:hd + 1], in_=e16[:], axis=AX.X, op=ALU.add), reads=[e16], writes=[rz])
            kb.op("dve", lambda e: e.reciprocal(out=rz[:], in_=rz[:]), reads=[rz], writes=[rz])
            O = [[pp[4 + 2 * t + h2] for h2 in range(2)] for t in range(NT)]
            for qi in range(NQ):
                for t in range(NT):
                    G3 = G[t][:].rearrange("p (i j) -> p i j", j=128)
                    for hd in range(8):
                        kb.op("dve", lambda e: e.tensor_tensor(out=E[:], in0=s1[:, t, hd, qi * QI:(qi + 1) * QI].unsqueeze(2).broadcast_to([128, QI, 128]),
                                                               in1=s2[:, t, hd, :].unsqueeze(1).broadcast_to([128, QI, 128]), op=ALU.add), reads=[s1, s2], writes=[E])
                        kb.op("act", lambda e: e.activation(out=X[:], in_=E[:], func=AF.Exp, bias=negm[:, t, hd:hd + 1]), reads=[E, negm], writes=[X])
                        kb.op("dve", lambda e: e.scalar_tensor_tensor(out=M[:], in0=E[:], scalar=tau[:, t, hd:hd + 1], in1=X[:], op0=ALU.is_ge, op1=ALU.mult),
                              reads=[E, tau, X], writes=[M])
                        if hd == 0:
                            kb.op("dve", lambda e: e.tensor_scalar(out=G3, in0=M[:], scalar1=rz[:, t, hd:hd + 1], scalar2=None, op0=ALU.mult),
                                  reads=[M, rz], writes=[G[t]])
                        else:
                            kb.op("dve", lambda e: e.scalar_tensor_tensor(out=G3, in0=M[:], scalar=rz[:, t, hd:hd + 1], in1=G3, op0=ALU.mult, op1=ALU.add),
                                  reads=[M, rz, G[t]], writes=[G[t]])
                for ch in range(NCH):
                    e0 = qi * QE + ch * 512
                    u, v, ubb, vbb = uc[nch % 2], vc[nch % 2], ub[nch % 2], vb[nch % 2]
                    nch += 1
                    kb.dma(u[:], cx.uT[e0 // 512], writes=[u])
                    kb.dma(v[:], cx.vv[e0 // 512], writes=[v])
                    kb.op("act", lambda e: e.copy(out=ubb[:], in_=u[:]), reads=[u], writes=[ubb])
                    kb.op("act", lambda e: e.copy(out=vbb[:], in_=v[:]), reads=[v], writes=[vbb])
                    for t in range(NT):
                        A, gab, gat = Ac[nt % 2], GAb[nt % 2], GAT[nt % 2]
                        pa, ptr = pp[nt % 2], pp[2 + nt % 2]
                        nt += 1
                        ptrb = ptr[:, 0:256].bitcast(BF16)
                        for k in range(8):
                            kb.op("pe", lambda e: e.matmul(pa[:, :], lhsT=hb[:, k, t * 128:(t + 1) * 128], rhs=ubb[:, k, :], start=(k == 0), stop=(k == 7)),
                                  reads=[hb, ubb], writes=[pa])
                        kb.op("act", lambda e: e.activation(out=A[:], in_=pa[:, :], func=AF.Gelu), reads=[pa], writes=[A])
                        kb.op("dve", lambda e: e.tensor_tensor(out=gab[:], in0=A[:], in1=G[t][:, ch * 512:(ch + 1) * 512], op=ALU.mult), reads=[A, G[t]], writes=[gab])
                        for s in range(4):
                            kb.op("pe", lambda e: e.transpose(ptrb[:, s * 128:(s + 1) * 128], gab[:, s * 128:(s + 1) * 128], identb[:]), reads=[gab, identb], writes=[ptr])
                        kb.op("act", lambda e: e.copy(out=gat[:], in_=ptrb.rearrange("p (s t) -> p s t", s=4)), reads=[ptr], writes=[gat])
                        first = (qi == 0 and ch == 0)
                        for s in range(4):
                            lastmm = (qi == NQ - 1 and ch == NCH - 1 and s == 3)
                            for h2 in range(2):
                                kb.op("pe", lambda e: e.matmul(O[t][h2][:, :], lhsT=gat[:, s, :], rhs=vbb[:, s, h2 * 512:(h2 + 1) * 512], start=(first and s == 0), stop=lastmm),
                                      reads=[gat, vbb], writes=[O[t][h2]])
            for t in range(NT):
                for h2 in range(2):
                    kb.op("dve", lambda e: e.tensor_tensor(out=xk[t][:, h2 * 512:(h2 + 1) * 512], in0=O[t][h2][:, :], in1=xk[t][:, h2 * 512:(h2 + 1) * 512], op=ALU.add),
                          reads=[O[t][h2], xk[t]], writes=[xk[t]])
                kb.dma(out_d[r0 + t * 128:r0 + (t + 1) * 128, :], xk[t][:], reads=[xk[t]], writes=[cx.OUT])
        kb.barrier()

C = np.ascontiguousarray
IN_SSD = 1024 + 2048 + 16
B2 = IN_SSD + 3072

def colT(v):
    return C(np.asarray(v, np.float32).reshape(-1, 128).T)

def wtiles(w):
    return C(np.asarray(w, np.float32).reshape(8, 128, -1))

def prep_common(x_b, norm_g):
    T = x_b.shape[0]
    return {"xT": C(x_b.T.reshape(8, 128, T)), "g1": colT(norm_g), "ident": np.eye(128, dtype=np.float32)}

def prep_rwkv(inp, l, hg):
    w_in = inp["w_in"][l]
    cs = slice(256 * hg, 256 * hg + 256)
    r = w_in[:, B2:B2 + 1024][:, cs]; k = w_in[:, B2 + 1024:B2 + 2048][:, cs]; v = w_in[:, B2 + 2048:B2 + 3072][:, cs]
    la = w_in[:, B2 + 3072:B2 + 3200]; gl = w_in[:, B2 + 3200:B2 + 3328]
    W = np.concatenate([r, k, v, la, gl], axis=1)
    mu = inp["rwkv_mu"][l]
    mu_all = np.concatenate([mu[0:1024][cs], mu[1024:2048][cs], mu[2048:3072][cs], mu[3072:3200], mu[3200:3328]])
    pc = np.concatenate([colT(inp[n][l].reshape(-1)[cs]) for n in ("w0", "a0", "k_k", "k_a", "r_k", "lnx_g", "lnx_b")], axis=1)
    la2 = np.concatenate([inp["w_w2"][l][:, cs], inp["w_a2"][l][:, cs]], axis=0)
    return {"rw_w": wtiles(W), "rw_mu": colT(mu_all), "rw_pc": C(pc), "rw_la2": C(la2.astype(np.float32)), "rw_g2": C(inp["w_g2"][l][:, cs])}

def prep_ssd(inp, l, hg):
    w_in = inp["w_in"][l]
    z = w_in[:, 256 * hg:256 * hg + 256]
    x = w_in[:, 1024 + 256 * hg:1024 + 256 * hg + 256]
    Bc = w_in[:, 2048 + 128 * hg:2048 + 128 * hg + 128]
    Cc = w_in[:, 2560 + 128 * hg:2560 + 128 * hg + 128]
    dtc = w_in[:, 3072 + 4 * hg:3072 + 4 * hg + 4]
    dtrep = np.concatenate([np.repeat(dtc[:, 2 * j + ep:2 * j + ep + 1], 64, axis=1) for j in range(2) for ep in range(2)], axis=1)
    W = np.concatenate([z, x, Bc, Cc, dtrep], axis=1)
    cw = inp["conv_w"][l]; cb = inp["conv_b"][l]
    chans = np.concatenate([np.arange(256 * hg, 256 * hg + 256), 1024 + np.arange(128 * hg, 128 * hg + 128), 1536 + np.arange(128 * hg, 128 * hg + 128)])
    convw = np.zeros((128, 16), np.float32); convb = np.zeros((128, 4), np.float32)
    for c in range(4):
        ch = chans[c * 128:(c + 1) * 128]
        for k in range(4):
            convw[:, c * 4 + k] = cw[k, ch]
        convb[:, c] = cb[ch]
    def hrep(v):
        v = np.asarray(v)[4 * hg:4 * hg + 4]
        return np.stack([np.repeat(v[2 * j:2 * j + 2], 64) for j in range(2)], axis=1).astype(np.float32)
    pc = np.concatenate([convw, convb, hrep(inp["dt_bias"][l]), hrep(inp["a_log"][l]), hrep(inp["d_skip"][l]),
                         colT(inp["ssd_norm_g"][l][256 * hg:256 * hg + 256])], axis=1)
    idx = np.arange(128)
    mk = np.stack([(idx[:, None] <= idx[None, :]), (idx[:, None] > idx[None, :]), -30000.0 * (idx[:, None] > idx[None, :])]).astype(np.float32)
    return {"ss_w": wtiles(W), "ss_pc": C(pc), "ss_mk": C(mk)}

def bucket_strip():
    import jax, jax.numpy as jnp, math
    cpu = jax.devices("cpu")[0]
    with jax.default_device(cpu):
        d = jnp.arange(0, 1280, dtype=jnp.int32)
        max_exact = 16
        df = jnp.maximum(d, 1).astype(jnp.float32)
        large = max_exact + (jnp.log(df / max_exact) / math.log(1024 / max_exact) * (32 - max_exact)).astype(jnp.int32)
        large = jnp.minimum(large, 31)
        bk = np.asarray(jnp.where(d < max_exact, d, large))
    m = np.arange(1280)[None, :] - np.arange(128)[:, None]
    out = np.where(m >= 0, bk[np.clip(m, 0, 1279)], -1).astype(np.float32)
    return C(out)

def prep_moba(inp, l, hg):
    w_in = inp["w_in"][l]
    hs = [2 * hg, 2 * hg + 1]
    q = np.concatenate([w_in[:, IN_SSD + h * 128:IN_SSD + (h + 1) * 128] for h in hs], axis=1)
    k = np.concatenate([w_in[:, IN_SSD + 1024 + h * 128:IN_SSD + 1024 + (h + 1) * 128] for h in hs], axis=1)
    v = np.concatenate([w_in[:, IN_SSD + 2048 + h * 128:IN_SSD + 2048 + (h + 1) * 128] for h in hs], axis=1)
    g = np.stack([inp["q_norm_g"][l], inp["k_norm_g"][l]], axis=1).astype(np.float32)
    rel = np.concatenate([inp["rel_bias"][:, h] for h in hs])[None, :].astype(np.float32)
    return {"mb_wqk": wtiles(np.concatenate([q, k], axis=1)), "mb_wv": wtiles(v), "mb_g": C(g), "mb_rel": C(rel), "mb_bi": bucket_strip()}

def prep_lb_weights(inp, l):
    return {"g1": colT(inp["norm1_g"][l]), "g2": colT(inp["norm2_g"][l]),
            "wg": wtiles(inp["w_gate"][l]), "bg": colT(inp["b_gate"][l]),
            "pall": C(np.concatenate([inp["p_ssd"][l], inp["p_moba"][l], inp["p_rwkv"][l]], axis=0).reshape(24, 128, 1024).astype(np.float32)),
            "wout": wtiles(inp["w_out"][l]), "wq": wtiles(inp["peer_wq"][l]),
            "k12T": C(np.concatenate([inp["peer_k1"][l].T, inp["peer_k2"][l].T], axis=1).astype(np.float32)),
            "uT": C(np.asarray(inp["peer_u"][l], np.float32).reshape(32, 512, 8, 128).transpose(0, 3, 2, 1)), "vv": C(np.asarray(inp["peer_v"][l], np.float32).reshape(32, 4, 128, 1024).transpose(0, 2, 1, 3)), "ident": np.eye(128, dtype=np.float32)}

def prep_lb_acts(x_tok, yfull):
    TB = x_tok.shape[0]
    return {"xT": C(x_tok.T.reshape(8, 128, TB)), "xtok": C(x_tok), "yT": C(yfull.T.reshape(24, 128, TB))}

import time as _time
from concourse.bass_utils import run_bass_kernel_spmd

T_FULL = 16384
TB_FULL = 4096
_PROG = {}


def build_la(T):
    nc = bass.Bass("TRN2", target_bir_lowering=False)
    kb = KB(nc)
    cx = Ctx()
    setup_consts(kb, nc, cx)
    xT_d = din(nc, "xT", [8, 128, T]); g_d = din(nc, "g1", [128, 8])
    rwkv_inputs(nc, cx); ssd_inputs(nc, cx); moba_inputs(nc, cx)
    ya = dout(nc, "yaT", [2, 128, T]); yb = dout(nc, "yb", [2, T, 128]); yc = dout(nc, "ycT", [2, 128, T])
    hT = kb.dram("hT_s", [8, 128, T])
    strm = kb.dram("rw_strm", [5, 2, T, 128], BF16); strm_w = kb.dram("rw_strmw", [2, T, 128]); rfm = kb.dram("rw_fm", [4, 2, 128, T])
    sfm = kb.dram("ss_fm", [4, 2, 128, T]); bcf = kb.dram("ss_bc", [2, 128, T])
    qT = kb.dram("mb_q", [2, 128, T], BF16); kT = kb.dram("mb_k", [2, 128, T], BF16); vd = kb.dram("mb_v", [2, T, 129], BF16); nm = kb.dram("mb_nm", [2, 64, T], BF16)
    phase_norm(kb, nc, cx, T, xT_d, g_d, hT)
    rwkv_phase1(kb, nc, cx, T, hT, strm, rfm, strm_w=strm_w)
    ssd_phase1(kb, nc, cx, T, hT, sfm, bcf)
    moba_phase1(kb, nc, cx, T, hT, qT, kT, vd, nm)
    moba_phase2(kb, nc, cx, T, qT, kT, vd, nm, yb)
    rwkv_phase2(kb, nc, cx, T, strm, rfm, yc, same=False, strm_w=strm_w)
    ssd_phase2(kb, nc, cx, T, sfm, bcf, ya)
    kb.finish()
    return nc, kb


def build_lb_prog(TB):
    nc = bass.Bass("TRN2", target_bir_lowering=False)
    kb = KB(nc)
    cx = Ctx()
    build_lb(nc, kb, cx, TB)
    return nc, kb


def _prog(kind, n):
    key = (kind, n)
    if key not in _PROG:
        _PROG[key] = (build_la if kind == "A" else build_lb_prog)(n)[0]
    return _PROG[key]


def kernel(**inputs):
    inp = {k: np.asarray(v) for k, v in inputs.items()}
    x = np.ascontiguousarray(inp["x"], dtype=np.float32)
    Bn, T, D = x.shape
    TB = (Bn * T) // 8
    per_b = T // TB
    ncores = 8
    for l in range(2):
        ncA = _prog("A", T)
        in_maps = []
        for c in range(ncores):
            b, hg = c // 4, c % 4
            m = prep_common(x[b], inp["norm1_g"][l])
            m.update(prep_rwkv(inp, l, hg)); m.update(prep_ssd(inp, l, hg)); m.update(prep_moba(inp, l, hg))
            in_maps.append(m)
        resA = run_bass_kernel_spmd(ncA, in_maps, core_ids=list(range(ncores))).results
        del in_maps
        yfull = np.empty((Bn, T, 3072), np.float32)
        for c in range(ncores):
            b, hg = c // 4, c % 4
            r = resA[c]
            yfull[b, :, 256 * hg:256 * hg + 256] = np.asarray(r["yaT"]).reshape(256, T).T
            yfull[b, :, 1024 + 256 * hg:1024 + 256 * hg + 256] = np.asarray(r["yb"]).transpose(1, 0, 2).reshape(T, 256)
            yfull[b, :, 2048 + 256 * hg:2048 + 256 * hg + 256] = np.asarray(r["ycT"]).reshape(256, T).T
        del resA
        ncB = _prog("B", TB)
        wts = prep_lb_weights(inp, l)
        in_maps = []
        for c in range(ncores):
            b, s0 = c // per_b, (c % per_b) * TB
            m = dict(wts)
            m.update(prep_lb_acts(x[b, s0:s0 + TB], yfull[b, s0:s0 + TB]))
            in_maps.append(m)
        resB = run_bass_kernel_spmd(ncB, in_maps, core_ids=list(range(ncores))).results
        del in_maps
        xn = np.empty_like(x)
        for c in range(ncores):
            b, s0 = c // per_b, (c % per_b) * TB
            xn[b, s0:s0 + TB] = np.asarray(resB[c]["xout"])
        x = xn
    return x
```

```python
import contextlib
import numpy as np
import concourse.bass as bass
import concourse.mybir as mybir

F32 = mybir.dt.float32
BF16 = mybir.dt.bfloat16
ALU = mybir.AluOpType
AF = mybir.ActivationFunctionType
AX = mybir.AxisListType

EP = 30000
KD = 24


class Buf:
    __slots__ = ("t", "w", "r", "name")

    def __init__(self, t, name=""):
        self.t = t
        self.w = {}
        self.r = {}
        self.name = name

    def __getitem__(self, idx):
        return self.t[idx]


class KB:
    def __init__(self, nc, same_engine_sync=True):
        self.nc = nc
        self.es = contextlib.ExitStack()
        self.eng = {"pe": nc.tensor, "dve": nc.vector, "act": nc.scalar, "pool": nc.gpsimd, "sp": nc.sync}
        self.cnt = {e: 0 for e in ("pe", "dve", "act", "pool")}
        self.sems = {}
        self.waited = {e: {} for e in self.eng}
        self.dma_i = 0
        self.dma_last = {}
        self.same = same_engine_sync
        self.nbuf = 0
        self.n_wait = 0

    def sem(self, key):
        s = self.sems.get(key)
        if s is None:
            s = self.es.enter_context(self.nc.semaphore("s_%s_%s" % key))
            self.sems[key] = s
        return s

    def sb(self, shape, dt=F32, name=None, stack=None):
        self.nbuf += 1
        name = ("s%d_" % self.nbuf + name) if name else "sb%d" % self.nbuf
        t = (stack or self.es).enter_context(self.nc.sbuf_tensor(name, list(shape), dt))
        return Buf(t, name)

    def ps(self, shape, dt=F32, name=None, stack=None):
        self.nbuf += 1
        name = ("p%d_" % self.nbuf + name) if name else "ps%d" % self.nbuf
        t = (stack or self.es).enter_context(self.nc.psum_tensor(name, list(shape), dt))
        return Buf(t, name)

    def dram(self, name, shape, dt=F32, kind="Internal"):
        t = self.nc.dram_tensor(name, list(shape), dt, kind=kind)
        return Buf(t.ap(), name)

    def _deps(self, reads, writes):
        deps = {}
        for b in reads:
            for k, v in b.w.items():
                if deps.get(k, 0) < v:
                    deps[k] = v
        for b in writes:
            for k, v in b.w.items():
                if deps.get(k, 0) < v:
                    deps[k] = v
            for k, v in b.r.items():
                if deps.get(k, 0) < v:
                    deps[k] = v
        return deps

    def _wait(self, E, deps, same=None):
        wd = self.waited[E]
        if same is None:
            same = self.same
        for k, v in deps.items():
            if k[0] == E:
                if E == "pe" or not same:
                    continue
            if wd.get(k, 0) >= v:
                continue
            self.eng[E].wait_ge(self.sem(k), v)
            self.n_wait += 1
            wd[k] = v

    def _mark(self, dep, reads, writes):
        k, v = dep
        for b in reads:
            b.r[k] = v
        for b in writes:
            b.w = {k: v}
            b.r = {}

    def op(self, E, fn, reads=(), writes=(), same=None):
        self._wait(E, self._deps(reads, writes), same)
        ins = fn(self.eng[E])
        n = self.cnt[E] = self.cnt[E] + 1
        key = (E, (n - 1) // EP)
        val = (n - 1) % EP + 1
        ins.then_inc(self.sem(key), 1)
        self._mark((key, val), reads, writes)
        return ins

    def dma(self, out, in_, reads=(), writes=(), Q="sp", **kw):
        self._wait(Q, self._deps(reads, writes))
        i = self.dma_i
        self.dma_i += 1
        s = i % KD
        key = ("dma", s)
        val = 16 * (i // KD + 1)
        if i >= KD:
            wd = self.waited[Q]
            if wd.get(key, 0) < val - 16:
                self.eng[Q].wait_ge(self.sem(key), val - 16)
                wd[key] = val - 16
        self.eng[Q].dma_start(out=out, in_=in_, **kw).then_inc(self.sem(key), 16)
        self.dma_last[s] = val
        self._mark((key, val), reads, writes)

    def barrier(self):
        state = {}
        for e, n in self.cnt.items():
            if n > 0:
                state[(e, (n - 1) // EP)] = (n - 1) % EP + 1
        for s, v in self.dma_last.items():
            state[("dma", s)] = v
        for E in self.eng:
            wd = self.waited[E]
            for k, v in state.items():
                if k[0] == E and E == "pe":
                    continue
                if wd.get(k, 0) >= v:
                    continue
                self.eng[E].wait_ge(self.sem(k), v)
                wd[k] = v

    def finish(self):
        self.barrier()

import math


def din(nc, name, shape, dt=F32):
    return nc.dram_tensor(name, list(shape), dt, kind="ExternalInput").ap()


def dout(nc, name, shape, dt=F32):
    return nc.dram_tensor(name, list(shape), dt, kind="ExternalOutput").ap()


class Ctx:
    pass


def setup_consts(kb, nc, cx):
    cx.ident_d = din(nc, "ident", [128, 128])
    cx.ident = kb.sb([128, 128], name="ident")
    kb.dma(cx.ident[:], cx.ident_d, writes=[cx.ident])
    cx.ones = kb.sb([128, 128], name="ones")
    kb.op("dve", lambda e: e.memset(cx.ones[:], 1.0), writes=[cx.ones])
    cx.bones = kb.sb([128, 128], name="bones")
    kb.op("dve", lambda e: e.memset(cx.bones[:], 0.0), writes=[cx.bones])
    kb.op("dve", lambda e: e.memset(cx.bones[0:64, 0:64], 1.0), writes=[cx.bones])
    kb.op("dve", lambda e: e.memset(cx.bones[64:128, 64:128], 1.0), writes=[cx.bones])
    cx.eps6 = kb.sb([128, 1], name="eps6")
    kb.op("dve", lambda e: e.memset(cx.eps6[:], 1e-6), writes=[cx.eps6])
    cx.psum = [kb.ps([128, 512], name="bank%d" % i) for i in range(8)]
    cx.OUT = Buf(None, "OUT")


def phase_norm(kb, nc, cx, T, xT_d, g_d, hT_dram):
    import contextlib
    with contextlib.ExitStack() as st:
        g = kb.sb([128, 8], name="n_g", stack=st)
        kb.dma(g[:], g_d, writes=[g])
        xs = [kb.sb([128, 8, 512], name="n_x%d" % i, stack=st) for i in range(2)]
        sq = kb.sb([128, 8, 512], name="n_sq", stack=st)
        rstd = kb.sb([128, 512], name="n_rstd", stack=st)
        hs = [kb.sb([128, 8, 512], name="n_h%d" % i, stack=st) for i in range(2)]
        p0 = cx.psum[0]
        for i in range(T // 512):
            x = xs[i % 2]
            h = hs[i % 2]
            kb.dma(x[:], xT_d[:, :, i * 512:(i + 1) * 512].rearrange("k p t -> p k t"), writes=[x])
            kb.op("act", lambda e: e.activation(out=sq[:], in_=x[:], func=AF.Square), reads=[x], writes=[sq])
            for k in range(8):
                kb.op("pe", lambda e: e.matmul(p0[:, :], lhsT=cx.ones[:], rhs=sq[:, k, :], start=(k == 0), stop=(k == 7)),
                      reads=[cx.ones, sq], writes=[p0])
            kb.op("act", lambda e: e.activation(out=rstd[:], in_=p0[:, :], func=AF.Sqrt, scale=1.0 / 1024, bias=cx.eps6[:]),
                  reads=[p0, cx.eps6], writes=[rstd])
            kb.op("dve", lambda e: e.reciprocal(out=rstd[:], in_=rstd[:]), reads=[rstd], writes=[rstd])
            for k in range(8):
                kb.op("dve", lambda e: e.scalar_tensor_tensor(out=h[:, k, :], in0=x[:, k, :], scalar=g[:, k:k + 1], in1=rstd[:],
                                                              op0=ALU.mult, op1=ALU.mult), reads=[x, g, rstd], writes=[h])
            kb.dma(hT_dram.t[:, :, i * 512:(i + 1) * 512].rearrange("k p t -> p k t"), h[:], reads=[h], writes=[hT_dram])
        kb.barrier()


EM05 = math.exp(-0.5)


def rwkv_inputs(nc, cx):
    cx.rw_w = din(nc, "rw_w", [8, 128, 1024])
    cx.rw_mu = din(nc, "rw_mu", [128, 8])
    cx.rw_pc = din(nc, "rw_pc", [128, 14])
    cx.rw_la2 = din(nc, "rw_la2", [128, 256])
    cx.rw_g2 = din(nc, "rw_g2", [128, 256])


def rwkv_phase1(kb, nc, cx, T, hT_dram, strm, fm, strm_w=None):
    with contextlib.ExitStack() as st:
        W = kb.sb([128, 8, 1024], name="rw_W", stack=st)
        kb.dma(W[:], cx.rw_w.rearrange("k p c -> p k c"), writes=[W])
        mu = kb.sb([128, 8], name="rw_mu", stack=st)
        kb.dma(mu[:], cx.rw_mu, writes=[mu])
        pc = kb.sb([128, 14], name="rw_pc", stack=st)
        kb.dma(pc[:], cx.rw_pc, writes=[pc])
        la2 = kb.sb([128, 256], name="rw_la2", stack=st)
        kb.dma(la2[:], cx.rw_la2, writes=[la2])
        g2 = kb.sb([128, 256], name="rw_g2", stack=st)
        kb.dma(g2[:], cx.rw_g2, writes=[g2])
        eps12 = kb.sb([128, 1], name="rw_eps12", stack=st)
        hTs = [kb.sb([128, 8, 512], name="rw_h%d" % i, stack=st) for i in range(2)]
        raw = [kb.sb([128, 513], name="rw_raw%d" % c, stack=st) for c in range(8)]
        xs = [kb.sb([128, 512], name="rw_xs%d" % c, stack=st) for c in range(8)]
        for c in range(8):
            kb.op("dve", lambda e: e.memset(raw[c][:, 0:1], 0.0), writes=[raw[c]])
        tmp = [kb.sb([128, 512], name="rw_tmp%d" % i, stack=st) for i in range(8)]
        sbs = [kb.sb([128, 512], name="rw_s%d" % i, stack=st) for i in range(5)]
        gj = kb.sb([128, 512], name="rw_gj", stack=st)
        tok = [kb.sb([128, 512], BF16, name="rw_tok%d" % i, stack=st) for i in range(2)]
        tokf = [kb.sb([128, 512], name="rw_tokf%d" % i, stack=st) for i in range(2)]
        pp = cx.psum
        ntok = 0
        for i in range(T // 512):
            hT = hTs[i % 2]
            kb.dma(hT[:], hT_dram.t[:, :, i * 512:(i + 1) * 512].rearrange("k p t -> p k t"), writes=[hT])
            for c in range(8):
                ps = pp[c % 2]
                for k in range(8):
                    kb.op("pe", lambda e: e.matmul(ps[:, :], lhsT=W[:, k, c * 128:(c + 1) * 128], rhs=hT[:, k, :], start=(k == 0), stop=(k == 7)),
                          reads=[W, hT], writes=[ps])
                kb.op("act", lambda e: e.copy(out=raw[c][:, 1:513], in_=ps[:, :]), reads=[ps], writes=[raw[c]])
                kb.op("dve", lambda e: e.tensor_tensor(out=xs[c][:], in0=raw[c][:, 0:512], in1=raw[c][:, 1:513], op=ALU.subtract),
                      reads=[raw[c]], writes=[xs[c]])
                kb.op("dve", lambda e: e.scalar_tensor_tensor(out=xs[c][:], in0=xs[c][:], scalar=mu[:, c:c + 1], in1=raw[c][:, 1:513],
                                                              op0=ALU.mult, op1=ALU.add), reads=[xs[c], mu, raw[c]], writes=[xs[c]])
                kb.op("dve", lambda e: e.tensor_copy(out=raw[c][:, 0:1], in_=raw[c][:, 512:513]), reads=[raw[c]], writes=[raw[c]])
            xla, xgl = xs[6], xs[7]
            tanh_wl, sig_gl = tmp[0], tmp[1]
            kb.op("act", lambda e: e.activation(out=tanh_wl[0:64, :], in_=xla[0:64, :], func=AF.Tanh), reads=[xla], writes=[tanh_wl])
            kb.op("act", lambda e: e.activation(out=sig_gl[:], in_=xgl[:], func=AF.Sigmoid), reads=[xgl], writes=[sig_gl])
            for j in range(2):
                R, K, V = xs[0 + j], xs[2 + j], xs[4 + j]
                A_s, W_s, B_s, K_s = sbs[0], sbs[1], sbs[2], sbs[3]
                asig, kx, t1 = tmp[2], tmp[3], tmp[4]
                js = slice(j * 128, (j + 1) * 128)
                kb.op("pe", lambda e: e.matmul(pp[2][:, :], lhsT=la2[0:64, js], rhs=tanh_wl[0:64, :], start=True, stop=True),
                      reads=[la2, tanh_wl], writes=[pp[2]])
                kb.op("act", lambda e: e.activation(out=W_s[:], in_=pp[2][:, :], func=AF.Sigmoid, bias=pc[:, 0 + j:1 + j]),
                      reads=[pp[2], pc], writes=[W_s])
                kb.op("act", lambda e: e.activation(out=W_s[:], in_=W_s[:], func=AF.Exp, scale=-EM05), reads=[W_s], writes=[W_s])
                kb.op("pe", lambda e: e.matmul(pp[3][:, :], lhsT=la2[64:128, js], rhs=xla[64:128, :], start=True, stop=True),
                      reads=[la2, xla], writes=[pp[3]])
                kb.op("act", lambda e: e.activation(out=asig[:], in_=pp[3][:, :], func=AF.Sigmoid, bias=pc[:, 2 + j:3 + j]),
                      reads=[pp[3], pc], writes=[asig])
                kb.op("pe", lambda e: e.matmul(pp[4][:, :], lhsT=g2[:, js], rhs=sig_gl[:], start=True, stop=True),
                      reads=[g2, sig_gl], writes=[pp[4]])
                kb.op("act", lambda e: e.copy(out=gj[:], in_=pp[4][:, :]), reads=[pp[4]], writes=[gj])
                kb.op("dve", lambda e: e.tensor_scalar(out=kx[:], in0=K[:], scalar1=pc[:, 4 + j:5 + j], scalar2=None, op0=ALU.mult),
                      reads=[K, pc], writes=[kx])
                kb.op("act", lambda e: e.activation(out=t1[:], in_=kx[:], func=AF.Square), reads=[kx], writes=[t1])
                kb.op("pe", lambda e: e.matmul(pp[5][:, :], lhsT=cx.bones[:], rhs=t1[:], start=True, stop=True),
                      reads=[cx.bones, t1], writes=[pp[5]])
                kb.op("dve", lambda e: e.tensor_scalar(out=t1[:], in0=pp[5][:, :], scalar1=1e-12, scalar2=None, op0=ALU.max),
                      reads=[pp[5]], writes=[t1])
                kb.op("act", lambda e: e.activation(out=t1[:], in_=t1[:], func=AF.Sqrt), reads=[t1], writes=[t1])
                kb.op("dve", lambda e: e.reciprocal(out=t1[:], in_=t1[:]), reads=[t1], writes=[t1])
                kb.op("dve", lambda e: e.tensor_tensor(out=kx[:], in0=kx[:], in1=t1[:], op=ALU.mult), reads=[kx, t1], writes=[kx])
                kb.op("dve", lambda e: e.tensor_scalar(out=A_s[:], in0=kx[:], scalar1=-1.0, scalar2=None, op0=ALU.mult),
                      reads=[kx], writes=[A_s])
                kb.op("dve", lambda e: e.tensor_tensor(out=B_s[:], in0=kx[:], in1=asig[:], op=ALU.mult), reads=[kx, asig], writes=[B_s])
                kb.op("dve", lambda e: e.tensor_scalar(out=t1[:], in0=asig[:], scalar1=-1.0, scalar2=pc[:, 6 + j:7 + j], op0=ALU.add, op1=ALU.mult),
                      reads=[asig, pc], writes=[t1])
                kb.op("dve", lambda e: e.scalar_tensor_tensor(out=K_s[:], in0=t1[:], scalar=1.0, in1=K[:], op0=ALU.add, op1=ALU.mult),
                      reads=[t1, K], writes=[K_s])
                for q, src in enumerate((R, K_s, V, gj)):
                    kb.dma(fm.t[q, j, :, i * 512:(i + 1) * 512], src[:], reads=[src])
                for q, src in enumerate((A_s, W_s, B_s, K_s, R)):
                    ps = pp[6 + (ntok % 2)]
                    tk = (tokf if q == 1 else tok)[ntok % 2]
                    ntok += 1
                    for s in range(4):
                        kb.op("pe", lambda e: e.transpose(ps[:, s * 128:(s + 1) * 128], src[:, s * 128:(s + 1) * 128], cx.ident[:]),
                              reads=[src, cx.ident], writes=[ps])
                    kb.op("act", lambda e: e.copy(out=tk[:], in_=ps[:, :]), reads=[ps], writes=[tk])
                    for hp in range(2):
                        dd = strm_w.t[hp] if q == 1 else strm.t[q, hp]
                        dst = dd[i * 512:(i + 1) * 512, j * 64:(j + 1) * 64].rearrange("(s p) k -> p s k", p=128)
                        srcap = tk[:].rearrange("p (s h k) -> p s h k", s=4, h=2)[:, :, hp, :]
                        kb.dma(dst, srcap, reads=[tk])
        kb.barrier()


def rwkv_phase2_gen(kb, nc, cx, T, strm, fm, yc_out, CH=16, same=True, strm_w=None, pbanks=(0, 1, 2)):
    with contextlib.ExitStack() as st:
        pc = kb.sb([128, 14], name="r2_pc", stack=st)
        kb.dma(pc[:], cx.rw_pc, writes=[pc])
        eps = kb.sb([128, 1], name="r2_eps", stack=st)
        kb.op("dve", lambda e: e.memset(eps[:], 64e-5), writes=[eps])
        S = kb.sb([128, 128], name="r2_S", stack=st)
        kb.op("dve", lambda e: e.memset(S[:], 0.0), writes=[S])
        T1 = kb.sb([128, 128], name="r2_T1", stack=st)
        T2 = kb.sb([128, 128], name="r2_T2", stack=st)
        T4 = kb.sb([128, 128], name="r2_T4", stack=st)
        T3 = [kb.sb([128, 128], name="r2_T3%d" % i, stack=st) for i in range(4)]
        sa = kb.sb([128, 2], name="r2_sa", stack=st)
        bcs = [[kb.sb([128, CH, 128], (F32 if q == 1 else BF16), name="r2_bc%d_%d" % (q, b), stack=st) for b in range(2)] for q in range(5)]
        fms = [[kb.sb([128, 2, 512], name="r2_fm%d_%d" % (q, b), stack=st) for b in range(2)] for q in range(4)]
        ys = [kb.sb([128, 2, 512], name="r2_y%d" % b, stack=st) for b in range(2)]
        wk = [kb.sb([128, 512], name="r2_wk%d" % b, stack=st) for b in range(3)]
        pp = [cx.psum[b] for b in pbanks]
        v3 = lambda ap: ap.rearrange("p (j k) -> p j k", j=2)
        for i in range(T // 512):
            fb = [fms[q][i % 2] for q in range(4)]
            for q in range(4):
                kb.dma(fb[q][:], fm.t[q, :, :, i * 512:(i + 1) * 512].rearrange("j p t -> p j t"), writes=[fb[q]])
            Rf, Kf, Vf, Gf = fb
            Y = ys[i % 2]
            for cch in range(512 // CH):
                t0 = i * 512 + cch * CH
                cb = [bcs[q][cch % 2] for q in range(5)]
                for q in range(5):
                    for hp in range(2):
                        kb.dma(cb[q][hp * 64:(hp + 1) * 64, :, :],

                               (strm_w.t[hp] if q == 1 else strm.t[q, hp])[t0:t0 + CH, :].unsqueeze(0).broadcast_to([64, CH, 128]),
                               writes=[cb[q]])
                Ab, Wb, Bb, Kb, Rb = cb
                for s in range(CH):
                    tt = cch * CH + s
                    t3 = T3[tt % 4]
                    for j2 in range(2):
                        kb.op("act", lambda e: e.activation(out=t3[:, j2 * 64:(j2 + 1) * 64], in_=Kb[:, s, j2 * 64:(j2 + 1) * 64], func=AF.Copy,
                                                            scale=Vf[:, j2, tt:tt + 1]), reads=[Kb, Vf], writes=[t3], same=same)
                    kb.op("dve", lambda e: e.tensor_tensor(out=T1[:], in0=S[:], in1=Ab[:, s, :], op=ALU.mult), reads=[S, Ab], writes=[T1], same=same)
                    kb.op("dve", lambda e: e.tensor_reduce(out=sa[:], in_=v3(T1[:]), axis=AX.X, op=ALU.add), reads=[T1], writes=[sa], same=same)
                    kb.op("dve", lambda e: e.tensor_tensor(out=S[:], in0=S[:], in1=Wb[:, s, :], op=ALU.mult), reads=[S, Wb], writes=[S], same=same)
                    kb.op("dve", lambda e: e.tensor_tensor(out=v3(T2[:]), in0=v3(Bb[:, s, :]), in1=sa[:].unsqueeze(2).broadcast_to([128, 2, 64]),
                                                           op=ALU.mult), reads=[Bb, sa], writes=[T2], same=same)
                    kb.op("dve", lambda e: e.tensor_tensor(out=S[:], in0=S[:], in1=T2[:], op=ALU.add), reads=[S, T2], writes=[S], same=same)
                    kb.op("dve", lambda e: e.tensor_tensor(out=S[:], in0=S[:], in1=t3[:], op=ALU.add), reads=[S, t3], writes=[S], same=same)
                    kb.op("dve", lambda e: e.tensor_tensor(out=T4[:], in0=S[:], in1=Rb[:, s, :], op=ALU.mult), reads=[S, Rb], writes=[T4], same=same)
                    kb.op("dve", lambda e: e.tensor_reduce(out=Y[:, :, tt], in_=v3(T4[:]), axis=AX.X, op=ALU.add), reads=[T4], writes=[Y], same=same)
                yield
            for j in range(2):
                yj = Y[:, j, :]
                a, b, c = wk
                kb.op("pe", lambda e: e.matmul(pp[0][:, :], lhsT=cx.bones[:], rhs=yj, start=True, stop=True), reads=[cx.bones, Y], writes=[pp[0]])
                kb.op("dve", lambda e: e.scalar_tensor_tensor(out=a[:], in0=pp[0][:, :], scalar=-1.0 / 64, in1=yj, op0=ALU.mult, op1=ALU.add),
                      reads=[pp[0], Y], writes=[a])
                kb.op("act", lambda e: e.activation(out=b[:], in_=a[:], func=AF.Square), reads=[a], writes=[b])
                kb.op("pe", lambda e: e.matmul(pp[1][:, :], lhsT=cx.bones[:], rhs=b[:], start=True, stop=True), reads=[cx.bones, b], writes=[pp[1]])
                kb.op("act", lambda e: e.activation(out=b[:], in_=pp[1][:, :], func=AF.Sqrt, scale=1.0 / 64, bias=eps[:]),
                      reads=[pp[1], eps], writes=[b])
                kb.op("dve", lambda e: e.reciprocal(out=b[:], in_=b[:]), reads=[b], writes=[b])
                kb.op("dve", lambda e: e.tensor_tensor(out=a[:], in0=a[:], in1=b[:], op=ALU.mult), reads=[a, b], writes=[a])
                kb.op("dve", lambda e: e.tensor_scalar(out=a[:], in0=a[:], scalar1=pc[:, 10 + j:11 + j], scalar2=pc[:, 12 + j:13 + j],
                                                       op0=ALU.mult, op1=ALU.add), reads=[a, pc], writes=[a])
                kb.op("dve", lambda e: e.scalar_tensor_tensor(out=b[:], in0=Rf[:, j, :], scalar=pc[:, 8 + j:9 + j], in1=Kf[:, j, :],
                                                              op0=ALU.mult, op1=ALU.mult), reads=[Rf, Kf, pc], writes=[b])
                kb.op("pe", lambda e: e.matmul(pp[2][:, :], lhsT=cx.bones[:], rhs=b[:], start=True, stop=True), reads=[cx.bones, b], writes=[pp[2]])
                kb.op("dve", lambda e: e.tensor_tensor(out=b[:], in0=pp[2][:, :], in1=Vf[:, j, :], op=ALU.mult), reads=[pp[2], Vf], writes=[b])
                kb.op("dve", lambda e: e.tensor_tensor(out=a[:], in0=a[:], in1=b[:], op=ALU.add), reads=[a, b], writes=[a])
                kb.op("dve", lambda e: e.tensor_tensor(out=c[:], in0=a[:], in1=Gf[:, j, :], op=ALU.mult), reads=[a, Gf], writes=[c])
                kb.dma(yc_out[j, :, i * 512:(i + 1) * 512], c[:], reads=[c], writes=[cx.OUT])
        kb.barrier()


def rwkv_phase2(*a, **kw):
    for _ in rwkv_phase2_gen(*a, **kw):
        pass


def ssd_inputs(nc, cx):
    cx.ss_w = din(nc, "ss_w", [8, 128, 1024])
    cx.ss_pc = din(nc, "ss_pc", [128, 28])
    cx.ss_mk = din(nc, "ss_mk", [3, 128, 128])


def ssd_phase1(kb, nc, cx, T, hT_dram, fm, bc_fm):
    with contextlib.ExitStack() as st:
        W = kb.sb([128, 8, 1024], name="ss_W", stack=st)
        kb.dma(W[:], cx.ss_w.rearrange("k p c -> p k c"), writes=[W])
        pc = kb.sb([128, 28], name="ss_pc", stack=st)
        kb.dma(pc[:], cx.ss_pc, writes=[pc])
        nega = kb.sb([128, 2], name="ss_nega", stack=st)
        kb.op("act", lambda e: e.activation(out=nega[:], in_=pc[:, 22:24], func=AF.Exp), reads=[pc], writes=[nega])
        kb.op("dve", lambda e: e.tensor_scalar(out=nega[:], in0=nega[:], scalar1=-1.0, scalar2=None, op0=ALU.mult), reads=[nega], writes=[nega])
        hTs = [kb.sb([128, 8, 512], name="ss_h%d" % i, stack=st) for i in range(2)]
        raw = [kb.sb([128, 515], name="ss_raw%d" % c, stack=st) for c in range(4)]
        for c in range(4):
            kb.op("dve", lambda e: e.memset(raw[c][:, 0:3], 0.0), writes=[raw[c]])
        cv = [kb.sb([128, 512], name="ss_cv%d" % c, stack=st) for c in range(4)]
        sz = kb.sb([128, 512], name="ss_sz", stack=st)
        dt = kb.sb([128, 512], name="ss_dt", stack=st)
        dA = kb.sb([128, 512], name="ss_dA", stack=st)
        xdt = kb.sb([128, 512], name="ss_xdt", stack=st)
        pp = cx.psum
        for i in range(T // 512):
            hT = hTs[i % 2]
            ts = slice(i * 512, (i + 1) * 512)
            kb.dma(hT[:], hT_dram.t[:, :, ts].rearrange("k p t -> p k t"), writes=[hT])

            def proj(c, ps):
                for k in range(8):
                    kb.op("pe", lambda e: e.matmul(ps[:, :], lhsT=W[:, k, c * 128:(c + 1) * 128], rhs=hT[:, k, :], start=(k == 0), stop=(k == 7)),
                          reads=[W, hT], writes=[ps])
            for j in range(2):
                ps = pp[j]
                proj(j, ps)
                kb.op("act", lambda e: e.activation(out=sz[:], in_=ps[:, :], func=AF.Silu), reads=[ps], writes=[sz])
                kb.dma(fm.t[0, j, :, ts], sz[:], reads=[sz])
            for c in range(4):
                ps = pp[2 + c % 2]
                proj(2 + c, ps)
                r = raw[c]
                kb.op("act", lambda e: e.copy(out=r[:, 3:515], in_=ps[:, :]), reads=[ps], writes=[r])
                kb.op("dve", lambda e: e.tensor_scalar(out=cv[c][:], in0=r[:, 0:512], scalar1=pc[:, c * 4:c * 4 + 1], scalar2=None, op0=ALU.mult),
                      reads=[r, pc], writes=[cv[c]])
                for k in range(1, 4):
                    kb.op("dve", lambda e: e.scalar_tensor_tensor(out=cv[c][:], in0=r[:, k:k + 512], scalar=pc[:, c * 4 + k:c * 4 + k + 1], in1=cv[c][:],
                                                                  op0=ALU.mult, op1=ALU.add), reads=[r, pc, cv[c]], writes=[cv[c]])
                kb.op("act", lambda e: e.activation(out=cv[c][:], in_=cv[c][:], func=AF.Silu, bias=pc[:, 16 + c:17 + c]), reads=[cv[c], pc], writes=[cv[c]])
                kb.op("dve", lambda e: e.tensor_copy(out=r[:, 0:3], in_=r[:, 512:515]), reads=[r], writes=[r])
                if c >= 2:
                    kb.dma(bc_fm.t[c - 2, :, ts], cv[c][:], reads=[cv[c]])
            for j in range(2):
                ps = pp[4 + j]
                proj(6 + j, ps)
                kb.op("act", lambda e: e.activation(out=dt[:], in_=ps[:, :], func=AF.Exp, bias=pc[:, 20 + j:21 + j]), reads=[ps, pc], writes=[dt])
                kb.op("act", lambda e: e.activation(out=dt[:], in_=dt[:], func=AF.Ln, bias=1.0), reads=[dt], writes=[dt])
                kb.op("dve", lambda e: e.tensor_scalar(out=dA[:], in0=dt[:], scalar1=nega[:, j:j + 1], scalar2=None, op0=ALU.mult), reads=[dt, nega], writes=[dA])
                kb.op("dve", lambda e: e.tensor_tensor(out=xdt[:], in0=cv[j][:], in1=dt[:], op=ALU.mult), reads=[cv[j], dt], writes=[xdt])
                kb.dma(fm.t[1, j, :, ts], cv[j][:], reads=[cv[j]])
                kb.dma(fm.t[2, j, :, ts], xdt[:], reads=[xdt])
                kb.dma(fm.t[3, j, :, ts], dA[:], reads=[dA])
        kb.barrier()


def ssd_phase2(kb, nc, cx, T, fm, bc_fm, ya_out):
    with contextlib.ExitStack() as st:
        pc = kb.sb([128, 28], name="s2_pc", stack=st)
        kb.dma(pc[:], cx.ss_pc, writes=[pc])
        mk = kb.sb([128, 3, 128], name="s2_mk", stack=st)
        kb.dma(mk[:], cx.ss_mk.rearrange("m p c -> p m c"), writes=[mk])
        Uin, Lst, CNeg = mk[:, 0, :], mk[:, 1, :], mk[:, 2, :]
        H = kb.sb([128, 256], name="s2_H", stack=st)
        kb.op("dve", lambda e: e.memset(H[:], 0.0), writes=[H])
        fms = [[kb.sb([128, 2, 512], name="s2_fm%d_%d" % (q, b), stack=st) for b in range(2)] for q in range(4)]
        bcs = [[kb.sb([128, 512], name="s2_bc%d_%d" % (q, b), stack=st) for b in range(2)] for q in range(2)]
        ys = [kb.sb([128, 2, 512], name="s2_y%d" % b, stack=st) for b in range(2)]
        wk = [kb.sb([128, 512], name="s2_wk%d" % b, stack=st) for b in range(3)]
        rstd = kb.sb([128, 512], name="s2_rstd", stack=st)
        dA_tok = kb.sb([128, 4], name="s2_dAtok", stack=st)
        sc = kb.sb([128, 12], name="s2_sc", stack=st)
        lh = kb.sb([128, 4, 128], name="s2_lh", stack=st)
        dec4 = kb.sb([128, 4, 128], name="s2_dec4", stack=st)
        M4 = kb.sb([128, 4, 128], name="s2_M4", stack=st)
        xtok = kb.sb([128, 256], name="s2_xtok", stack=st)
        xdec = kb.sb([128, 256], name="s2_xdec", stack=st)
        Btok = kb.sb([128, 128], name="s2_Btok", stack=st)
        ytmp = kb.sb([128, 256], name="s2_ytmp", stack=st)
        ytok = kb.sb([128, 256], name="s2_ytok", stack=st)
        Ht = kb.sb([128, 256], name="s2_Ht", stack=st)
        pp = cx.psum
        e4 = lambda ap: ap.unsqueeze(2).broadcast_to([128, 4, 64])
        v4 = lambda ap: ap.rearrange("p (e q) -> p e q", e=4)
        for i in range(T // 512):
            ts = slice(i * 512, (i + 1) * 512)
            fb = [fms[q][i % 2] for q in range(4)]
            for q in range(4):
                kb.dma(fb[q][:], fm.t[q, :, :, ts].rearrange("j p t -> p j t"), writes=[fb[q]])
            SZ, XS, XDT, DA = fb
            BT, CT = bcs[0][i % 2], bcs[1][i % 2]
            kb.dma(BT[:], bc_fm.t[0, :, ts], writes=[BT])
            kb.dma(CT[:], bc_fm.t[1, :, ts], writes=[CT])
            Y = ys[i % 2]
            for c in range(4):
                cs = slice(c * 128, (c + 1) * 128)
                for j in range(2):
                    kb.op("pe", lambda e: e.transpose(pp[0][:, j * 128:(j + 1) * 128], XDT[:, j, cs], cx.ident[:]), reads=[XDT, cx.ident], writes=[pp[0]])
                kb.op("pe", lambda e: e.transpose(pp[0][:, 256:384], BT[:, cs], cx.ident[:]), reads=[BT, cx.ident], writes=[pp[0]])
                for j in range(2):
                    kb.op("pe", lambda e: e.transpose(pp[1][:, j * 128:(j + 1) * 128], DA[:, j, cs], cx.ident[:]), reads=[DA, cx.ident], writes=[pp[1]])
                kb.op("act", lambda e: e.copy(out=xtok[:], in_=pp[0][:, 0:256]), reads=[pp[0]], writes=[xtok])
                kb.op("act", lambda e: e.copy(out=Btok[:], in_=pp[0][:, 256:384]), reads=[pp[0]], writes=[Btok])
                kb.op("dve", lambda e: e.tensor_copy(out=dA_tok[:], in_=pp[1][:, 0:256:64]), reads=[pp[1]], writes=[dA_tok])
                kb.op("pe", lambda e: e.matmul(pp[2][:, 0:4], lhsT=Uin, rhs=dA_tok[:], start=True, stop=True), reads=[mk, dA_tok], writes=[pp[2]])
                kb.op("pe", lambda e: e.matmul(pp[2][:, 4:8], lhsT=Lst, rhs=dA_tok[:], start=True, stop=True), reads=[mk, dA_tok], writes=[pp[2]])
                kb.op("pe", lambda e: e.matmul(pp[2][:, 8:12], lhsT=cx.ones[:], rhs=dA_tok[:], start=True, stop=True), reads=[cx.ones, dA_tok], writes=[pp[2]])
                kb.op("act", lambda e: e.activation(out=sc[:], in_=pp[2][:, 0:12], func=AF.Exp), reads=[pp[2]], writes=[sc])
                for e_ in range(4):
                    kb.op("dve", lambda e: e.tensor_scalar(out=lh[:, e_, :], in0=Lst, scalar1=dA_tok[:, e_:e_ + 1], scalar2=None, op0=ALU.mult),
                          reads=[mk, dA_tok], writes=[lh])
                for e_ in range(4):
                    kb.op("pe", lambda e: e.matmul(pp[3][:, e_ * 128:(e_ + 1) * 128], lhsT=lh[:, e_, :], rhs=Uin, start=True, stop=False), reads=[lh, mk], writes=[pp[3]])
                    kb.op("pe", lambda e: e.matmul(pp[3][:, e_ * 128:(e_ + 1) * 128], lhsT=cx.ident[:], rhs=CNeg, start=False, stop=True), reads=[cx.ident, mk], writes=[pp[3]])
                kb.op("act", lambda e: e.activation(out=dec4[:], in_=pp[3][:, :].rearrange("p (e l) -> p e l", e=4), func=AF.Exp), reads=[pp[3]], writes=[dec4])
                kb.op("pe", lambda e: e.matmul(pp[4][:, 0:128], lhsT=BT[:, cs], rhs=CT[:, cs], start=True, stop=True), reads=[BT, CT], writes=[pp[4]])
                kb.op("dve", lambda e: e.tensor_tensor(out=M4[:], in0=dec4[:], in1=pp[4][:, 0:128].unsqueeze(1).broadcast_to([128, 4, 128]), op=ALU.mult),
                      reads=[dec4, pp[4]], writes=[M4])
                for e_ in range(4):
                    kb.op("pe", lambda e: e.matmul(pp[5][:, e_ * 64:(e_ + 1) * 64], lhsT=M4[:, e_, :], rhs=xtok[:, e_ * 64:(e_ + 1) * 64], start=True, stop=True),
                          reads=[M4, xtok], writes=[pp[5]])
                kb.op("pe", lambda e: e.matmul(pp[6][:, 0:256], lhsT=CT[:, cs], rhs=H[:], start=True, stop=True), reads=[CT, H], writes=[pp[6]])
                kb.op("dve", lambda e: e.tensor_tensor(out=v4(ytmp[:]), in0=v4(pp[6][:, 0:256]), in1=e4(sc[:, 0:4]), op=ALU.mult), reads=[pp[6], sc], writes=[ytmp])
                kb.op("dve", lambda e: e.tensor_tensor(out=ytok[:], in0=ytmp[:], in1=pp[5][:, 0:256], op=ALU.add), reads=[ytmp, pp[5]], writes=[ytok])
                kb.op("dve", lambda e: e.tensor_tensor(out=v4(xdec[:]), in0=v4(xtok[:]), in1=e4(sc[:, 4:8]), op=ALU.mult), reads=[xtok, sc], writes=[xdec])
                kb.op("pe", lambda e: e.matmul(pp[7][:, 0:256], lhsT=Btok[:], rhs=xdec[:], start=True, stop=True), reads=[Btok, xdec], writes=[pp[7]])
                kb.op("dve", lambda e: e.tensor_tensor(out=v4(Ht[:]), in0=v4(H[:]), in1=e4(sc[:, 8:12]), op=ALU.mult), reads=[H, sc], writes=[Ht])
                kb.op("dve", lambda e: e.tensor_tensor(out=H[:], in0=Ht[:], in1=pp[7][:, 0:256], op=ALU.add), reads=[Ht, pp[7]], writes=[H])
                for j in range(2):
                    kb.op("pe", lambda e: e.transpose(pp[2][:, 128 + j * 128:256 + j * 128], ytok[:, j * 128:(j + 1) * 128], cx.ident[:]), reads=[ytok, cx.ident], writes=[pp[2]])
                kb.op("act", lambda e: e.copy(out=Y[:, :, cs], in_=pp[2][:, 128:384].rearrange("p (j l) -> p j l", j=2)), reads=[pp[2]], writes=[Y])
            a, b, c_ = wk
            for j in range(2):
                yj = Y[:, j, :]
                kb.op("dve", lambda e: e.scalar_tensor_tensor(out=yj, in0=XS[:, j, :], scalar=pc[:, 24 + j:25 + j], in1=yj, op0=ALU.mult, op1=ALU.add),
                      reads=[XS, pc, Y], writes=[Y])
                kb.op("dve", lambda e: e.tensor_tensor(out=yj, in0=yj, in1=SZ[:, j, :], op=ALU.mult), reads=[Y, SZ], writes=[Y])
                kb.op("act", lambda e: e.activation(out=a[:], in_=yj, func=AF.Square), reads=[Y], writes=[a])
                kb.op("pe", lambda e: e.matmul(pp[0][:, :], lhsT=cx.ones[:], rhs=a[:], start=(j == 0), stop=(j == 1)), reads=[cx.ones, a], writes=[pp[0]])
            kb.op("act", lambda e: e.activation(out=rstd[:], in_=pp[0][:, :], func=AF.Sqrt, scale=1.0 / 256, bias=cx.eps6[:]),
                  reads=[pp[0], cx.eps6], writes=[rstd])
            kb.op("dve", lambda e: e.reciprocal(out=rstd[:], in_=rstd[:]), reads=[rstd], writes=[rstd])
            for j in range(2):
                o = (b, c_)[j]
                kb.op("dve", lambda e: e.scalar_tensor_tensor(out=o[:], in0=Y[:, j, :], scalar=pc[:, 26 + j:27 + j], in1=rstd[:], op0=ALU.mult, op1=ALU.mult),
                      reads=[Y, pc, rstd], writes=[o])
                kb.dma(ya_out[j, :, ts], o[:], reads=[o], writes=[cx.OUT])
        kb.barrier()


NEGM = -30000.0


def moba_inputs(nc, cx):
    cx.mb_wqk = din(nc, "mb_wqk", [8, 128, 512])
    cx.mb_wv = din(nc, "mb_wv", [8, 128, 256])
    cx.mb_g = din(nc, "mb_g", [128, 2])
    cx.mb_rel = din(nc, "mb_rel", [1, 64])
    cx.mb_bi = din(nc, "mb_bi", [128, 1280])


def moba_phase1(kb, nc, cx, T, hT_dram, qT_d, kT_d, v_d, nm_d):
    with contextlib.ExitStack() as st:
        Wqk = kb.sb([128, 8, 512], name="mb_Wqk", stack=st)
        kb.dma(Wqk[:], cx.mb_wqk.rearrange("k p c -> p k c"), writes=[Wqk])
        Wv = kb.sb([128, 8, 256], name="mb_Wv", stack=st)
        kb.dma(Wv[:], cx.mb_wv.rearrange("k p c -> p k c"), writes=[Wv])
        gg = kb.sb([128, 2], name="mb_g", stack=st)
        kb.dma(gg[:], cx.mb_g, writes=[gg])
        hTs = [kb.sb([128, 8, 512], name="mb_h%d" % i, stack=st) for i in range(2)]
        kmT = [kb.sb([128, 64], name="mb_km%d" % h, stack=st) for h in range(2)]
        for h in range(2):
            kb.op("dve", lambda e: e.memset(kmT[h][:], 0.0), writes=[kmT[h]])
        rawt = kb.sb([128, 512], name="mb_raw", stack=st)
        sq = kb.sb([128, 512], name="mb_sq", stack=st)
        rstd = kb.sb([128, 512], name="mb_rstd", stack=st)
        QT = kb.sb([128, 512], name="mb_QT", stack=st)
        KT = kb.sb([128, 512], name="mb_KT", stack=st)
        QS = kb.sb([128, 512], BF16, name="mb_QS", stack=st)
        KTb = kb.sb([128, 512], BF16, name="mb_KTb", stack=st)
        gt = kb.sb([128, 64], name="mb_gt", stack=st)
        sel = kb.sb([128, 64], name="mb_sel", stack=st)
        mx = kb.sb([128, 8], name="mb_mx", stack=st)
        nmT = kb.sb([64, 512], BF16, name="mb_nmT", stack=st)
        va = [kb.sb([128, 4, 2, 129], BF16, name="mb_va%d" % i, stack=st) for i in range(2)]
        for i in range(2):
            kb.op("dve", lambda e: e.memset(va[i][:], 1.0), writes=[va[i]])
        pp = cx.psum
        for i in range(T // 512):
            hT = hTs[i % 2]
            ts = slice(i * 512, (i + 1) * 512)
            kb.dma(hT[:], hT_dram.t[:, :, ts].rearrange("k p t -> p k t"), writes=[hT])

            def normed(c, gcol, dst):
                ps = pp[c % 2]
                for k in range(8):
                    kb.op("pe", lambda e: e.matmul(ps[:, :], lhsT=Wqk[:, k, c * 128:(c + 1) * 128], rhs=hT[:, k, :], start=(k == 0), stop=(k == 7)),
                          reads=[Wqk, hT], writes=[ps])
                kb.op("act", lambda e: e.copy(out=rawt[:], in_=ps[:, :]), reads=[ps], writes=[rawt])
                kb.op("act", lambda e: e.activation(out=sq[:], in_=rawt[:], func=AF.Square), reads=[rawt], writes=[sq])
                kb.op("pe", lambda e: e.matmul(pp[2][:, :], lhsT=cx.ones[:], rhs=sq[:], start=True, stop=True), reads=[cx.ones, sq], writes=[pp[2]])
                kb.op("act", lambda e: e.activation(out=rstd[:], in_=pp[2][:, :], func=AF.Sqrt, scale=1.0 / 128, bias=cx.eps6[:]),
                      reads=[pp[2], cx.eps6], writes=[rstd])
                kb.op("dve", lambda e: e.reciprocal(out=rstd[:], in_=rstd[:]), reads=[rstd], writes=[rstd])
                kb.op("dve", lambda e: e.scalar_tensor_tensor(out=dst[:], in0=rawt[:], scalar=gg[:, gcol:gcol + 1], in1=rstd[:], op0=ALU.mult, op1=ALU.mult),
                      reads=[rawt, gg, rstd], writes=[dst])
            for hh in range(2):
                normed(2 + hh, 1, KT)
                kb.op("act", lambda e: e.copy(out=KTb[:], in_=KT[:]), reads=[KT], writes=[KTb])
                kb.dma(kT_d.t[hh, :, ts], KTb[:], reads=[KTb])
                kb.op("dve", lambda e: e.tensor_reduce(out=kmT[hh][:, 2 * i:2 * i + 2], in_=KT[:].rearrange("p (b t) -> p b t", b=2), axis=AX.X, op=ALU.add),
                      reads=[KT], writes=[kmT[hh]])
                normed(hh, 0, QT)
                kb.op("dve", lambda e: e.tensor_scalar(out=QS[:], in0=QT[:], scalar1=128.0 ** -0.5, scalar2=None, op0=ALU.mult), reads=[QT], writes=[QS])
                kb.dma(qT_d.t[hh, :, ts], QS[:], reads=[QS])
                for s in range(4):
                    qb = 2 * i + s // 2
                    if qb == 0:
                        kb.op("dve", lambda e: e.memset(sel[:], 0.0), writes=[sel])
                    else:
                        kb.op("pe", lambda e: e.matmul(pp[3][:, 0:64], lhsT=QT[:, s * 128:(s + 1) * 128], rhs=kmT[hh][:, 0:64], start=True, stop=True),
                              reads=[QT, kmT[hh]], writes=[pp[3]])
                        kb.op("dve", lambda e: e.memset(gt[:], -1e30), writes=[gt])
                        kb.op("dve", lambda e: e.tensor_copy(out=gt[:, 0:qb], in_=pp[3][:, 0:qb]), reads=[pp[3]], writes=[gt])
                        if qb >= 3:
                            kb.op("dve", lambda e: e.max(out=mx[:], in_=gt[:]), reads=[gt], writes=[mx])
                            kb.op("dve", lambda e: e.tensor_scalar(out=sel[:], in0=gt[:], scalar1=mx[:, 2:3], scalar2=None, op0=ALU.is_ge),
                                  reads=[gt, mx], writes=[sel])
                        else:
                            kb.op("dve", lambda e: e.tensor_scalar(out=sel[:], in0=gt[:], scalar1=-1e29, scalar2=None, op0=ALU.is_ge),
                                  reads=[gt], writes=[sel])
                    kb.op("dve", lambda e: e.tensor_scalar(out=sel[:], in0=sel[:], scalar1=-1.0, scalar2=-NEGM, op0=ALU.add, op1=ALU.mult),
                          reads=[sel], writes=[sel])
                    kb.op("pe", lambda e: e.transpose(pp[4][0:64, 0:128], sel[:, 0:64], cx.ident[:]), reads=[sel, cx.ident], writes=[pp[4]])
                    kb.op("act", lambda e: e.copy(out=nmT[0:64, s * 128:(s + 1) * 128], in_=pp[4][0:64, 0:128]), reads=[pp[4]], writes=[nmT])
                kb.dma(nm_d.t[hh, :, ts], nmT[0:64, :], reads=[nmT])
            vt = va[i % 2]
            for s in range(4):
                ps = pp[5 + s % 2]
                for k in range(8):
                    kb.op("pe", lambda e: e.matmul(ps[:, 0:256], lhsT=hT[:, k, s * 128:(s + 1) * 128], rhs=Wv[:, k, :], start=(k == 0), stop=(k == 7)),
                          reads=[hT, Wv], writes=[ps])
                kb.op("act", lambda e: e.copy(out=vt[:, s, :, 0:128], in_=ps[:, 0:256].rearrange("p (h d) -> p h d", h=2)), reads=[ps], writes=[vt])
            for hh in range(2):
                kb.dma(v_d.t[hh, ts, :].rearrange("(s p) c -> p s c", p=128), vt[:, :, hh, :], reads=[vt])
        kb.barrier()


def moba_phase2_gen(kb, nc, cx, T, qT_d, kT_d, v_d, nm_d, yb_out, sbanks=(4, 5, 6, 7)):
    NT = T // 128
    NG = T // 256
    with contextlib.ExitStack() as st:
        KT = kb.sb([128, T], BF16, name="m2_KT", stack=st)
        VA = kb.sb([128, NT, 129], BF16, name="m2_VA", stack=st)
        En = kb.sb([64, 64, 128], BF16, name="m2_En", stack=st)
        identb = kb.sb([128, 128], BF16, name="m2_identb", stack=st)
        kb.op("dve", lambda e: e.tensor_copy(out=identb[:], in_=cx.ident[:]), reads=[cx.ident], writes=[identb])
        kb.op("dve", lambda e: e.tensor_copy(out=En[:], in_=cx.ident[0:64, 0:64].unsqueeze(2).broadcast_to([64, 64, 128])), reads=[cx.ident], writes=[En])
        BI = kb.sb([128, 1280], name="m2_BI", stack=st)
        kb.dma(BI[:], cx.mb_bi, writes=[BI])
        rel = kb.sb([128, 64], name="m2_rel", stack=st)
        kb.dma(rel[:], cx.mb_rel.broadcast_to([128, 64]), writes=[rel])
        BB = kb.sb([128, 1280], name="m2_BB", stack=st)
        BBb = kb.sb([128, 1280], BF16, name="m2_BBb", stack=st)
        tmpb = kb.sb([128, 1280], name="m2_tmpb", stack=st)
        QG = [kb.sb([128, 256], BF16, name="m2_QG%d" % i, stack=st) for i in range(2)]
        NM = [kb.sb([64, 256], BF16, name="m2_NM%d" % i, stack=st) for i in range(2)]
        PT = [kb.sb([128, 256], BF16, name="m2_PT%d" % i, stack=st) for i in range(4)]
        yt = [kb.sb([128, 2, 128], name="m2_yt%d" % i, stack=st) for i in range(2)]
        rinv = kb.sb([128, 1], name="m2_rinv", stack=st)
        pp = cx.psum
        npt = 0
        for hh in range(2):
            kb.dma(KT[:], kT_d.t[hh, :, :], writes=[KT])
            kb.dma(VA[:], v_d.t[hh, :, :].rearrange("(n p) c -> p n c", p=128), writes=[VA])
            kb.op("dve", lambda e: e.tensor_scalar(out=BB[:], in0=BI[:], scalar1=-1.0, scalar2=NEGM, op0=ALU.is_equal, op1=ALU.mult), reads=[BI], writes=[BB])
            for b in range(32):
                kb.op("dve", lambda e: e.tensor_scalar(out=tmpb[:], in0=BI[:], scalar1=float(b), scalar2=rel[:, hh * 32 + b:hh * 32 + b + 1],
                                                       op0=ALU.is_equal, op1=ALU.mult), reads=[BI, rel], writes=[tmpb])
                kb.op("dve", lambda e: e.tensor_tensor(out=BB[:], in0=BB[:], in1=tmpb[:], op=ALU.add), reads=[BB, tmpb], writes=[BB])
            kb.op("act", lambda e: e.copy(out=BBb[:], in_=BB[:]), reads=[BB], writes=[BBb])
            for g in range(NG):
                qg, nm = QG[g % 2], NM[g % 2]
                gs = slice(g * 256, (g + 1) * 256)
                kb.dma(qg[:], qT_d.t[hh, :, gs], writes=[qg])
                kb.dma(nm[:], nm_d.t[hh, :, gs], writes=[nm])
                O = [pp[(g % 2) * 2 + 0], pp[(g % 2) * 2 + 1]]
                for kt in range(2 * g + 2):
                    n = kt // 2
                    past = n < g
                    if kt == 2 * g + 1:
                        q0, w, b0 = 128, 128, 0
                    else:
                        q0, w, b0 = 0, 256, min(2 * g - kt, 8) * 128
                    S = pp[sbanks[npt % len(sbanks)]]
                    pt = PT[npt % 4]
                    npt += 1
                    kb.op("pe", lambda e: e.matmul(S[:, 0:w], lhsT=KT[:, kt * 128:(kt + 1) * 128], rhs=qg[:, q0:q0 + w], start=True, stop=False),
                          reads=[KT, qg], writes=[S])
                    kb.op("pe", lambda e: e.matmul(S[:, 0:w], lhsT=identb[:], rhs=BBb[:, b0:b0 + w], start=False, stop=(not past)),
                          reads=[identb, BBb], writes=[S])
                    if past:
                        kb.op("pe", lambda e: e.matmul(S[:, 0:w], lhsT=En[0:64, n, :], rhs=nm[0:64, q0:q0 + w], start=False, stop=True),
                              reads=[En, nm], writes=[S])
                    kb.op("act", lambda e: e.activation(out=pt[:, 0:w], in_=S[:, 0:w], func=AF.Exp), reads=[S], writes=[pt])
                    for qq in range(2):
                        if kt == 2 * g + 1 and qq == 0:
                            continue
                        c0 = qq * 128 - q0
                        last = (2 * g) if qq == 0 else (2 * g + 1)
                        kb.op("pe", lambda e: e.matmul(O[qq][:, 0:129], lhsT=pt[:, c0:c0 + 128], rhs=VA[:, kt, :], start=(kt == 0), stop=(kt == last)),
                              reads=[pt, VA], writes=[O[qq]])
                    yield
                y = yt[g % 2]
                for qq in range(2):
                    kb.op("dve", lambda e: e.reciprocal(out=rinv[:], in_=O[qq][:, 128:129]), reads=[O[qq]], writes=[rinv])
                    kb.op("dve", lambda e: e.tensor_scalar(out=y[:, qq, :], in0=O[qq][:, 0:128], scalar1=rinv[:, 0:1], scalar2=None, op0=ALU.mult),
                          reads=[O[qq], rinv], writes=[y])
                kb.dma(yb_out[hh, gs, :].rearrange("(q p) d -> p q d", p=128), y[:], reads=[y], writes=[cx.OUT])
        kb.barrier()


def moba_phase2(*a, **kw):
    for _ in moba_phase2_gen(*a, **kw):
        pass


def lb_inputs(nc, cx, TB):
    cx.xT = din(nc, "xT", [8, 128, TB]); cx.xtok = din(nc, "xtok", [TB, 1024])
    cx.yT = din(nc, "yT", [24, 128, TB])
    cx.g1 = din(nc, "g1", [128, 8]); cx.g2 = din(nc, "g2", [128, 8])
    cx.wg = din(nc, "wg", [8, 128, 3072]); cx.bg = din(nc, "bg", [128, 24])
    cx.pall = din(nc, "pall", [24, 128, 1024])
    cx.wout = din(nc, "wout", [8, 128, 1024])
    cx.wq = din(nc, "wq", [8, 128, 2048])
    cx.k12T = din(nc, "k12T", [128, 256])
    cx.uT = din(nc, "uT", [32, 128, 8, 512])
    cx.vv = din(nc, "vv", [32, 128, 4, 1024])


def lb_gate_merge(kb, nc, cx, TB, hT_d, mT_d):
    with contextlib.ExitStack() as st:
        bg = kb.sb([128, 24], name="b1_bg", stack=st)
        kb.dma(bg[:], cx.bg, writes=[bg])
        Wg = [kb.sb([128, 8, 3, 128], name="b1_wg%d" % i, stack=st) for i in range(2)]
        Pc = [kb.sb([128, 24, 128], name="b1_pc%d" % i, stack=st) for i in range(2)]
        hTs = [kb.sb([128, 8, 512], name="b1_h%d" % i, stack=st) for i in range(2)]
        yTs = [kb.sb([128, 24, 512], name="b1_y%d" % i, stack=st) for i in range(2)]
        gs = [kb.sb([128, 512], name="b1_g%d" % i, stack=st) for i in range(3)]
        m = [kb.sb([128, 512], name="b1_m%d" % i, stack=st) for i in range(2)]
        t = kb.sb([128, 512], name="b1_t", stack=st)
        pp = cx.psum
        it = 0
        for c in range(8):
            wg, pc = Wg[c % 2], Pc[c % 2]
            for br in range(3):
                kb.dma(wg[:, :, br, :], cx.wg[:, :, br * 1024 + c * 128:br * 1024 + (c + 1) * 128].rearrange("k p c -> p k c"), writes=[wg])
            kb.dma(pc[:], cx.pall[:, :, c * 128:(c + 1) * 128].rearrange("k p c -> p k c"), writes=[pc])
            for ti in range(TB // 512):
                ts = slice(ti * 512, (ti + 1) * 512)
                hT, yT = hTs[it % 2], yTs[it % 2]
                mm = m[it % 2]
                it += 1
                kb.dma(hT[:], hT_d.t[:, :, ts].rearrange("k p t -> p k t"), writes=[hT])
                kb.dma(yT[:], cx.yT[:, :, ts].rearrange("k p t -> p k t"), writes=[yT])
                for br in range(3):
                    pg, pj = pp[br], pp[3 + br]
                    for k in range(8):
                        kb.op("pe", lambda e: e.matmul(pg[:, :], lhsT=wg[:, k, br, :], rhs=hT[:, k, :], start=(k == 0), stop=(k == 7)), reads=[wg, hT], writes=[pg])
                    kb.op("act", lambda e: e.activation(out=gs[br][:], in_=pg[:, :], func=AF.Sigmoid, bias=bg[:, br * 8 + c:br * 8 + c + 1]),
                          reads=[pg, bg], writes=[gs[br]])
                    for k in range(8):
                        kb.op("pe", lambda e: e.matmul(pj[:, :], lhsT=pc[:, br * 8 + k, :], rhs=yT[:, br * 8 + k, :], start=(k == 0), stop=(k == 7)), reads=[pc, yT], writes=[pj])
                    if br == 0:
                        kb.op("dve", lambda e: e.tensor_tensor(out=mm[:], in0=gs[br][:], in1=pj[:, :], op=ALU.mult), reads=[gs[br], pj], writes=[mm])
                    else:
                        kb.op("dve", lambda e: e.tensor_tensor(out=t[:], in0=gs[br][:], in1=pj[:, :], op=ALU.mult), reads=[gs[br], pj], writes=[t])
                        kb.op("dve", lambda e: e.tensor_tensor(out=mm[:], in0=mm[:], in1=t[:], op=ALU.add), reads=[mm, t], writes=[mm])
                kb.dma(mT_d.t[c, :, ts], mm[:], reads=[mm])
        kb.barrier()


def lb_outproj(kb, nc, cx, TB, mT_d, xnT_d, xn_tok_d):
    with contextlib.ExitStack() as st:
        Wo = kb.sb([128, 8, 1024], name="b2_wo", stack=st)
        kb.dma(Wo[:], cx.wout.rearrange("k p c -> p k c"), writes=[Wo])
        mTs = [kb.sb([128, 8, 512], name="b2_m%d" % i, stack=st) for i in range(2)]
        xTs = [kb.sb([128, 8, 512], name="b2_x%d" % i, stack=st) for i in range(2)]
        xn = [kb.sb([128, 8, 512], name="b2_xn%d" % i, stack=st) for i in range(2)]
        xt = [kb.sb([128, 1024], name="b2_xt%d" % i, stack=st) for i in range(2)]
        pp = cx.psum
        nx = 0
        for ti in range(TB // 512):
            ts = slice(ti * 512, (ti + 1) * 512)
            mT, xT, xo = mTs[ti % 2], xTs[ti % 2], xn[ti % 2]
            kb.dma(mT[:], mT_d.t[:, :, ts].rearrange("k p t -> p k t"), writes=[mT])
            kb.dma(xT[:], cx.xT[:, :, ts].rearrange("k p t -> p k t"), writes=[xT])
            for c in range(8):
                ps = pp[c % 2]
                for k in range(8):
                    kb.op("pe", lambda e: e.matmul(ps[:, :], lhsT=Wo[:, k, c * 128:(c + 1) * 128], rhs=mT[:, k, :], start=(k == 0), stop=(k == 7)), reads=[Wo, mT], writes=[ps])
                kb.op("dve", lambda e: e.tensor_tensor(out=xo[:, c, :], in0=ps[:, :], in1=xT[:, c, :], op=ALU.add), reads=[ps, xT], writes=[xo])
            kb.dma(xnT_d.t[:, :, ts].rearrange("k p t -> p k t"), xo[:], reads=[xo])
            for s in range(4):
                xk = xt[nx % 2]
                nx += 1
                r0 = ti * 512 + s * 128
                kb.dma(xk[:], cx.xtok[r0:r0 + 128, :], writes=[xk])
                for hf in range(2):
                    ps = pp[2 + hf]
                    for k in range(8):
                        kb.op("pe", lambda e: e.matmul(ps[:, :], lhsT=mT[:, k, s * 128:(s + 1) * 128], rhs=Wo[:, k, hf * 512:(hf + 1) * 512], start=(k == 0), stop=(k == 7)),
                              reads=[Wo, mT], writes=[ps])
                    kb.op("dve", lambda e: e.tensor_tensor(out=xk[:, hf * 512:(hf + 1) * 512], in0=ps[:, :], in1=xk[:, hf * 512:(hf + 1) * 512], op=ALU.add),
                          reads=[ps, xk], writes=[xk])
                kb.dma(xn_tok_d.t[r0:r0 + 128, :], xk[:], reads=[xk])
        kb.barrier()


def lb_peer_scores(kb, nc, cx, TB, h2T_d, s_d):
    with contextlib.ExitStack() as st:
        Wq = kb.sb([128, 8, 2048], name="b3_wq", stack=st)
        kb.dma(Wq[:], cx.wq.rearrange("k p c -> p k c"), writes=[Wq])
        k12 = kb.sb([128, 256], name="b3_k12", stack=st)
        kb.dma(k12[:], cx.k12T, writes=[k12])
        hTs = [kb.sb([128, 8, 512], name="b3_h%d" % i, stack=st) for i in range(2)]
        qT = [kb.sb([128, 512], name="b3_q%d" % i, stack=st) for i in range(2)]
        sc = [kb.sb([128, 4, 128], name="b3_s%d" % i, stack=st) for i in range(2)]
        pp = cx.psum
        for ti in range(TB // 512):
            ts = slice(ti * 512, (ti + 1) * 512)
            hT = hTs[ti % 2]
            kb.dma(hT[:], h2T_d.t[:, :, ts].rearrange("k p t -> p k t"), writes=[hT])
            for ct in range(16):
                ps = pp[ct % 2]
                q = qT[ct % 2]
                so = sc[ct % 2]
                for k in range(8):
                    kb.op("pe", lambda e: e.matmul(ps[:, :], lhsT=Wq[:, k, ct * 128:(ct + 1) * 128], rhs=hT[:, k, :], start=(k == 0), stop=(k == 7)), reads=[Wq, hT], writes=[ps])
                kb.op("act", lambda e: e.copy(out=q[:], in_=ps[:, :]), reads=[ps], writes=[q])
                p2 = pp[2 + ct % 2]
                for s in range(4):
                    kb.op("pe", lambda e: e.matmul(p2[:, s * 128:(s + 1) * 128], lhsT=q[:, s * 128:(s + 1) * 128], rhs=k12[:, (ct % 2) * 128:(ct % 2) * 128 + 128], start=True, stop=True),
                          reads=[q, k12], writes=[p2])
                kb.op("dve", lambda e: e.tensor_copy(out=so[:], in_=p2[:, :].rearrange("p (s n) -> p s n", s=4)), reads=[p2], writes=[so])
                kb.dma(s_d.t[ct % 2, ts, ct // 2, :].rearrange("(s p) n -> p s n", p=128), so[:], reads=[so])
        kb.barrier()


def build_lb(nc, kb, cx, TB):
    setup_consts(kb, nc, cx)
    lb_inputs(nc, cx, TB)
    out_d = dout(nc, "xout", [TB, 1024])
    hT_d = kb.dram("hT_s", [8, 128, TB]); mT_d = kb.dram("mT_s", [8, 128, TB])
    xnT_d = kb.dram("xnT_s", [8, 128, TB]); xn_tok_d = kb.dram("xntok_s", [TB, 1024])
    h2T_d = kb.dram("h2T_s", [8, 128, TB]); s_d = kb.dram("s_s", [2, TB, 8, 128])
    phase_norm(kb, nc, cx, TB, cx.xT, cx.g1, hT_d)
    lb_gate_merge(kb, nc, cx, TB, hT_d, mT_d)
    lb_outproj(kb, nc, cx, TB, mT_d, xnT_d, xn_tok_d)
    phase_norm(kb, nc, cx, TB, xnT_d.t, cx.g2, h2T_d)
    lb_peer_scores(kb, nc, cx, TB, h2T_d, s_d)
    lb_peer_main2(kb, nc, cx, TB, h2T_d, s_d, xn_tok_d, out_d)
    kb.finish()


def lb_peer_main2(kb, nc, cx, TB, h2T_d, s_d, xn_tok_d, out_d, NQ=8, NT=2):
    QI = 128 // NQ
    QE = QI * 128
    NCH = QE // 512
    TP = NT * 128
    with contextlib.ExitStack() as st:
        identb = kb.sb([128, 128], BF16, name="b5_identb", stack=st)
        kb.op("dve", lambda e: e.tensor_copy(out=identb[:], in_=cx.ident[:]), reads=[cx.ident], writes=[identb])
        hf = kb.sb([128, 8, TP], name="b5_hf", stack=st)
        hb = kb.sb([128, 8, TP], BF16, name="b5_hb", stack=st)
        s1 = kb.sb([128, NT, 8, 128], name="b5_s1", stack=st)
        s2 = kb.sb([128, NT, 8, 128], name="b5_s2", stack=st)
        xk = [kb.sb([128, 1024], name="b5_xk%d" % i, stack=st) for i in range(NT)]
        wk = kb.sb([128, 256], name="b5_wk", stack=st)
        v12 = kb.sb([128, 2, 16], name="b5_v12", stack=st)
        cand = kb.sb([128, 16, 16], name="b5_cand", stack=st)
        c16 = kb.sb([128, 16], name="b5_c16", stack=st)
        e16 = kb.sb([128, 16], name="b5_e16", stack=st)
        tau = kb.sb([128, NT, 8], name="b5_tau", stack=st)
        negm = kb.sb([128, NT, 8], name="b5_negm", stack=st)
        rz = kb.sb([128, NT, 8], name="b5_rz", stack=st)
        E = kb.sb([128, QI, 128], name="b5_E", stack=st)
        M = kb.sb([128, QI, 128], name="b5_M", stack=st)
        X = kb.sb([128, QI, 128], name="b5_X", stack=st)
        G = [kb.sb([128, QE], name="b5_G%d" % i, stack=st) for i in range(NT)]
        Ac = [kb.sb([128, 512], name="b5_A%d" % i, stack=st) for i in range(2)]
        GAb = [kb.sb([128, 512], BF16, name="b5_GAb%d" % i, stack=st) for i in range(2)]
        GAT = [kb.sb([128, 4, 128], BF16, name="b5_GAT%d" % i, stack=st) for i in range(2)]
        uc = [kb.sb([128, 8, 512], name="b5_u%d" % i, stack=st) for i in range(2)]
        ub = [kb.sb([128, 8, 512], BF16, name="b5_ub%d" % i, stack=st) for i in range(2)]
        vc = [kb.sb([128, 4, 1024], name="b5_v%d" % i, stack=st) for i in range(2)]
        vb = [kb.sb([128, 4, 1024], BF16, name="b5_vb%d" % i, stack=st) for i in range(2)]
        pp = cx.psum
        nch = 0
        nt = 0
        for tp in range(TB // TP):
            r0 = tp * TP
            kb.dma(hf[:], h2T_d.t[:, :, r0:r0 + TP].rearrange("k p t -> p k t"), writes=[hf])
            kb.op("act", lambda e: e.copy(out=hb[:], in_=hf[:]), reads=[hf], writes=[hb])
            for t in range(NT):
                kb.dma(s1[:, t, :, :], s_d.t[0, r0 + t * 128:r0 + (t + 1) * 128, :, :], writes=[s1])
                kb.dma(s2[:, t, :, :], s_d.t[1, r0 + t * 128:r0 + (t + 1) * 128, :, :], writes=[s2])
                kb.dma(xk[t][:], xn_tok_d.t[r0 + t * 128:r0 + (t + 1) * 128, :], writes=[xk[t]])
            for t in range(NT):
                for hd in range(8):
                    for w, src in enumerate((s1, s2)):
                        kb.op("dve", lambda e: e.max(out=v12[:, w, 0:8], in_=src[:, t, hd, :]), reads=[src], writes=[v12])
                        kb.op("dve", lambda e: e.match_replace(out=wk[:, 0:128], in_to_replace=v12[:, w, 0:8], in_values=src[:, t, hd, :], imm_value=-1e30),
                              reads=[src, v12], writes=[wk])
                        kb.op("dve", lambda e: e.max(out=v12[:, w, 8:16], in_=wk[:, 0:128]), reads=[wk], writes=[v12])
                    kb.op("dve", lambda e: e.tensor_tensor(out=cand[:], in0=v12[:, 0, :].unsqueeze(2).broadcast_to([128, 16, 16]),
                                                           in1=v12[:, 1, :].unsqueeze(1).broadcast_to([128, 16, 16]), op=ALU.add), reads=[v12], writes=[cand])
                    cf = cand[:].rearrange("p a b -> p (a b)")
                    kb.op("dve", lambda e: e.max(out=c16[:, 0:8], in_=cf), reads=[cand], writes=[c16])
                    kb.op("dve", lambda e: e.match_replace(out=wk[:], in_to_replace=c16[:, 0:8], in_values=cf, imm_value=-1e30), reads=[cand, c16], writes=[wk])
                    kb.op("dve", lambda e: e.max(out=c16[:, 8:16], in_=wk[:]), reads=[wk], writes=[c16])
                    kb.op("dve", lambda e: e.tensor_copy(out=tau[:, t, hd:hd + 1], in_=c16[:, 15:16]), reads=[c16], writes=[tau])
                    kb.op("dve", lambda e: e.tensor_scalar(out=negm[:, t, hd:hd + 1], in0=c16[:, 0:1], scalar1=-1.0, scalar2=None, op0=ALU.mult), reads=[c16], writes=[negm])
                    kb.op("act", lambda e: e.activation(out=e16[:], in_=c16[:], func=AF.Exp, bias=negm[:, t, hd:hd + 1]), reads=[c16, negm], writes=[e16])
                    kb.op("dve", lambda e: e.tensor_reduce(out=rz[:, t, hd:hd + 1], in_=e16[:], axis=AX.X, op=ALU.add), reads=[e16], writes=[rz])
            kb.op("dve", lambda e: e.reciprocal(out=rz[:], in_=rz[:]), reads=[rz], writes=[rz])
            O = [[pp[4 + 2 * t + h2] for h2 in range(2)] for t in range(NT)]
            for qi in range(NQ):
                for t in range(NT):
                    G3 = G[t][:].rearrange("p (i j) -> p i j", j=128)
                    for hd in range(8):
                        kb.op("dve", lambda e: e.tensor_tensor(out=E[:], in0=s1[:, t, hd, qi * QI:(qi + 1) * QI].unsqueeze(2).broadcast_to([128, QI, 128]),
                                                               in1=s2[:, t, hd, :].unsqueeze(1).broadcast_to([128, QI, 128]), op=ALU.add), reads=[s1, s2], writes=[E])
                        kb.op("act", lambda e: e.activation(out=X[:], in_=E[:], func=AF.Exp, bias=negm[:, t, hd:hd + 1]), reads=[E, negm], writes=[X])
                        kb.op("dve", lambda e: e.scalar_tensor_tensor(out=M[:], in0=E[:], scalar=tau[:, t, hd:hd + 1], in1=X[:], op0=ALU.is_ge, op1=ALU.mult),
                              reads=[E, tau, X], writes=[M])
                        if hd == 0:
                            kb.op("dve", lambda e: e.tensor_scalar(out=G3, in0=M[:], scalar1=rz[:, t, hd:hd + 1], scalar2=None, op0=ALU.mult),
                                  reads=[M, rz], writes=[G[t]])
                        else:
                            kb.op("dve", lambda e: e.scalar_tensor_tensor(out=G3, in0=M[:], scalar=rz[:, t, hd:hd + 1], in1=G3, op0=ALU.mult, op1=ALU.add),
                                  reads=[M, rz, G[t]], writes=[G[t]])
                for ch in range(NCH):
                    e0 = qi * QE + ch * 512
                    u, v, ubb, vbb = uc[nch % 2], vc[nch % 2], ub[nch % 2], vb[nch % 2]
                    nch += 1
                    kb.dma(u[:], cx.uT[e0 // 512], writes=[u])
                    kb.dma(v[:], cx.vv[e0 // 512], writes=[v])
                    kb.op("act", lambda e: e.copy(out=ubb[:], in_=u[:]), reads=[u], writes=[ubb])
                    kb.op("act", lambda e: e.copy(out=vbb[:], in_=v[:]), reads=[v], writes=[vbb])
                    for t in range(NT):
                        A, gab, gat = Ac[nt % 2], GAb[nt % 2], GAT[nt % 2]
                        pa, ptr = pp[nt % 2], pp[2 + nt % 2]
                        nt += 1
                        ptrb = ptr[:, 0:256].bitcast(BF16)
                        for k in range(8):
                            kb.op("pe", lambda e: e.matmul(pa[:, :], lhsT=hb[:, k, t * 128:(t + 1) * 128], rhs=ubb[:, k, :], start=(k == 0), stop=(k == 7)),
                                  reads=[hb, ubb], writes=[pa])
                        kb.op("act", lambda e: e.activation(out=A[:], in_=pa[:, :], func=AF.Gelu), reads=[pa], writes=[A])
                        kb.op("dve", lambda e: e.tensor_tensor(out=gab[:], in0=A[:], in1=G[t][:, ch * 512:(ch + 1) * 512], op=ALU.mult), reads=[A, G[t]], writes=[gab])
                        for s in range(4):
                            kb.op("pe", lambda e: e.transpose(ptrb[:, s * 128:(s + 1) * 128], gab[:, s * 128:(s + 1) * 128], identb[:]), reads=[gab, identb], writes=[ptr])
                        kb.op("act", lambda e: e.copy(out=gat[:], in_=ptrb.rearrange("p (s t) -> p s t", s=4)), reads=[ptr], writes=[gat])
                        first = (qi == 0 and ch == 0)
                        for s in range(4):
                            lastmm = (qi == NQ - 1 and ch == NCH - 1 and s == 3)
                            for h2 in range(2):
                                kb.op("pe", lambda e: e.matmul(O[t][h2][:, :], lhsT=gat[:, s, :], rhs=vbb[:, s, h2 * 512:(h2 + 1) * 512], start=(first and s == 0), stop=lastmm),
                                      reads=[gat, vbb], writes=[O[t][h2]])
            for t in range(NT):
                for h2 in range(2):
                    kb.op("dve", lambda e: e.tensor_tensor(out=xk[t][:, h2 * 512:(h2 + 1) * 512], in0=O[t][h2][:, :], in1=xk[t][:, h2 * 512:(h2 + 1) * 512], op=ALU.add),
                          reads=[O[t][h2], xk[t]], writes=[xk[t]])
                kb.dma(out_d[r0 + t * 128:r0 + (t + 1) * 128, :], xk[t][:], reads=[xk[t]], writes=[cx.OUT])
        kb.barrier()

C = np.ascontiguousarray
IN_SSD = 1024 + 2048 + 16
B2 = IN_SSD + 3072

def colT(v):
    return C(np.asarray(v, np.float32).reshape(-1, 128).T)

def wtiles(w):
    return C(np.asarray(w, np.float32).reshape(8, 128, -1))

def prep_common(x_b, norm_g):
    T = x_b.shape[0]
    return {"xT": C(x_b.T.reshape(8, 128, T)), "g1": colT(norm_g), "ident": np.eye(128, dtype=np.float32)}

def prep_rwkv(inp, l, hg):
    w_in = inp["w_in"][l]
    cs = slice(256 * hg, 256 * hg + 256)
    r = w_in[:, B2:B2 + 1024][:, cs]; k = w_in[:, B2 + 1024:B2 + 2048][:, cs]; v = w_in[:, B2 + 2048:B2 + 3072][:, cs]
    la = w_in[:, B2 + 3072:B2 + 3200]; gl = w_in[:, B2 + 3200:B2 + 3328]
    W = np.concatenate([r, k, v, la, gl], axis=1)
    mu = inp["rwkv_mu"][l]
    mu_all = np.concatenate([mu[0:1024][cs], mu[1024:2048][cs], mu[2048:3072][cs], mu[3072:3200], mu[3200:3328]])
    pc = np.concatenate([colT(inp[n][l].reshape(-1)[cs]) for n in ("w0", "a0", "k_k", "k_a", "r_k", "lnx_g", "lnx_b")], axis=1)
    la2 = np.concatenate([inp["w_w2"][l][:, cs], inp["w_a2"][l][:, cs]], axis=0)
    return {"rw_w": wtiles(W), "rw_mu": colT(mu_all), "rw_pc": C(pc), "rw_la2": C(la2.astype(np.float32)), "rw_g2": C(inp["w_g2"][l][:, cs])}

def prep_ssd(inp, l, hg):
    w_in = inp["w_in"][l]
    z = w_in[:, 256 * hg:256 * hg + 256]
    x = w_in[:, 1024 + 256 * hg:1024 + 256 * hg + 256]
    Bc = w_in[:, 2048 + 128 * hg:2048 + 128 * hg + 128]
    Cc = w_in[:, 2560 + 128 * hg:2560 + 128 * hg + 128]
    dtc = w_in[:, 3072 + 4 * hg:3072 + 4 * hg + 4]
    dtrep = np.concatenate([np.repeat(dtc[:, 2 * j + ep:2 * j + ep + 1], 64, axis=1) for j in range(2) for ep in range(2)], axis=1)
    W = np.concatenate([z, x, Bc, Cc, dtrep], axis=1)
    cw = inp["conv_w"][l]; cb = inp["conv_b"][l]
    chans = np.concatenate([np.arange(256 * hg, 256 * hg + 256), 1024 + np.arange(128 * hg, 128 * hg + 128), 1536 + np.arange(128 * hg, 128 * hg + 128)])
    convw = np.zeros((128, 16), np.float32); convb = np.zeros((128, 4), np.float32)
    for c in range(4):
        ch = chans[c * 128:(c + 1) * 128]
        for k in range(4):
            convw[:, c * 4 + k] = cw[k, ch]
        convb[:, c] = cb[ch]
    def hrep(v):
        v = np.asarray(v)[4 * hg:4 * hg + 4]
        return np.stack([np.repeat(v[2 * j:2 * j + 2], 64) for j in range(2)], axis=1).astype(np.float32)
    pc = np.concatenate([convw, convb, hrep(inp["dt_bias"][l]), hrep(inp["a_log"][l]), hrep(inp["d_skip"][l]),
                         colT(inp["ssd_norm_g"][l][256 * hg:256 * hg + 256])], axis=1)
    idx = np.arange(128)
    mk = np.stack([(idx[:, None] <= idx[None, :]), (idx[:, None] > idx[None, :]), -30000.0 * (idx[:, None] > idx[None, :])]).astype(np.float32)
    return {"ss_w": wtiles(W), "ss_pc": C(pc), "ss_mk": C(mk)}

def bucket_strip():
    import jax, jax.numpy as jnp, math
    cpu = jax.devices("cpu")[0]
    with jax.default_device(cpu):
        d = jnp.arange(0, 1280, dtype=jnp.int32)
        max_exact = 16
        df = jnp.maximum(d, 1).astype(jnp.float32)
        large = max_exact + (jnp.log(df / max_exact) / math.log(1024 / max_exact) * (32 - max_exact)).astype(jnp.int32)
        large = jnp.minimum(large, 31)
        bk = np.asarray(jnp.where(d < max_exact, d, large))
    m = np.arange(1280)[None, :] - np.arange(128)[:, None]
    out = np.where(m >= 0, bk[np.clip(m, 0, 1279)], -1).astype(np.float32)
    return C(out)

def prep_moba(inp, l, hg):
    w_in = inp["w_in"][l]
    hs = [2 * hg, 2 * hg + 1]
    q = np.concatenate([w_in[:, IN_SSD + h * 128:IN_SSD + (h + 1) * 128] for h in hs], axis=1)
    k = np.concatenate([w_in[:, IN_SSD + 1024 + h * 128:IN_SSD + 1024 + (h + 1) * 128] for h in hs], axis=1)
    v = np.concatenate([w_in[:, IN_SSD + 2048 + h * 128:IN_SSD + 2048 + (h + 1) * 128] for h in hs], axis=1)
    g = np.stack([inp["q_norm_g"][l], inp["k_norm_g"][l]], axis=1).astype(np.float32)
    rel = np.concatenate([inp["rel_bias"][:, h] for h in hs])[None, :].astype(np.float32)
    return {"mb_wqk": wtiles(np.concatenate([q, k], axis=1)), "mb_wv": wtiles(v), "mb_g": C(g), "mb_rel": C(rel), "mb_bi": bucket_strip()}

def prep_lb_weights(inp, l):
    return {"g1": colT(inp["norm1_g"][l]), "g2": colT(inp["norm2_g"][l]),
            "wg": wtiles(inp["w_gate"][l]), "bg": colT(inp["b_gate"][l]),
            "pall": C(np.concatenate([inp["p_ssd"][l], inp["p_moba"][l], inp["p_rwkv"][l]], axis=0).reshape(24, 128, 1024).astype(np.float32)),
            "wout": wtiles(inp["w_out"][l]), "wq": wtiles(inp["peer_wq"][l]),
            "k12T": C(np.concatenate([inp["peer_k1"][l].T, inp["peer_k2"][l].T], axis=1).astype(np.float32)),
            "uT": C(np.asarray(inp["peer_u"][l], np.float32).reshape(32, 512, 8, 128).transpose(0, 3, 2, 1)), "vv": C(np.asarray(inp["peer_v"][l], np.float32).reshape(32, 4, 128, 1024).transpose(0, 2, 1, 3)), "ident": np.eye(128, dtype=np.float32)}

def prep_lb_acts(x_tok, yfull):
    TB = x_tok.shape[0]
    return {"xT": C(x_tok.T.reshape(8, 128, TB)), "xtok": C(x_tok), "yT": C(yfull.T.reshape(24, 128, TB))}

import time as _time
from concourse.bass_utils import run_bass_kernel_spmd

T_FULL = 16384
TB_FULL = 4096
_PROG = {}


def interleave(gm, gr, ratio):
    done_m = done_r = False
    while not (done_m and done_r):
        if not done_r:
            try:
                next(gr)
            except StopIteration:
                done_r = True
        if not done_m:
            for _ in range(ratio):
                try:
                    next(gm)
                except StopIteration:
                    done_m = True
                    break


def build_la(T):
    nc = bass.Bass("TRN2", target_bir_lowering=False)
    kb = KB(nc)
    cx = Ctx()
    setup_consts(kb, nc, cx)
    xT_d = din(nc, "xT", [8, 128, T]); g_d = din(nc, "g1", [128, 8])
    rwkv_inputs(nc, cx); ssd_inputs(nc, cx); moba_inputs(nc, cx)
    ya = dout(nc, "yaT", [2, 128, T]); yb = dout(nc, "yb", [2, T, 128]); yc = dout(nc, "ycT", [2, 128, T])
    hT = kb.dram("hT_s", [8, 128, T])
    strm = kb.dram("rw_strm", [5, 2, T, 128], BF16); strm_w = kb.dram("rw_strmw", [2, T, 128]); rfm = kb.dram("rw_fm", [4, 2, 128, T])
    sfm = kb.dram("ss_fm", [4, 2, 128, T]); bcf = kb.dram("ss_bc", [2, 128, T])
    qT = kb.dram("mb_q", [2, 128, T], BF16); kT = kb.dram("mb_k", [2, 128, T], BF16); vd = kb.dram("mb_v", [2, T, 129], BF16); nm = kb.dram("mb_nm", [2, 64, T], BF16)
    phase_norm(kb, nc, cx, T, xT_d, g_d, hT)
    rwkv_phase1(kb, nc, cx, T, hT, strm, rfm, strm_w=strm_w)
    ssd_phase1(kb, nc, cx, T, hT, sfm, bcf)
    moba_phase1(kb, nc, cx, T, hT, qT, kT, vd, nm)
    gm = moba_phase2_gen(kb, nc, cx, T, qT, kT, vd, nm, yb, sbanks=(4, 5))
    gr = rwkv_phase2_gen(kb, nc, cx, T, strm, rfm, yc, same=False, strm_w=strm_w, pbanks=(6, 7, 6))
    NG = T // 256
    interleave(gm, gr, -(-(2 * (NG * NG + NG)) // (T // 16)) + 1)
    ssd_phase2(kb, nc, cx, T, sfm, bcf, ya)
    kb.finish()
    return nc, kb


def build_lb_prog(TB):
    nc = bass.Bass("TRN2", target_bir_lowering=False)
    kb = KB(nc)
    cx = Ctx()
    build_lb(nc, kb, cx, TB)
    return nc, kb


def _prog(kind, n):
    key = (kind, n)
    if key not in _PROG:
        _PROG[key] = (build_la if kind == "A" else build_lb_prog)(n)[0]
    return _PROG[key]


def kernel(**inputs):
    inp = {k: np.asarray(v) for k, v in inputs.items()}
    x = np.ascontiguousarray(inp["x"], dtype=np.float32)
    Bn, T, D = x.shape
    TB = (Bn * T) // 8
    per_b = T // TB
    ncores = 8
    for l in range(2):
        ncA = _prog("A", T)
        in_maps = []
        for c in range(ncores):
            b, hg = c // 4, c % 4
            m = prep_common(x[b], inp["norm1_g"][l])
            m.update(prep_rwkv(inp, l, hg)); m.update(prep_ssd(inp, l, hg)); m.update(prep_moba(inp, l, hg))
            in_maps.append(m)
        resA = run_bass_kernel_spmd(ncA, in_maps, core_ids=list(range(ncores))).results
        del in_maps
        yfull = np.empty((Bn, T, 3072), np.float32)
        for c in range(ncores):
            b, hg = c // 4, c % 4
            r = resA[c]
            yfull[b, :, 256 * hg:256 * hg + 256] = np.asarray(r["yaT"]).reshape(256, T).T
            yfull[b, :, 1024 + 256 * hg:1024 + 256 * hg + 256] = np.asarray(r["yb"]).transpose(1, 0, 2).reshape(T, 256)
            yfull[b, :, 2048 + 256 * hg:2048 + 256 * hg + 256] = np.asarray(r["ycT"]).reshape(256, T).T
        del resA
        ncB = _prog("B", TB)
        wts = prep_lb_weights(inp, l)
        in_maps = []
        for c in range(ncores):
            b, s0 = c // per_b, (c % per_b) * TB
            m = dict(wts)
            m.update(prep_lb_acts(x[b, s0:s0 + TB], yfull[b, s0:s0 + TB]))
            in_maps.append(m)
        resB = run_bass_kernel_spmd(ncB, in_maps, core_ids=list(range(ncores))).results
        del in_maps
        xn = np.empty_like(x)
        for c in range(ncores):
            b, s0 = c // per_b, (c % per_b) * TB
            xn[b, s0:s0 + TB] = np.asarray(resB[c]["xout"])
        x = xn
    return x
```
